# Optimizing a Trainium2 kernel written in Bass

```python
import math
import jax, jax.numpy as jnp
from jax import lax
import numpy as np

D_MODEL = 1024
BATCH = 2
SEQ = 8192
DEPTH = 2

D_MIX = D_MODEL
HEAD_DIM = 64
RWKV_WIDTH = D_MIX // 2
SB_WIDTH = D_MIX - RWKV_WIDTH
RWKV_HEADS = RWKV_WIDTH // HEAD_DIM
SB_HEADS = SB_WIDTH // HEAD_DIM
DECAY_LORA = 64
ICLR_LORA = 64
SHIFT_COLS = 3 * RWKV_WIDTH + DECAY_LORA + ICLR_LORA
RWKV_COLS = SHIFT_COLS + RWKV_WIDTH
SB_COLS = 4 * SB_WIDTH
IN_COLS = RWKV_COLS + SB_COLS
Q_BLOCK = 128
NORM_EPS = 1e-6
GN_EPS = 64e-5

kernel_name = "hymba_rwkv7_stickbreaking_adaln"


def rmsnorm(x, g):
    xf = x.astype(jnp.float32)
    y = xf * lax.rsqrt(jnp.mean(xf * xf, axis=-1, keepdims=True) + NORM_EPS)
    return (y * g.astype(jnp.float32)).astype(x.dtype)


def rwkv7_group(p, mu, w0, w2, a0, a2, k_k, k_a, r_k, ln_w, ln_b):
    B, S, _ = p.shape
    H, N = RWKV_HEADS, HEAD_DIM
    ps, g = p[..., :SHIFT_COLS], p[..., SHIFT_COLS:]
    prev = jnp.pad(ps, ((0, 0), (1, 0), (0, 0)))[:, :-1]
    ps = ps + (prev - ps) * mu
    r, k, v, zw, za = jnp.split(
        ps, [RWKV_WIDTH, 2 * RWKV_WIDTH, 3 * RWKV_WIDTH, 3 * RWKV_WIDTH + DECAY_LORA], axis=-1)
    w_log = -jax.nn.softplus(-(w0 + jnp.tanh(zw) @ w2)) - 0.5
    decay = jnp.exp(-jnp.exp(w_log.astype(jnp.float32)))
    a = jax.nn.sigmoid(a0 + za @ a2)
    hd = lambda t: t.reshape(B, S, H, N)
    r, k, v, decay, a = hd(r), hd(k), hd(v), hd(decay), hd(a)
    kk = hd(ps[..., RWKV_WIDTH:2 * RWKV_WIDTH] * k_k)
    kkf = kk.astype(jnp.float32)
    kk = kkf * lax.rsqrt(jnp.maximum(jnp.sum(kkf * kkf, axis=-1, keepdims=True), 1e-24))
    k = k * (1.0 + (a - 1.0) * k_a.reshape(H, N))

    def step(state, inp):
        r_t, w_t, k_t, v_t, kk_t, b_t = inp
        sa = jnp.einsum('bhij,bhj->bhi', state, -kk_t)
        state = (state * w_t[:, :, None, :]
                 + sa[..., None] * b_t[:, :, None, :]
                 + v_t[..., None] * k_t[:, :, None, :])
        y_t = jnp.einsum('bhij,bhj->bhi', state, r_t)
        return state, y_t

    tm = lambda t: jnp.moveaxis(t.astype(jnp.float32), 1, 0)
    state0 = jnp.zeros((B, H, N, N), jnp.float32)
    _, y = lax.scan(step, state0, (tm(r), tm(decay), tm(k), tm(v), tm(kk), tm(kk * a)))
    y = jnp.moveaxis(y, 0, 1)
    mean = jnp.mean(y, axis=-1, keepdims=True)
    var = jnp.mean(jnp.square(y - mean), axis=-1, keepdims=True)
    y = (y - mean) * lax.rsqrt(var + GN_EPS)
    y = y * ln_w.reshape(H, N) + ln_b.reshape(H, N)
    rf, kf, vf = r.astype(jnp.float32), k.astype(jnp.float32), v.astype(jnp.float32)
    y = y + jnp.sum(rf * kf * r_k, axis=-1, keepdims=True) * vf
    y = y.reshape(B, S, RWKV_WIDTH).astype(p.dtype)
    return y * jax.nn.silu(g)


def stick_breaking_group(p, sb_g):
    B, S, _ = p.shape
    H, N = SB_HEADS, HEAD_DIM
    n_blocks = S // Q_BLOCK
    q, k, v, g = jnp.split(p, 4, axis=-1)
    hd = lambda t: t.reshape(B, S, H, N).transpose(0, 2, 1, 3)
    q, k, v = hd(q), hd(k), hd(v)
    scale = 1.0 / math.sqrt(N)
    k_pos = jnp.arange(S)
    q_blocks = q.reshape(B, H, n_blocks, Q_BLOCK, N).transpose(2, 0, 1, 3, 4)

    def block(args):
        q_blk, idx = args
        q_pos = idx * Q_BLOCK + jnp.arange(Q_BLOCK)
        z = jnp.einsum('bhqd,bhkd->bhqk', q_blk, k).astype(jnp.float32) * scale
        mask = k_pos[None, :] < q_pos[:, None]
        log_1m = jnp.where(mask, jax.nn.log_sigmoid(-z), 0.0)
        after = lax.cumsum(log_1m, axis=3, reverse=True) - log_1m
        attn = jnp.where(mask, jnp.exp(jax.nn.log_sigmoid(z) + after), 0.0)
        return jnp.einsum('bhqk,bhkd->bhqd', attn.astype(v.dtype), v)

    o = lax.map(block, (q_blocks, jnp.arange(n_blocks)))
    o = o.transpose(1, 3, 0, 2, 4).reshape(B, S, H, N)
    o = rmsnorm(o, sb_g.reshape(H, N)).reshape(B, S, SB_WIDTH)
    return o * jax.nn.silu(g)


def setup_inputs(seed: int = 0) -> dict:
    key = jax.random.key(seed)
    ks = jax.random.split(key, 20)
    nrm = lambda k, shape, s: s * jax.random.normal(k, shape, jnp.float32)
    return {
        "x": nrm(ks[0], (BATCH, SEQ, D_MODEL), 1.0),
        "c": nrm(ks[1], (BATCH, D_MODEL), 1.0),
        "norm_g": 1.0 + nrm(ks[2], (DEPTH, D_MODEL), 0.05),
        "ada_w": nrm(ks[3], (DEPTH, D_MODEL, 3 * D_MODEL), 0.5 * D_MODEL ** -0.5),
        "ada_b": nrm(ks[4], (DEPTH, 3 * D_MODEL), 0.02),
        "w_in": nrm(ks[5], (DEPTH, D_MODEL, IN_COLS), D_MODEL ** -0.5),
        "w_out": nrm(ks[6], (DEPTH, D_MIX, D_MODEL), D_MIX ** -0.5),
        "tshift_mu": jax.random.uniform(ks[7], (DEPTH, SHIFT_COLS), jnp.float32),
        "decay_w0": jax.random.uniform(ks[8], (DEPTH, RWKV_WIDTH), jnp.float32, minval=-4.0, maxval=1.0),
        "decay_w2": nrm(ks[9], (DEPTH, DECAY_LORA, RWKV_WIDTH), 0.5 * DECAY_LORA ** -0.5),
        "iclr_a0": nrm(ks[10], (DEPTH, RWKV_WIDTH), 0.5),
        "iclr_a2": nrm(ks[11], (DEPTH, ICLR_LORA, RWKV_WIDTH), 0.5 * ICLR_LORA ** -0.5),
        "k_k": 0.85 + nrm(ks[12], (DEPTH, RWKV_WIDTH), 0.05),
        "k_a": 1.0 + nrm(ks[13], (DEPTH, RWKV_WIDTH), 0.05),
        "r_k": nrm(ks[14], (DEPTH, RWKV_HEADS, HEAD_DIM), 0.1),
        "rwkv_ln_w": 1.0 + nrm(ks[15], (DEPTH, RWKV_WIDTH), 0.05),
        "rwkv_ln_b": nrm(ks[16], (DEPTH, RWKV_WIDTH), 0.02),
        "sb_norm_g": 1.0 + nrm(ks[17], (DEPTH, SB_WIDTH), 0.05),
        "final_g": 1.0 + nrm(ks[18], (D_MODEL,), 0.05),
    }


def reference(x, c, norm_g, ada_w, ada_b, w_in, w_out, tshift_mu, decay_w0, decay_w2,
              iclr_a0, iclr_a2, k_k, k_a, r_k, rwkv_ln_w, rwkv_ln_b, sb_norm_g, final_g):
    c_act = jax.nn.silu(c)
    for l in range(DEPTH):
        mod = c_act @ ada_w[l] + ada_b[l]
        shift, scale, gate = jnp.split(mod, 3, axis=-1)
        h = rmsnorm(x, norm_g[l]) * (1.0 + scale[:, None, :]) + shift[:, None, :]
        p = h @ w_in[l]
        y_rwkv = rwkv7_group(p[..., :RWKV_COLS], tshift_mu[l], decay_w0[l], decay_w2[l],
                             iclr_a0[l], iclr_a2[l], k_k[l], k_a[l], r_k[l],
                             rwkv_ln_w[l], rwkv_ln_b[l])
        y_sb = stick_breaking_group(p[..., RWKV_COLS:], sb_norm_g[l])
        y = jnp.concatenate([y_rwkv, y_sb], axis=-1) @ w_out[l]
        x = x + gate[:, None, :] * y
    return rmsnorm(x, final_g)
```

```python
from contextlib import ExitStack
from concourse.bass_utils import run_bass_kernel_spmd
import numpy as np
import concourse.bass as bass
import concourse.mybir as mybir

F32 = mybir.dt.float32
BF16 = mybir.dt.bfloat16
ALU = mybir.AluOpType
AF = mybir.ActivationFunctionType
AX = mybir.AxisListType


class Buf:
    __slots__ = ("name", "w", "r")

    def __init__(self, name):
        self.name = name
        self.w = None
        self.r = []


class Sched:
    CENG = ("pe", "act", "dve", "pool")
    NDMA = 24

    def __init__(self, nc):
        self.nc = nc
        self.streams = {e: [] for e in self.CENG + ("sp",)}
        self.count = {e: 0 for e in self.CENG}
        self.known = {e: {} for e in self.CENG + ("sp",)}
        self.ndma = 0
        self.dma_events = []
        self.enabled = True

    def _waits(self, eng, reads, writes):
        need = {}

        def add(ev):
            if ev is None:
                return
            k, v = ev
            if need.get(k, 0) < v:
                need[k] = v
        for b in reads:
            add(b.w)
        for b in writes:
            add(b.w)
            for ev in b.r:
                add(ev)
        for k, v in need.items():
            if k == eng:
                if eng == "pe":
                    continue
            if self.known[eng].get(k, 0) >= v:
                continue
            self.known[eng][k] = v
            self.streams[eng].append(("wait", k, v))

    def op(self, eng, meth, *args, reads=(), writes=(), **kw):
        if not self.enabled:
            return None
        self._waits(eng, reads, writes)
        self.count[eng] += 1
        ev = (eng, self.count[eng])
        self.streams[eng].append(("op", meth, args, kw))
        for b in reads:
            b.r.append(ev)
        for b in writes:
            b.w = ev
            b.r = []
        return ev

    def dma(self, q, out, in_, reads=(), writes=()):
        if not self.enabled:
            return None
        n = self.ndma
        self.ndma += 1
        slot = n % self.NDMA
        k = ("d", slot)
        prev = 16 * (n // self.NDMA)
        if prev > 0 and self.known[q].get(k, 0) < prev:
            self.known[q][k] = prev
            self.streams[q].append(("wait", k, prev))
        self._waits(q, reads, writes)
        ev = (k, prev + 16)
        self.streams[q].append(("dma", out, in_, slot))
        for b in reads:
            b.r.append(ev)
        for b in writes:
            b.w = ev
            b.r = []
        self.dma_events.append(ev)
        return ev

    def wait_event(self, eng, ev):
        k, v = ev
        if self.known[eng].get(k, 0) >= v:
            return
        self.known[eng][k] = v
        self.streams[eng].append(("wait", k, v))

    def emit(self, stack):
        nc = self.nc
        sems = {}
        for e in self.CENG:
            sems[e] = stack.enter_context(nc.semaphore("c_" + e))
        for s in range(self.NDMA):
            sems[("d", s)] = stack.enter_context(nc.semaphore("d%d" % s))
        block = stack.enter_context(nc.Block())
        streams = self.streams

        def run(engname, eng):
            for item in streams[engname]:
                if item[0] == "wait":
                    eng.wait_ge(sems[item[1]], item[2])
                elif item[0] == "op":
                    getattr(eng, item[1])(*item[2], **item[3]).then_inc(sems[engname], 1)
                else:
                    _, out, in_, slot = item
                    eng.dma_start(out=out, in_=in_).then_inc(sems[("d", slot)], 16)

        @block.tensor
        def _(e):
            run("pe", e)

        @block.scalar
        def _(e):
            run("act", e)

        @block.vector
        def _(e):
            run("dve", e)

        @block.gpsimd
        def _(e):
            run("pool", e)

        @block.sync
        def _(e):
            run("sp", e)


NORM_EPS = 1e-6
GN_EPS = 64e-5


class TL:
    def __init__(self, h, name):
        self.h = h
        self.b = Buf(name)

    def __getitem__(self, k):
        return self.h[k]


def build_phase_a(S_len, stop=99):
    NB = S_len // 512
    NKB = S_len // 128
    nc = bass.Bass("TRN2", target_bir_lowering=False)
    x_d = nc.dram_tensor("x", [S_len, 1024], F32, kind="ExternalInput").ap()
    wc_d = nc.dram_tensor("wc", [1024, 1152], F32, kind="ExternalInput").ap()
    adaw_d = nc.dram_tensor("adaw", [1024, 2048], F32, kind="ExternalInput").ap()
    pv_d = nc.dram_tensor("pv", [128, 48], F32, kind="ExternalInput").ap()
    w2a2_d = nc.dram_tensor("w2a2", [128, 128], F32, kind="ExternalInput").ap()
    bc_d = nc.dram_tensor("bc", [128, 3, 128], F32, kind="ExternalInput").ap()
    y_d = nc.dram_tensor("y", [S_len, 256], F32, kind="ExternalOutput").ap()
    g_d = nc.dram_tensor("gs", [S_len, 128], F32, kind="ExternalOutput").ap()

    S = Sched(nc)
    st = ExitStack()
    with st:
        def sb(name, shape, dt=F32):
            return TL(st.enter_context(nc.sbuf_tensor("s_" + name, shape, dt)), name)

        def ps(name, shape, dt=F32):
            return TL(st.enter_context(nc.psum_tensor("p_" + name, shape, dt)), name)

        def rd(*ts):
            return [t.b for t in ts]

        bank = [ps("bank%d" % i, [128, 512], F32) for i in range(8)]

        def bview(i, shape_str=None, dt=None, **kw):
            ap = bank[i][:]
            if dt is not None:
                ap = ap.bitcast(dt)
            if shape_str:
                ap = ap.rearrange(shape_str, **kw)
            return ap

        pv = sb("pv", [128, 48])
        w2a2 = sb("w2a2", [128, 128])
        bct = sb("bct", [128, 3, 128])
        S.dma("sp", pv[:], pv_d[:, :], writes=rd(pv))
        S.dma("sp", w2a2[:], w2a2_d[:, :], writes=rd(w2a2))
        S.dma("sp", bct[:], bc_d[:, :, :], writes=rd(bct))

        identf = sb("identf", [128, 128])
        ident = sb("ident", [128, 128], BF16)
        S.op("pool", "memset", identf[:], 1.0, writes=rd(identf))
        S.op("pool", "affine_select", identf[:], identf[:], pattern=[[-1, 128]], compare_op=ALU.is_equal,
                                               fill=0.0, base=0, channel_multiplier=1, reads=rd(identf), writes=rd(identf))
        S.op("dve", "tensor_copy", ident[:], identf[:], reads=rd(identf), writes=rd(ident))

        def mkmask(name, pat, cm, op):
            m = sb(name, [128, 128])
            S.op("pool", "memset", m[:], 1.0, writes=rd(m))
            S.op("pool", "affine_select", m[:], m[:], pattern=[[pat, 128]], compare_op=op, fill=0.0,
                                                   base=0, channel_multiplier=cm, reads=rd(m), writes=rd(m))
            return m
        m_su = mkmask("m_su", 1, -1, ALU.is_gt)
        m_iu = mkmask("m_iu", 1, -1, ALU.is_ge)
        m_sl = mkmask("m_sl", -1, 1, ALU.is_gt)
        m_il = mkmask("m_il", -1, 1, ALU.is_ge)
        maskA = sb("maskA", [128, 4, 128])
        maskB = sb("maskB", [128, 6, 128])
        for i, m in enumerate([m_sl, m_su, m_sl, m_su]):
            S.op("pool", "tensor_copy", maskA[:, i, :], m[:], reads=rd(m), writes=rd(maskA))
        for i, m in enumerate([m_su, m_su, m_iu, m_iu, m_iu, m_iu]):
            S.op("pool", "tensor_copy", maskB[:, i, :], m[:], reads=rd(m), writes=rd(maskB))
        tri = sb("tri", [128, 128], BF16)
        S.op("dve", "tensor_copy", tri[:], m_il[:], reads=rd(m_il), writes=rd(tri))
        ones_bf = sb("ones_bf", [128, 64], BF16)
        S.op("pool", "memset", ones_bf[:], 1.0, writes=rd(ones_bf))
        blockones = sb("blockones", [128, 128])
        S.op("pool", "memset", blockones[:], 0.0, writes=rd(blockones))
        S.op("pool", "memset", blockones[0:64, 0:64], 1.0, writes=rd(blockones))
        S.op("pool", "memset", blockones[64:128, 64:128], 1.0, writes=rd(blockones))
        resetm = sb("resetm", [128, 512])
        S.op("pool", "memset", resetm[:], 1.0, writes=rd(resetm))
        S.op("pool", "memset", resetm[:].rearrange("p (c t) -> p c t", t=128)[:, :, 0:1], 0.0, writes=rd(resetm))

        cact = sb("cact", [128, 8])
        tmp8 = sb("tmp8", [128, 8])
        S.op("act", "activation", tmp8[:], pv[:, 0:8], AF.Exp, scale=-1.0, reads=rd(pv), writes=rd(tmp8))
        S.op("dve", "tensor_scalar", tmp8[:], tmp8[:], 1.0, None, ALU.add, reads=rd(tmp8), writes=rd(tmp8))
        S.op("dve", "reciprocal", tmp8[:], tmp8[:], reads=rd(tmp8), writes=rd(tmp8))
        S.op("dve", "tensor_tensor", cact[:], tmp8[:], pv[:, 0:8], ALU.mult, reads=rd(tmp8, pv), writes=rd(cact))

        stage = [sb("stage%d" % i, [128, 1152]) for i in range(2)]
        modp = bank[0]
        for kc in range(8):
            for hf in range(2):
                stg = stage[hf]
                S.dma("sp", stg[:, 0:1024], adaw_d[kc * 128:(kc + 1) * 128, hf * 1024:(hf + 1) * 1024], writes=rd(stg))
                for jj in range(8):
                    j = hf * 8 + jj
                    S.op("pe", "matmul",
                        modp[:, kc * 16 + j: kc * 16 + j + 1], stg[:, jj * 128:(jj + 1) * 128], cact[:, kc:kc + 1],
                        start=True, stop=True, reads=rd(stg, cact), writes=rd(modp))
        modT = sb("modT", [128, 16])
        S.op("dve", "tensor_reduce", modT[:], modp[:, 0:128].rearrange("p (k j) -> p j k", j=16),
                                              axis=AX.X, op=ALU.add, reads=rd(modp), writes=rd(modT))
        Bco = sb("Bco", [128, 8])
        Aco = sb("Aco", [128, 8])
        S.op("dve", "tensor_tensor", Bco[:], modT[:, 0:8], pv[:, 16:24], ALU.add, reads=rd(modT, pv), writes=rd(Bco))
        S.op("dve", "tensor_tensor", Aco[:], modT[:, 8:16], pv[:, 24:32], ALU.add, reads=rd(modT, pv), writes=rd(Aco))
        S.op("dve", "scalar_tensor_tensor", Aco[:], Aco[:], 1.0, pv[:, 8:16], ALU.add, ALU.mult,
             reads=rd(Aco, pv), writes=rd(Aco))
        der = sb("der", [128, 4])
        S.op("dve", "tensor_scalar", der[:, 0:2], pv[:, 36:38], -1.0, None, ALU.mult, reads=rd(pv), writes=rd(der))
        S.op("dve", "tensor_scalar", der[:, 2:3], pv[:, 39:40], -1.0, 1.0, ALU.mult, ALU.add, reads=rd(pv), writes=rd(der))
        MU = lambda g: pv[:, 32 + g:33 + g]
        NEGW0, NEGA0, OMKA = der[:, 0:1], der[:, 1:2], der[:, 2:3]
        KK, KA, RK = pv[:, 38:39], pv[:, 39:40], pv[:, 40:41]

        W = sb("W", [128, 8, 1152], BF16)
        for kc in range(8):
            stg = stage[kc % 2]
            S.dma("sp", stg[:, 0:1152], wc_d[kc * 128:(kc + 1) * 128, :], writes=rd(stg))
            eng = "dve" if kc % 2 == 0 else "act"
            if eng == "dve":
                S.op("dve", "tensor_copy", W[:, kc, :], stg[:, 0:1152], reads=rd(stg), writes=rd(W))
            else:
                S.op("act", "activation", W[:, kc, :], stg[:, 0:1152], AF.Copy, reads=rd(stg), writes=rd(W))

        kT_all = sb("kT_all", [128, S_len], BF16)
        v_all = sb("v_all", [128, NKB, 128], BF16)
        raw = [sb("raw%d" % g, [128, 513]) for g in range(4)]
        for g in range(4):
            S.op("pool", "memset", raw[g][:, 0:1], 0.0, writes=rd(raw[g]))
        Tst = sb("Tst", [128, 64])
        Tb = sb("Tb", [128, 64], BF16)
        S.op("pool", "memset", Tst[:], 0.0, writes=rd(Tst))
        S.op("pool", "memset", Tb[:], 0.0, writes=rd(Tb))

        xt = [sb("xt%d" % i, [128, 1024]) for i in range(4)]
        junk = sb("junk", [128, 1024], BF16)
        ss = sb("ss", [128, 4])
        rstd = sb("rstd", [128, 4])
        xn = [sb("xn%d" % i, [128, 1024], BF16) for i in range(2)]
        tmpf = sb("tmpf", [128, 8, 128])
        hT = [sb("hT%d" % i, [128, 8, 512], BF16) for i in range(1)]
        sh = [sb("sh%d" % g, [128, 512]) for g in range(4)]
        dif = sb("dif", [128, 512])
        gate = sb("gate", [128, 4, 256])
        gtmp = sb("gtmp", [128, 256])
        th = dif
        e1 = sb("e1", [128, 512])
        ew = sb("ew", [128, 512])
        alpha = sb("alpha", [128, 512])
        kks = sb("kks", [128, 512])
        sq = sb("sq", [128, 512])
        rn = sb("rn", [128, 512])
        kkn = sb("kkn", [128, 512])
        bv = sb("bv", [128, 512])
        kmod = sb("kmod", [128, 512])
        cwn = sb("cwn", [128, 512])
        gam = sb("gam", [128, 512])
        ginv = sb("ginv", [128, 512])
        gprev = sb("gprev", [128, 512])
        gco = sb("gco", [128, 512])
        ncC = sb("ncC", [128, 4])
        RtTh = [sb("RtTh%d" % h, [128, 512], BF16) for h in range(2)]
        AtTh = [sb("AtTh%d" % h, [128, 512], BF16) for h in range(2)]
        rkTh = [sb("rkTh%d" % h, [128, 512], BF16) for h in range(2)]
        qTh = [sb("qTh%d" % h, [128, 512], BF16) for h in range(2)]
        for _t in RtTh + AtTh + rkTh + qTh:
            S.op("pool", "memset", _t[:], 0.0, writes=rd(_t))
        KtT = sb("KtT", [128, 512], BF16)
        BtT = sb("BtT", [128, 512], BF16)
        KcT = sb("KcT", [128, 512], BF16)
        BcT = sb("BcT", [128, 512], BF16)
        VT = sb("VT", [128, 512], BF16)
        tokm = [sb("tokm%d" % c, [128, 3, 128], BF16) for c in range(4)]
        Lab = [[sb("Lab%d_%d" % (c, i), [128, 4, 128], BF16) for i in range(2)] for c in range(4)]
        Mx = [sb("Mx%d" % c, [128, 6, 128], BF16) for c in range(4)]
        MT = [[sb("MT%d_%d" % (c, i), [128, 2, 128], BF16) for i in range(2)] for c in range(4)]
        Xb = sb("Xb", [128, 128], BF16)
        Ub = sb("Ub", [128, 128], BF16)
        yr = sb("yr", [128, 4, 128])
        yb = sb("yb", [128, 4, 128])
        st1 = sb("st1", [128, 8])
        st2 = sb("st2", [128, 8])
        st3 = sb("st3", [128, 8])
        ystage = [sb("ystage%d" % i, [128, 4, 256]) for i in range(1)]
        NSB = 2
        e_t = [kks, kkn]
        sp_t = [sb("sp_t%d" % i, [128, 512], BF16) for i in range(NSB)]
        E_t = [bv, alpha]
        at_t = [sb("at_t%d" % i, [128, 512], BF16) for i in range(NSB)]
        sc_t = [sb("sc_t%d" % i, [128, 4]) for i in range(NSB)]
        oacc = [sb("oacc%d" % h, [128, 4, 64]) for h in range(2)]
        osq = sb("osq", [128, 4, 64])
        ost = sb("ost", [128, 4])

        B_T = 0
        B_IP = [1, 2]
        B_SQ = [3, 4]
        B_MM = 5
        sbsets = [(6, 7, 5), (3, 4, 2)]

        ipn = [0]

        def ipbank():
            b = B_IP[ipn[0] % 2]
            ipn[0] += 1
            return bank[b]

        for tb in range(NB):
            hTb = hT[0]
            yst = ystage[0]
            xts = []
            for ts in range(4):
                xtb = xt[ts]
                xts.append(xtb)
                S.dma("sp", xtb[:], x_d[tb * 512 + ts * 128: tb * 512 + (ts + 1) * 128, :], writes=rd(xtb))
            if stop == 1:
                S.enabled = False
            for ts in range(4):
                S.op("act", "activation", junk[:], xts[ts][:], AF.Square, scale=1.0 / 32.0,
                                                          accum_out=ss[:, ts:ts + 1], reads=rd(xts[ts]), writes=rd(junk, ss))
            S.op("act", "activation", rstd[:], ss[:], AF.Ln, bias=NORM_EPS, reads=rd(ss), writes=rd(rstd))
            S.op("act", "activation", rstd[:], rstd[:], AF.Exp, scale=-0.5, reads=rd(rstd), writes=rd(rstd))
            for ts in range(4):
                xnb = xn[ts % 2]
                S.op("dve", "tensor_scalar", xnb[:], xts[ts][:], rstd[:, ts:ts + 1], None, ALU.mult,
                     reads=rd(xts[ts], rstd), writes=rd(xnb))
                pT = bank[B_T]
                pTv = bview(B_T, "p (k t) -> p k t", dt=BF16, t=128)
                for kc in range(8):
                    S.op("pe", "transpose", pTv[:, kc, :], xnb[:, kc * 128:(kc + 1) * 128], ident[:],
                         reads=rd(xnb, ident), writes=rd(pT))
                S.op("dve", "tensor_tensor", tmpf[:], pTv[:, 0:8, :], Aco[:].unsqueeze(2).to_broadcast([128, 8, 128]), ALU.mult,
                     reads=rd(pT, Aco), writes=rd(tmpf))
                S.op("pool", "tensor_tensor", hTb[:, :, ts * 128:(ts + 1) * 128], tmpf[:],
                                                               Bco[:].unsqueeze(2).to_broadcast([128, 8, 128]), ALU.add,
                     reads=rd(tmpf, Bco), writes=rd(hTb))
            for g in range(6):
                pb = ipbank()
                for kc in range(8):
                    S.op("pe", "matmul", pb[:], W[:, kc, g * 128:(g + 1) * 128], hTb[:, kc, :],
                                                                     start=(kc == 0), stop=(kc == 7),
                         reads=rd(W, hTb), writes=rd(pb))
                if g < 4:
                    S.op("act", "activation", raw[g][:, 1:513], pb[:], AF.Copy, reads=rd(pb), writes=rd(raw[g]))
                    S.op("dve", "tensor_tensor", dif[:], raw[g][:, 0:512], raw[g][:, 1:513], ALU.subtract,
                         reads=rd(raw[g]), writes=rd(dif))
                    S.op("dve", "scalar_tensor_tensor", sh[g][:], dif[:], MU(g), raw[g][:, 1:513], ALU.mult, ALU.add,
                         reads=rd(dif, raw[g], pv), writes=rd(sh[g]))
                    S.op("pool", "tensor_copy", raw[g][:, 0:1], raw[g][:, 512:513], reads=rd(raw[g]), writes=rd(raw[g]))
                elif g == 4:
                    for h in range(2):
                        hs = slice(64 * h, 64 * h + 64)
                        S.op("act", "activation", qTh[h][hs, :], pb[hs, :], AF.Copy, reads=rd(pb), writes=rd(qTh[h]))
                else:
                    S.op("act", "activation", kT_all[:, tb * 512:(tb + 1) * 512], pb[:], AF.Copy,
                         reads=rd(pb), writes=rd(kT_all))
            for ts in range(4):
                pb = ipbank()
                for kc in range(8):
                    S.op("pe", "matmul", pb[:, 0:384], hTb[:, kc, ts * 128:(ts + 1) * 128],
                                                                       W[:, kc, 768:1152], start=(kc == 0), stop=(kc == 7),
                         reads=rd(W, hTb), writes=rd(pb))
                S.op("act", "activation", v_all[:, tb * 4 + ts, :], pb[:, 0:128], AF.Copy,
                     reads=rd(pb), writes=rd(v_all))
                S.op("act", "activation", gtmp[:], pb[:, 128:384], AF.Exp, scale=-1.0, reads=rd(pb), writes=rd(gtmp))
                S.op("dve", "tensor_scalar", gtmp[:], gtmp[:], 1.0, None, ALU.add, reads=rd(gtmp), writes=rd(gtmp))
                S.op("dve", "reciprocal", gtmp[:], gtmp[:], reads=rd(gtmp), writes=rd(gtmp))
                S.op("dve", "tensor_tensor", gate[:, ts, :], gtmp[:], pb[:, 128:384], ALU.mult,
                     reads=rd(gtmp, pb), writes=rd(gate))

            if stop == 2:
                S.enabled = False
            shr, shk, shv, shz = sh
            S.op("act", "activation", th[0:64, :], shz[0:64, :], AF.Exp, scale=2.0, reads=rd(shz), writes=rd(th))
            S.op("dve", "tensor_scalar", th[0:64, :], th[0:64, :], 1.0, None, ALU.add, reads=rd(th), writes=rd(th))
            S.op("dve", "reciprocal", th[0:64, :], th[0:64, :], reads=rd(th), writes=rd(th))
            S.op("dve", "tensor_scalar", th[0:64, :], th[0:64, :], -2.0, 1.0, ALU.mult, ALU.add, reads=rd(th), writes=rd(th))
            pdl = ipbank()
            S.op("pe", "matmul", pdl[:], w2a2[0:64, :], th[0:64, :], start=True, stop=True, reads=rd(w2a2, th), writes=rd(pdl))
            S.op("act", "activation", e1[:], pdl[:], AF.Exp, scale=-1.0, bias=NEGW0, reads=rd(pdl, der), writes=rd(e1))
            S.op("act", "activation", e1[:], e1[:], AF.Ln, bias=1.0, reads=rd(e1), writes=rd(e1))
            S.op("act", "activation", ew[:], e1[:], AF.Exp, scale=-1.0, bias=-0.5, reads=rd(e1), writes=rd(ew))
            if stop == 21:
                S.enabled = False
            pda = ipbank()
            S.op("pe", "matmul", pda[:], w2a2[64:128, :], shz[64:128, :], start=True, stop=True, reads=rd(w2a2, shz), writes=rd(pda))
            S.op("act", "activation", alpha[:], pda[:], AF.Exp, scale=-1.0, bias=NEGA0, reads=rd(pda, der), writes=rd(alpha))
            S.op("dve", "tensor_scalar", alpha[:], alpha[:], 1.0, None, ALU.add, reads=rd(alpha), writes=rd(alpha))
            S.op("dve", "reciprocal", alpha[:], alpha[:], reads=rd(alpha), writes=rd(alpha))
            if stop == 22:
                S.enabled = False
            S.op("dve", "tensor_scalar", kks[:], shk[:], KK, None, ALU.mult, reads=rd(shk, pv), writes=rd(kks))
            S.op("pool", "tensor_tensor", sq[:], kks[:], kks[:], ALU.mult, reads=rd(kks), writes=rd(sq))
            pss = ipbank()
            S.op("pe", "matmul", pss[:], blockones[:], sq[:], start=True, stop=True, reads=rd(blockones, sq), writes=rd(pss))
            S.op("dve", "tensor_scalar", rn[:], pss[:], 1e-24, None, ALU.max, reads=rd(pss), writes=rd(rn))
            S.op("act", "activation", rn[:], rn[:], AF.Ln, reads=rd(rn), writes=rd(rn))
            S.op("act", "activation", rn[:], rn[:], AF.Exp, scale=-0.5, reads=rd(rn), writes=rd(rn))
            S.op("dve", "tensor_tensor", kkn[:], kks[:], rn[:], ALU.mult, reads=rd(kks, rn), writes=rd(kkn))
            S.op("pool", "tensor_tensor", bv[:], kkn[:], alpha[:], ALU.mult, reads=rd(kkn, alpha), writes=rd(bv))
            S.op("dve", "tensor_scalar", kmod[:], alpha[:], KA, OMKA, ALU.mult, ALU.add, reads=rd(alpha, pv, der), writes=rd(kmod))
            S.op("dve", "tensor_tensor", kmod[:], kmod[:], shk[:], ALU.mult, reads=rd(kmod, shk), writes=rd(kmod))
            if stop == 23:
                S.enabled = False
            S.op("dve", "tensor_tensor_scan", cwn[:], resetm[:], ew[:], 0.0, ALU.mult, ALU.add, reads=rd(resetm, ew), writes=rd(cwn))
            S.op("act", "activation", gam[:], cwn[:], AF.Exp, scale=-1.0, reads=rd(cwn), writes=rd(gam))
            S.op("act", "activation", ginv[:], cwn[:], AF.Exp, reads=rd(cwn), writes=rd(ginv))
            S.op("pool", "tensor_tensor", gprev[:], cwn[:], ew[:], ALU.subtract, reads=rd(cwn, ew), writes=rd(gprev))
            S.op("act", "activation", gprev[:], gprev[:], AF.Exp, scale=-1.0, reads=rd(gprev), writes=rd(gprev))
            cwn3 = cwn[:].rearrange("p (c t) -> p c t", t=128)
            S.op("dve", "tensor_scalar", ncC[:], cwn3[:, :, 127], -1.0, None, ALU.mult, reads=rd(cwn), writes=rd(ncC))
            for c in range(4):
                S.op("act", "activation", gco[:, c * 128:(c + 1) * 128], cwn[:, c * 128:(c + 1) * 128], AF.Exp,
                                                        bias=ncC[:, c:c + 1], reads=rd(cwn, ncC), writes=rd(gco))
            if stop == 24:
                S.enabled = False
            for h in range(2):
                hs = slice(64 * h, 64 * h + 64)
                S.op("dve", "tensor_tensor", RtTh[h][hs, :], shr[hs, :], gam[hs, :], ALU.mult, reads=rd(shr, gam), writes=rd(RtTh[h]))
                S.op("dve", "scalar_tensor_tensor", AtTh[h][hs, :], kkn[hs, :], -1.0, gprev[hs, :], ALU.mult, ALU.mult, reads=rd(kkn, gprev), writes=rd(AtTh[h]))
            S.op("dve", "tensor_tensor", KtT[:], kmod[:], ginv[:], ALU.mult, reads=rd(kmod, ginv), writes=rd(KtT))
            S.op("pool", "tensor_tensor", BtT[:], bv[:], ginv[:], ALU.mult, reads=rd(bv, ginv), writes=rd(BtT))
            S.op("dve", "tensor_tensor", KcT[:], kmod[:], gco[:], ALU.mult, reads=rd(kmod, gco), writes=rd(KcT))
            S.op("pool", "tensor_tensor", BcT[:], bv[:], gco[:], ALU.mult, reads=rd(bv, gco), writes=rd(BcT))
            S.op("act", "activation", VT[:], shv[:], AF.Copy, reads=rd(shv), writes=rd(VT))
            for h in range(2):
                hs = slice(64 * h, 64 * h + 64)
                S.op("dve", "scalar_tensor_tensor", rkTh[h][hs, :], shr[hs, :], pv[hs, 40:41], kmod[hs, :], ALU.mult, ALU.mult, reads=rd(shr, kmod, pv), writes=rd(rkTh[h]))
            if stop == 25:
                S.enabled = False
            for c in range(4):
                cs = slice(c * 128, (c + 1) * 128)
                pT = bank[B_T]
                pTv = bview(B_T, "p (k t) -> p k t", dt=BF16, t=128)
                for i, src in enumerate([KcT, BcT, VT]):
                    S.op("pe", "transpose", pTv[:, i, :], src[:, cs], ident[:], reads=rd(src, ident), writes=rd(pT))
                S.op("act", "activation", tokm[c][:], pTv[:, 0:3, :], AF.Copy, reads=rd(pT), writes=rd(tokm[c]))
            if stop == 26:
                S.enabled = False
            pbn = ipbank()
            for c in range(4):
                cs = slice(c * 128, (c + 1) * 128)
                for h in range(2):
                    hs = slice(64 * h, 64 * h + 64)
                    S.op("pe", "matmul", pbn[:, c * 128 + 64 * h: c * 128 + 64 * h + 64], rkTh[h][:, cs], ones_bf[:, :],
                                                                           start=True, stop=True, reads=rd(rkTh[h], ones_bf), writes=rd(pbn))
            for c in range(4):
                S.op("dve", "tensor_tensor", yb[:, c, :], pbn[:, c * 128:(c + 1) * 128], tokm[c][:, 2, :], ALU.mult,
                     reads=rd(pbn, tokm[c]), writes=rd(yb))

            if stop == 3:
                S.enabled = False
            for c in range(4):
                cs = slice(c * 128, (c + 1) * 128)
                pa = bank[B_SQ[c % 2]]
                pav = bview(B_SQ[c % 2], "p (k t) -> p k t", t=128)
                for h in range(2):
                    hs = slice(64 * h, 64 * h + 64)
                    S.op("pe", "matmul", pav[:, 2 * h, :], AtTh[h][:, cs], BtT[:, cs], start=True, stop=True,
                         reads=rd(AtTh[h], BtT), writes=rd(pa))
                    S.op("pe", "matmul", pav[:, 2 * h + 1, :], BtT[:, cs], AtTh[h][:, cs], start=True, stop=True,
                         reads=rd(AtTh[h], BtT), writes=rd(pa))
                L0 = Lab[c][0]
                S.op("dve", "tensor_tensor", L0[:], pav[:, :, :], maskA[:], ALU.mult, reads=rd(pa, maskA), writes=rd(L0))
                pb_ = ipbank()
                pbv = pb_[:].rearrange("p (k t) -> p k t", t=128)
                combos = [(KtT, AtTh[0]), (KtT, AtTh[1]), (BtT, RtTh[0]), (BtT, RtTh[1])]
                for i in range(4):
                    h = i % 2
                    hs = slice(64 * h, 64 * h + 64)
                    l, r = combos[i]
                    S.op("pe", "matmul", pbv[:, i, :], l[:, cs], r[:, cs], start=True, stop=True,
                         reads=rd(l, r), writes=rd(pb_))
                S.op("dve", "tensor_tensor", Mx[c][:, 0:4, :], pbv[:, :, :], maskB[:, 0:4, :], ALU.mult,
                     reads=rd(pb_, maskB), writes=rd(Mx[c]))
                pc_ = ipbank()
                pcv = pc_[:].rearrange("p (k t) -> p k t", t=128)
                for h in range(2):
                    hs = slice(64 * h, 64 * h + 64)
                    S.op("pe", "matmul", pcv[:, h, :], KtT[:, cs], RtTh[h][:, cs], start=True, stop=True,
                         reads=rd(KtT, RtTh[h]), writes=rd(pc_))
                S.op("dve", "tensor_tensor", Mx[c][:, 4:6, :], pcv[:, 0:2, :], maskB[:, 4:6, :], ALU.mult,
                     reads=rd(pc_, maskB), writes=rd(Mx[c]))
                M0 = MT[c][0]
                for h in range(2):
                    S.op("pool", "tensor_tensor", M0[:, h, :], L0[:, 2 * h + 1, :], ident[:], ALU.add,
                         reads=rd(L0, ident), writes=rd(M0))
            for k in range(1, 7):
                for c in range(4):
                    Lp = Lab[c][(k - 1) % 2]
                    Ln_ = Lab[c][k % 2]
                    Mp = MT[c][(k - 1) % 2]
                    Mn = MT[c][k % 2]
                    pa = bank[B_SQ[c % 2]]
                    pav = bview(B_SQ[c % 2], "p (k t) -> p k t", t=128)
                    for h in range(2):
                        S.op("pe", "matmul", pav[:, 2 * h, :], Lp[:, 2 * h + 1, :], Lp[:, 2 * h, :], start=True, stop=True,
                             reads=rd(Lp), writes=rd(pa))
                        S.op("pe", "matmul", pav[:, 2 * h + 1, :], Lp[:, 2 * h, :], Lp[:, 2 * h + 1, :], start=True, stop=True,
                             reads=rd(Lp), writes=rd(pa))
                    S.op("act", "activation", Ln_[:], pav[:, :, :], AF.Copy, reads=rd(pa), writes=rd(Ln_))
                    pm = bank[B_MM]
                    pmv = bview(B_MM, "p (k t) -> p k t", t=128)
                    off = 2 * (c % 2)
                    for h in range(2):
                        S.op("pe", "matmul", pmv[:, off + h, :], Ln_[:, 2 * h, :], Mp[:, h, :], start=True, stop=True,
                             reads=rd(Ln_, Mp), writes=rd(pm))
                    S.op("dve", "tensor_tensor", Mn[:], pmv[:, off:off + 2, :], Mp[:], ALU.add,
                         reads=rd(pm, Mp), writes=rd(Mn))
            MTf = [MT[c][0] for c in range(4)]

            if stop == 4:
                S.enabled = False
            gam3 = gam[:].rearrange("p (c t) -> p c t", t=128)
            for c in range(4):
                cs = slice(c * 128, (c + 1) * 128)
                pm = bank[B_MM]
                Vt = tokm[c]
                for h in range(2):
                    hs = slice(64 * h, 64 * h + 64)
                    S.op("pe", "matmul", pm[:, hs], AtTh[h][:, cs], Tb[:, :], start=True, stop=False,
                         reads=rd(AtTh[h], Tb), writes=rd(pm))
                    S.op("pe", "matmul", pm[:, hs], Mx[c][:, h, :], Vt[:, 2, hs], start=False, stop=True,
                         reads=rd(Mx[c], Vt), writes=rd(pm))
                S.op("dve", "tensor_copy", Xb[:], pm[:, 0:128], reads=rd(pm), writes=rd(Xb))
                for h in range(2):
                    hs = slice(64 * h, 64 * h + 64)
                    S.op("pe", "matmul", pm[:, 128 + 64 * h:128 + 64 * h + 64], MTf[c][:, h, :], Xb[:, hs], start=True, stop=True,
                         reads=rd(MTf[c], Xb), writes=rd(pm))
                S.op("act", "activation", Ub[:], pm[:, 128:256], AF.Copy, reads=rd(pm), writes=rd(Ub))
                for h in range(2):
                    hs = slice(64 * h, 64 * h + 64)
                    ys = slice(256 + 64 * h, 256 + 64 * h + 64)
                    S.op("pe", "matmul", pm[:, ys], RtTh[h][:, cs], Tb[:, :], start=True, stop=False,
                         reads=rd(RtTh[h], Tb), writes=rd(pm))
                    S.op("pe", "matmul", pm[:, ys], Mx[c][:, 2 + h, :], Ub[:, hs], start=False, stop=False,
                         reads=rd(Mx[c], Ub), writes=rd(pm))
                    S.op("pe", "matmul", pm[:, ys], Mx[c][:, 4 + h, :], Vt[:, 2, hs], start=False, stop=True,
                         reads=rd(Mx[c], Vt), writes=rd(pm))
                S.op("act", "activation", yr[:, c, :], pm[:, 256:384], AF.Copy, reads=rd(pm), writes=rd(yr))
                for h in range(2):
                    hs = slice(64 * h, 64 * h + 64)
                    S.op("pe", "matmul", pm[hs, 384:448], Vt[:, 1, hs], Ub[:, hs], start=True, stop=False,
                         reads=rd(Vt, Ub), writes=rd(pm))
                    S.op("pe", "matmul", pm[hs, 384:448], Vt[:, 0, hs], Vt[:, 2, hs], start=False, stop=True,
                         reads=rd(Vt), writes=rd(pm))
                S.op("dve", "scalar_tensor_tensor", Tst[:], Tst[:], gam3[:, c, 127:128], pm[:, 384:448], ALU.mult, ALU.add,
                     reads=rd(Tst, gam, pm), writes=rd(Tst))
                S.op("act", "activation", Tb[:], Tst[:], AF.Copy, reads=rd(Tst), writes=rd(Tb))

            yr8 = yr[:].rearrange("p c (h i) -> p (c h) i", i=64)
            ysq = gco
            ysq8 = ysq[:].rearrange("p (c h i) -> p (c h) i", h=2, i=64)
            S.op("dve", "tensor_reduce", st1[:], yr8, axis=AX.X, op=ALU.add, reads=rd(yr), writes=rd(st1))
            S.op("pool", "tensor_tensor", ysq[:], yr[:].rearrange("p c x -> p (c x)"), yr[:].rearrange("p c x -> p (c x)"), ALU.mult, reads=rd(yr), writes=rd(ysq))
            S.op("dve", "tensor_reduce", st2[:], ysq8, axis=AX.X, op=ALU.add, reads=rd(ysq), writes=rd(st2))
            S.op("dve", "tensor_scalar", st1[:], st1[:], 1.0 / 64, None, ALU.mult, reads=rd(st1), writes=rd(st1))
            S.op("dve", "tensor_tensor", st3[:], st1[:], st1[:], ALU.mult, reads=rd(st1), writes=rd(st3))
            S.op("dve", "scalar_tensor_tensor", st2[:], st2[:], 1.0 / 64, st3[:], ALU.mult, ALU.subtract, reads=rd(st2, st3), writes=rd(st2))
            S.op("act", "activation", st2[:], st2[:], AF.Ln, bias=GN_EPS, reads=rd(st2), writes=rd(st2))
            S.op("act", "activation", st2[:], st2[:], AF.Exp, scale=-0.5, reads=rd(st2), writes=rd(st2))
            S.op("dve", "tensor_tensor", yr8, yr8, st1[:].unsqueeze(2).to_broadcast([128, 8, 64]), ALU.subtract, reads=rd(yr, st1), writes=rd(yr))
            S.op("dve", "tensor_tensor", yr8, yr8, st2[:].unsqueeze(2).to_broadcast([128, 8, 64]), ALU.mult, reads=rd(yr, st2), writes=rd(yr))
            for c in range(4):
                S.op("pool", "tensor_tensor", yr[:, c, :], yr[:, c, :], bct[:, 0, :], ALU.mult, reads=rd(yr, bct), writes=rd(yr))
                S.op("pool", "tensor_tensor", yr[:, c, :], yr[:, c, :], bct[:, 1, :], ALU.add, reads=rd(yr, bct), writes=rd(yr))
            S.op("dve", "tensor_tensor", yr[:], yr[:], yb[:], ALU.add, reads=rd(yr, yb), writes=rd(yr))
            S.op("dve", "tensor_tensor", yst[:, :, 0:128], yr[:], gate[:, :, 0:128], ALU.mult, reads=rd(yr, gate), writes=rd(yst))

            if stop == 5:
                S.enabled = False
            nsb = 0
            for h in range(2):
                hs = slice(64 * h, 64 * h + 64)
                oa = oacc[h]
                nkb = 4 * tb + 4
                for kb in range(nkb):
                    dk = max(0, kb - 4 * tb)
                    q0 = dk * 128
                    qsl = slice(q0, 512)
                    i = nsb % NSB
                    bz, bcum, bP = [bank[j] for j in sbsets[nsb % 2]]
                    nsb += 1
                    et, spt, Et, att, sct = e_t[i], sp_t[i], E_t[i], at_t[i], sc_t[i]
                    S.op("pe", "matmul", bz[:, qsl], kT_all[:, kb * 128:(kb + 1) * 128], qTh[h][:, qsl],
                                                                               start=True, stop=True, reads=rd(kT_all, qTh[h]), writes=rd(bz))
                    S.op("act", "activation", et[:, qsl], bz[:, qsl], AF.Exp, scale=0.125, reads=rd(bz), writes=rd(et))
                    if kb >= 4 * tb:
                        S.op("pool", "affine_select",
                            et[:, qsl], et[:, qsl], pattern=[[1, 512 - q0]], compare_op=ALU.is_gt, fill=0.0,
                            base=q0 - dk * 128, channel_multiplier=-1, reads=rd(et), writes=rd(et))
                    S.op("act", "activation", spt[:, qsl], et[:, qsl], AF.Ln, bias=1.0, reads=rd(et), writes=rd(spt))
                    S.op("pe", "matmul", bcum[:, qsl], tri[:], spt[:, qsl], start=True, stop=True,
                         reads=rd(tri, spt), writes=rd(bcum))
                    for qs in range(dk, 4):
                        S.op("pe", "matmul", bP[:, 256 + qs:257 + qs], spt[:, qs * 128:(qs + 1) * 128], ones_bf[:, 0:1],
                                                                            start=True, stop=True, reads=rd(spt, ones_bf), writes=rd(bP))
                    S.op("act", "activation", Et[:, qsl], bcum[:, qsl], AF.Exp, scale=-1.0, reads=rd(bcum), writes=rd(Et))
                    S.op("dve", "tensor_tensor", att[:, qsl], et[:, qsl], Et[:, qsl], ALU.mult,
                         reads=rd(et, Et), writes=rd(att))
                    for qs in range(dk, 4):
                        S.op("pe", "matmul", bP[:, qs * 64:(qs + 1) * 64], att[:, qs * 128:(qs + 1) * 128],
                                                                                          v_all[:, kb, hs], start=True, stop=True,
                             reads=rd(att, v_all), writes=rd(bP))
                    bPv = bP[:, 0:256].rearrange("p (q d) -> p q d", d=64)
                    if kb == 0:
                        S.op("dve", "tensor_copy", oa[:], bPv, reads=rd(bP), writes=rd(oa))
                    else:
                        S.op("act", "activation", sct[:, dk:4], bP[:, 256 + dk:260], AF.Exp, scale=-1.0,
                             reads=rd(bP), writes=rd(sct))
                        S.op("pool", "tensor_tensor",
                            oa[:, dk:4, :], oa[:, dk:4, :], sct[:, dk:4].unsqueeze(2).to_broadcast([128, 4 - dk, 64]), ALU.mult,
                            reads=rd(oa, sct), writes=rd(oa))
                        S.op("dve", "tensor_tensor", oa[:, dk:4, :], oa[:, dk:4, :], bPv[:, dk:4, :], ALU.add,
                             reads=rd(oa, bP), writes=rd(oa))
                S.op("pool", "tensor_tensor", osq[:], oa[:], oa[:], ALU.mult, reads=rd(oa), writes=rd(osq))
                S.op("dve", "tensor_reduce", ost[:], osq[:], axis=AX.X, op=ALU.add, reads=rd(osq), writes=rd(ost))
                S.op("act", "activation", ost[:], ost[:], AF.Ln, scale=1.0 / 64, bias=NORM_EPS, reads=rd(ost), writes=rd(ost))
                S.op("act", "activation", ost[:], ost[:], AF.Exp, scale=-0.5, reads=rd(ost), writes=rd(ost))
                S.op("dve", "tensor_tensor", oa[:], oa[:], ost[:].unsqueeze(2).to_broadcast([128, 4, 64]), ALU.mult,
                     reads=rd(oa, ost), writes=rd(oa))
                for qs in range(4):
                    S.op("pool", "tensor_tensor", oa[:, qs, :], oa[:, qs, :], bct[:, 2, hs], ALU.mult,
                         reads=rd(oa, bct), writes=rd(oa))
                S.dma("sp", y_d.rearrange("(q nb) c -> q nb c", nb=NKB)[:, 4 * tb:4 * tb + 4, 128 + 64 * h:128 + 64 * h + 64],
                      oa[:], reads=rd(oa))
            S.dma("sp", y_d[tb * 512:(tb + 1) * 512, 0:128].rearrange("(ts p) c -> p ts c", p=128), yst[:, :, 0:128], reads=rd(yst))
            S.dma("sp", g_d[tb * 512:(tb + 1) * 512, :].rearrange("(ts p) c -> p ts c", p=128), gate[:, :, 128:256], reads=rd(gate))

        for ev in S.dma_events[-S.NDMA:]:
            S.wait_event("sp", ev)
        S.emit(st)
    return nc


def build_phase_b(ntok, final):
    NT = ntok // 128
    nc = bass.Bass("TRN2", target_bir_lowering=False)
    x_d = nc.dram_tensor("x", [ntok, 1024], F32, kind="ExternalInput").ap()
    y_d = nc.dram_tensor("y", [ntok, 1024], F32, kind="ExternalInput").ap()
    g_d = nc.dram_tensor("gs", [ntok, 512], F32, kind="ExternalInput").ap()
    wo_d = nc.dram_tensor("wo", [1024, 1024], F32, kind="ExternalInput").ap()
    adaw_d = nc.dram_tensor("adaw", [1024, 1024], F32, kind="ExternalInput").ap()
    pv_d = nc.dram_tensor("pv", [128, 8], F32, kind="ExternalInput").ap()
    brow_d = nc.dram_tensor("brow", [1, 1024], F32, kind="ExternalInput").ap()
    fg_d = nc.dram_tensor("fg", [128, 1024], F32, kind="ExternalInput").ap()
    o_d = nc.dram_tensor("xo", [ntok, 1024], F32, kind="ExternalOutput").ap()

    S = Sched(nc)
    st = ExitStack()
    with st:
        def sb(name, shape, dt=F32):
            return TL(st.enter_context(nc.sbuf_tensor("s_" + name, shape, dt)), name)

        def ps(name, shape, dt=F32):
            return TL(st.enter_context(nc.psum_tensor("p_" + name, shape, dt)), name)

        def rd(*ts):
            return [t.b for t in ts]

        bank = [ps("bank%d" % i, [128, 512], F32) for i in range(8)]
        pv = sb("pv", [128, 8])
        brow = sb("brow", [1, 1024])
        fg = sb("fg", [128, 1024])
        S.dma("sp", pv[:], pv_d[:, :], writes=rd(pv))
        S.dma("sp", brow[:], brow_d[:, :], writes=rd(brow))
        S.dma("sp", fg[:], fg_d[:, :], writes=rd(fg))
        identf = sb("identf", [128, 128])
        ident = sb("ident", [128, 128], BF16)
        S.op("pool", "memset", identf[:], 1.0, writes=rd(identf))
        S.op("pool", "affine_select", identf[:], identf[:], pattern=[[-1, 128]], compare_op=ALU.is_equal,
             fill=0.0, base=0, channel_multiplier=1, reads=rd(identf), writes=rd(identf))
        S.op("dve", "tensor_copy", ident[:], identf[:], reads=rd(identf), writes=rd(ident))
        onesf = sb("onesf", [1, 128])
        S.op("pool", "memset", onesf[:], 1.0, writes=rd(onesf))
        cact = sb("cact", [128, 8])
        tmp8 = sb("tmp8", [128, 8])
        S.op("act", "activation", tmp8[:], pv[:, 0:8], AF.Exp, scale=-1.0, reads=rd(pv), writes=rd(tmp8))
        S.op("dve", "tensor_scalar", tmp8[:], tmp8[:], 1.0, None, ALU.add, reads=rd(tmp8), writes=rd(tmp8))
        S.op("dve", "reciprocal", tmp8[:], tmp8[:], reads=rd(tmp8), writes=rd(tmp8))
        S.op("dve", "tensor_tensor", cact[:], tmp8[:], pv[:, 0:8], ALU.mult, reads=rd(tmp8, pv), writes=rd(cact))
        stage = [sb("stage%d" % i, [128, 1024]) for i in range(2)]
        grow = sb("grow", [1, 1024])
        for kc in range(8):
            stg = stage[kc % 2]
            S.dma("sp", stg[:], adaw_d[kc * 128:(kc + 1) * 128, :], writes=rd(stg))
            for hf in range(2):
                S.op("pe", "matmul", bank[hf][0:1, :], cact[:, kc:kc + 1], stg[:, hf * 512:(hf + 1) * 512],
                     start=(kc == 0), stop=(kc == 7), reads=rd(cact, stg), writes=rd(bank[hf]))
        for hf in range(2):
            S.op("dve", "tensor_tensor", grow[:, hf * 512:(hf + 1) * 512], bank[hf][0:1, :], brow[:, hf * 512:(hf + 1) * 512], ALU.add,
                 reads=rd(bank[hf], brow), writes=rd(grow))
        gbc = sb("gbc", [128, 1024])
        for hf in range(2):
            S.op("pe", "matmul", bank[2 + hf][:], onesf[0:1, :], grow[0:1, hf * 512:(hf + 1) * 512], start=True, stop=True,
                 reads=rd(onesf, grow), writes=rd(bank[2 + hf]))
            S.op("act", "activation", gbc[:, hf * 512:(hf + 1) * 512], bank[2 + hf][:], AF.Copy, reads=rd(bank[2 + hf]), writes=rd(gbc))
        Wo = sb("Wo", [128, 8, 1024], BF16)
        for kc in range(8):
            stg = stage[kc % 2]
            S.dma("sp", stg[:], wo_d[kc * 128:(kc + 1) * 128, :], writes=rd(stg))
            S.op("dve", "tensor_tensor", Wo[:, kc, :], stg[:], gbc[:], ALU.mult, reads=rd(stg, gbc), writes=rd(Wo))

        xt = [sb("xt%d" % i, [128, 1024]) for i in range(2)]
        yt = [sb("yt%d" % i, [128, 1024]) for i in range(2)]
        gt = [sb("gt%d" % i, [128, 512]) for i in range(2)]
        yb = [sb("yb%d" % i, [128, 1024], BF16) for i in range(2)]
        yT = [sb("yT%d" % i, [128, 8, 128], BF16) for i in range(2)]
        xo = [sb("xo%d" % i, [128, 1024]) for i in range(2)]
        junk = sb("junk", [128, 1024], BF16)
        ss = sb("ss", [128, 1])
        for t in range(NT):
            i = t % 2
            rows = slice(t * 128, (t + 1) * 128)
            S.dma("sp", xt[i][:], x_d[rows, :], writes=rd(xt[i]))
            S.dma("sp", yt[i][:], y_d[rows, :], writes=rd(yt[i]))
            S.dma("sp", gt[i][:], g_d[rows, :], writes=rd(gt[i]))
            S.op("act", "activation", yb[i][:, 0:512], yt[i][:, 0:512], AF.Copy, reads=rd(yt[i]), writes=rd(yb[i]))
            S.op("dve", "tensor_tensor", yb[i][:, 512:1024], yt[i][:, 512:1024], gt[i][:], ALU.mult, reads=rd(yt[i], gt[i]), writes=rd(yb[i]))
            pT = bank[4 + i]
            pTv = pT[:].bitcast(BF16).rearrange("p (k t) -> p k t", t=128)
            for kc in range(8):
                S.op("pe", "transpose", pTv[:, kc, :], yb[i][:, kc * 128:(kc + 1) * 128], ident[:], reads=rd(yb[i], ident), writes=rd(pT))
            S.op("act", "activation", yT[i][:], pTv[:, 0:8, :], AF.Copy, reads=rd(pT), writes=rd(yT[i]))
            for hf in range(2):
                pb = bank[hf * 2 + i]
                for kc in range(8):
                    S.op("pe", "matmul", pb[:], yT[i][:, kc, :], Wo[:, kc, hf * 512:(hf + 1) * 512], start=(kc == 0), stop=(kc == 7),
                         reads=rd(yT[i], Wo), writes=rd(pb))
                S.op("dve", "tensor_tensor", xo[i][:, hf * 512:(hf + 1) * 512], pb[:], xt[i][:, hf * 512:(hf + 1) * 512], ALU.add,
                     reads=rd(pb, xt[i]), writes=rd(xo[i]))
            if final:
                S.op("act", "activation", junk[:], xo[i][:], AF.Square, scale=1.0 / 32.0, accum_out=ss[:], reads=rd(xo[i]), writes=rd(junk, ss))
                S.op("act", "activation", ss[:], ss[:], AF.Ln, bias=NORM_EPS, reads=rd(ss), writes=rd(ss))
                S.op("act", "activation", ss[:], ss[:], AF.Exp, scale=-0.5, reads=rd(ss), writes=rd(ss))
                S.op("dve", "scalar_tensor_tensor", xo[i][:], xo[i][:], ss[:, 0:1], fg[:], ALU.mult, ALU.mult, reads=rd(xo[i], ss, fg), writes=rd(xo[i]))
            S.dma("sp", o_d[rows, :], xo[i][:], reads=rd(xo[i]))
        for ev in S.dma_events[-S.NDMA:]:
            S.wait_event("sp", ev)
        S.emit(st)
    return nc


D=1024; RW=512; SHIFT=1664; RWKV_COLS=2176

def cols_for(hp):
    r = np.arange(128)+128*hp
    return dict(r=r, k=512+r, v=1024+r, zz=np.arange(1536,1664), g=1664+r,
                q=RWKV_COLS+r, ks=RWKV_COLS+512+r, vs=RWKV_COLS+1024+r, gs=RWKV_COLS+1536+r)

def prep_a(inp, l, xcur, S_len):
    maps = []
    for core in range(8):
        b, hp = core // 4, core % 4
        cs = cols_for(hp)
        order = [cs['r'], cs['k'], cs['v'], cs['zz'], cs['q'], cs['ks'], cs['vs'], cs['g'], cs['gs']]
        wc = np.ascontiguousarray(inp['w_in'][l][:, np.concatenate(order)])
        adaw = np.ascontiguousarray(inp['ada_w'][l][:, :2048])
        pv = np.zeros((128, 48), np.float32)
        t8 = lambda v: np.ascontiguousarray(v.reshape(8, 128).T)
        pv[:, 0:8] = t8(inp['c'][b])
        pv[:, 8:16] = t8(inp['norm_g'][l])
        pv[:, 16:24] = t8(inp['ada_b'][l][0:1024])
        pv[:, 24:32] = t8(inp['ada_b'][l][1024:2048])
        mu = inp['tshift_mu'][l]
        pv[:, 32] = mu[cs['r']]; pv[:, 33] = mu[cs['k']]; pv[:, 34] = mu[cs['v']]; pv[:, 35] = mu[cs['zz']]
        hc = cs['r']
        pv[:, 36] = inp['decay_w0'][l][hc]; pv[:, 37] = inp['iclr_a0'][l][hc]
        pv[:, 38] = inp['k_k'][l][hc]; pv[:, 39] = inp['k_a'][l][hc]
        pv[:, 40] = inp['r_k'][l].reshape(512)[hc]
        w2a2 = np.concatenate([inp['decay_w2'][l][:, hc], inp['iclr_a2'][l][:, hc]], axis=0).astype(np.float32)
        bc = np.stack([np.broadcast_to(inp['rwkv_ln_w'][l][hc], (128, 128)),
                       np.broadcast_to(inp['rwkv_ln_b'][l][hc], (128, 128)),
                       np.broadcast_to(inp['sb_norm_g'][l][hc], (128, 128))], axis=1).astype(np.float32)
        maps.append(dict(x=np.ascontiguousarray(xcur[b, :S_len]), wc=wc, adaw=adaw, pv=pv,
                         w2a2=np.ascontiguousarray(w2a2), bc=np.ascontiguousarray(bc)))
    return maps

def assemble_y(results, S_len):
    y = np.zeros((2, S_len, 1024), np.float32)
    g = np.zeros((2, S_len, 512), np.float32)
    for core in range(8):
        b, hp = core // 4, core % 4
        yc = results[core]['y']
        y[b, :, 128*hp:128*hp+128] = yc[:, 0:128]
        y[b, :, 512+128*hp:512+128*hp+128] = yc[:, 128:256]
        g[b, :, 128*hp:128*hp+128] = results[core]['gs']
    return y, g


_CACHE = {}


def _get(kind, *args):
    key = (kind,) + args
    if key not in _CACHE:
        _CACHE[key] = build_phase_a(*args) if kind == "a" else build_phase_b(*args)
    return _CACHE[key]


def prep_b(inp, l, xcur, y, g):
    maps = []
    t8 = lambda v: np.ascontiguousarray(v.reshape(8, 128).T)
    xf = xcur.reshape(16384, 1024)
    yf = y.reshape(16384, 1024)
    gf = g.reshape(16384, 512)
    for core in range(8):
        b = core // 4
        rows = slice(core * 2048, (core + 1) * 2048)
        maps.append(dict(
            x=np.ascontiguousarray(xf[rows]), y=np.ascontiguousarray(yf[rows]), gs=np.ascontiguousarray(gf[rows]),
            wo=np.ascontiguousarray(inp['w_out'][l]), adaw=np.ascontiguousarray(inp['ada_w'][l][:, 2048:3072]),
            pv=t8(inp['c'][b]), brow=np.ascontiguousarray(inp['ada_b'][l][2048:3072].reshape(1, 1024)),
            fg=np.ascontiguousarray(np.broadcast_to(inp['final_g'], (128, 1024)))))
    return maps


def kernel(**inputs):
    inp = {k: np.asarray(v, dtype=np.float32) for k, v in inputs.items()}
    S_len = inp['x'].shape[1]
    depth = inp['w_in'].shape[0]
    xcur = inp['x']
    for l in range(depth):
        nca = _get("a", S_len)
        res = run_bass_kernel_spmd(nca, prep_a(inp, l, xcur, S_len), core_ids=list(range(8)))
        y, g = assemble_y(res.results, S_len)
        ncb = _get("b", 2048, l == depth - 1)
        res = run_bass_kernel_spmd(ncb, prep_b(inp, l, xcur, y, g), core_ids=list(range(8)))
        xcur = np.concatenate([r['xo'] for r in res.results], axis=0).reshape(2, S_len, 1024)
    return np.ascontiguousarray(xcur.astype(np.float32))
```

```python
from contextlib import ExitStack
from concourse.bass_utils import run_bass_kernel_spmd
import numpy as np
import concourse.bass as bass
import concourse.mybir as mybir

F32 = mybir.dt.float32
BF16 = mybir.dt.bfloat16
ALU = mybir.AluOpType
AF = mybir.ActivationFunctionType
AX = mybir.AxisListType


class Buf:
    __slots__ = ("name", "w", "r")

    def __init__(self, name):
        self.name = name
        self.w = None
        self.r = []


class Sched:
    CENG = ("pe", "act", "dve", "pool")
    NDMA = 24

    def __init__(self, nc):
        self.nc = nc
        self.streams = {e: [] for e in self.CENG + ("sp",)}
        self.count = {e: 0 for e in self.CENG}
        self.known = {e: {} for e in self.CENG + ("sp",)}
        self.ndma = 0
        self.dma_events = []
        self.enabled = True

    def _waits(self, eng, reads, writes):
        need = {}

        def add(ev):
            if ev is None:
                return
            k, v = ev
            if need.get(k, 0) < v:
                need[k] = v
        for b in reads:
            add(b.w)
        for b in writes:
            add(b.w)
            for ev in b.r:
                add(ev)
        for k, v in need.items():
            if k == eng:
                if eng == "pe":
                    continue
            if self.known[eng].get(k, 0) >= v:
                continue
            self.known[eng][k] = v
            self.streams[eng].append(("wait", k, v))

    def op(self, eng, meth, *args, reads=(), writes=(), **kw):
        if not self.enabled:
            return None
        self._waits(eng, reads, writes)
        self.count[eng] += 1
        ev = (eng, self.count[eng])
        self.streams[eng].append(("op", meth, args, kw))
        for b in reads:
            b.r.append(ev)
        for b in writes:
            b.w = ev
            b.r = []
        return ev

    def dma(self, q, out, in_, reads=(), writes=()):
        if not self.enabled:
            return None
        n = self.ndma
        self.ndma += 1
        slot = n % self.NDMA
        k = ("d", slot)
        prev = 16 * (n // self.NDMA)
        if prev > 0 and self.known[q].get(k, 0) < prev:
            self.known[q][k] = prev
            self.streams[q].append(("wait", k, prev))
        self._waits(q, reads, writes)
        ev = (k, prev + 16)
        self.streams[q].append(("dma", out, in_, slot))
        for b in reads:
            b.r.append(ev)
        for b in writes:
            b.w = ev
            b.r = []
        self.dma_events.append(ev)
        return ev

    def wait_event(self, eng, ev):
        k, v = ev
        if self.known[eng].get(k, 0) >= v:
            return
        self.known[eng][k] = v
        self.streams[eng].append(("wait", k, v))

    def emit(self, stack):
        nc = self.nc
        sems = {}
        for e in self.CENG:
            sems[e] = stack.enter_context(nc.semaphore("c_" + e))
        for s in range(self.NDMA):
            sems[("d", s)] = stack.enter_context(nc.semaphore("d%d" % s))
        block = stack.enter_context(nc.Block())
        streams = self.streams

        def run(engname, eng):
            for item in streams[engname]:
                if item[0] == "wait":
                    eng.wait_ge(sems[item[1]], item[2])
                elif item[0] == "op":
                    getattr(eng, item[1])(*item[2], **item[3]).then_inc(sems[engname], 1)
                else:
                    _, out, in_, slot = item
                    eng.dma_start(out=out, in_=in_).then_inc(sems[("d", slot)], 16)

        @block.tensor
        def _(e):
            run("pe", e)

        @block.scalar
        def _(e):
            run("act", e)

        @block.vector
        def _(e):
            run("dve", e)

        @block.gpsimd
        def _(e):
            run("pool", e)

        @block.sync
        def _(e):
            run("sp", e)


NORM_EPS = 1e-6
GN_EPS = 64e-5


class TL:
    def __init__(self, h, name):
        self.h = h
        self.b = Buf(name)

    def __getitem__(self, k):
        return self.h[k]


def build_phase_a(S_len):
    NB = S_len // 512
    NKB = S_len // 128
    nc = bass.Bass("TRN2", target_bir_lowering=False)
    x_d = nc.dram_tensor("x", [S_len, 1024], F32, kind="ExternalInput").ap()
    wc_d = nc.dram_tensor("wc", [1024, 1152], F32, kind="ExternalInput").ap()
    adaw_d = nc.dram_tensor("adaw", [1024, 2048], F32, kind="ExternalInput").ap()
    pv_d = nc.dram_tensor("pv", [128, 48], F32, kind="ExternalInput").ap()
    w2a2_d = nc.dram_tensor("w2a2", [128, 128], F32, kind="ExternalInput").ap()
    bc_d = nc.dram_tensor("bc", [128, 3, 128], F32, kind="ExternalInput").ap()
    y_d = nc.dram_tensor("y", [S_len, 256], F32, kind="ExternalOutput").ap()
    g_d = nc.dram_tensor("gs", [S_len, 128], F32, kind="ExternalOutput").ap()

    S = Sched(nc)
    st = ExitStack()
    with st:
        def sb(name, shape, dt=F32):
            return TL(st.enter_context(nc.sbuf_tensor("s_" + name, shape, dt)), name)

        def ps(name, shape, dt=F32):
            return TL(st.enter_context(nc.psum_tensor("p_" + name, shape, dt)), name)

        def rd(*ts):
            return [t.b for t in ts]

        bank = [ps("bank%d" % i, [128, 512], F32) for i in range(8)]

        def bview(i, shape_str=None, dt=None, **kw):
            ap = bank[i][:]
            if dt is not None:
                ap = ap.bitcast(dt)
            if shape_str:
                ap = ap.rearrange(shape_str, **kw)
            return ap

        pv = sb("pv", [128, 48])
        w2a2 = sb("w2a2", [128, 128])
        bct = sb("bct", [128, 3, 128])
        S.dma("sp", pv[:], pv_d[:, :], writes=rd(pv))
        S.dma("sp", w2a2[:], w2a2_d[:, :], writes=rd(w2a2))
        S.dma("sp", bct[:], bc_d[:, :, :], writes=rd(bct))

        identf = sb("identf", [128, 128])
        ident = sb("ident", [128, 128], BF16)
        S.op("pool", "memset", identf[:], 1.0, writes=rd(identf))
        S.op("pool", "affine_select", identf[:], identf[:], pattern=[[-1, 128]], compare_op=ALU.is_equal,
                                               fill=0.0, base=0, channel_multiplier=1, reads=rd(identf), writes=rd(identf))
        S.op("dve", "tensor_copy", ident[:], identf[:], reads=rd(identf), writes=rd(ident))

        def mkmask(name, pat, cm, op):
            m = sb(name, [128, 128])
            S.op("pool", "memset", m[:], 1.0, writes=rd(m))
            S.op("pool", "affine_select", m[:], m[:], pattern=[[pat, 128]], compare_op=op, fill=0.0,
                                                   base=0, channel_multiplier=cm, reads=rd(m), writes=rd(m))
            return m
        m_su = mkmask("m_su", 1, -1, ALU.is_gt)
        m_iu = mkmask("m_iu", 1, -1, ALU.is_ge)
        m_sl = mkmask("m_sl", -1, 1, ALU.is_gt)
        m_il = mkmask("m_il", -1, 1, ALU.is_ge)
        maskA = sb("maskA", [128, 4, 128])
        maskB = sb("maskB", [128, 6, 128])
        for i, m in enumerate([m_sl, m_su, m_sl, m_su]):
            S.op("pool", "tensor_copy", maskA[:, i, :], m[:], reads=rd(m), writes=rd(maskA))
        for i, m in enumerate([m_su, m_su, m_iu, m_iu, m_iu, m_iu]):
            S.op("pool", "tensor_copy", maskB[:, i, :], m[:], reads=rd(m), writes=rd(maskB))
        tri = sb("tri", [128, 128], BF16)
        S.op("dve", "tensor_copy", tri[:], m_il[:], reads=rd(m_il), writes=rd(tri))
        ones_bf = sb("ones_bf", [128, 64], BF16)
        S.op("pool", "memset", ones_bf[:], 1.0, writes=rd(ones_bf))
        blockones = sb("blockones", [128, 128])
        S.op("pool", "memset", blockones[:], 0.0, writes=rd(blockones))
        S.op("pool", "memset", blockones[0:64, 0:64], 1.0, writes=rd(blockones))
        S.op("pool", "memset", blockones[64:128, 64:128], 1.0, writes=rd(blockones))
        resetm = sb("resetm", [128, 512])
        S.op("pool", "memset", resetm[:], 1.0, writes=rd(resetm))
        S.op("pool", "memset", resetm[:].rearrange("p (c t) -> p c t", t=128)[:, :, 0:1], 0.0, writes=rd(resetm))

        cact = sb("cact", [128, 8])
        tmp8 = sb("tmp8", [128, 8])
        S.op("act", "activation", tmp8[:], pv[:, 0:8], AF.Exp, scale=-1.0, reads=rd(pv), writes=rd(tmp8))
        S.op("dve", "tensor_scalar", tmp8[:], tmp8[:], 1.0, None, ALU.add, reads=rd(tmp8), writes=rd(tmp8))
        S.op("dve", "reciprocal", tmp8[:], tmp8[:], reads=rd(tmp8), writes=rd(tmp8))
        S.op("dve", "tensor_tensor", cact[:], tmp8[:], pv[:, 0:8], ALU.mult, reads=rd(tmp8, pv), writes=rd(cact))

        stage = [sb("stage0", [128, 1152])] * 2
        modp = bank[0]
        for kc in range(8):
            for hf in range(2):
                stg = stage[hf]
                S.dma("sp", stg[:, 0:1024], adaw_d[kc * 128:(kc + 1) * 128, hf * 1024:(hf + 1) * 1024], writes=rd(stg))
                for jj in range(8):
                    j = hf * 8 + jj
                    S.op("pe", "matmul",
                        modp[:, kc * 16 + j: kc * 16 + j + 1], stg[:, jj * 128:(jj + 1) * 128], cact[:, kc:kc + 1],
                        start=True, stop=True, reads=rd(stg, cact), writes=rd(modp))
        modT = sb("modT", [128, 16])
        S.op("dve", "tensor_reduce", modT[:], modp[:, 0:128].rearrange("p (k j) -> p j k", j=16),
                                              axis=AX.X, op=ALU.add, reads=rd(modp), writes=rd(modT))
        Bco = sb("Bco", [128, 8])
        Aco = sb("Aco", [128, 8])
        S.op("dve", "tensor_tensor", Bco[:], modT[:, 0:8], pv[:, 16:24], ALU.add, reads=rd(modT, pv), writes=rd(Bco))
        S.op("dve", "tensor_tensor", Aco[:], modT[:, 8:16], pv[:, 24:32], ALU.add, reads=rd(modT, pv), writes=rd(Aco))
        S.op("dve", "scalar_tensor_tensor", Aco[:], Aco[:], 1.0, pv[:, 8:16], ALU.add, ALU.mult,
             reads=rd(Aco, pv), writes=rd(Aco))
        der = sb("der", [128, 4])
        S.op("dve", "tensor_scalar", der[:, 0:2], pv[:, 36:38], -1.0, None, ALU.mult, reads=rd(pv), writes=rd(der))
        S.op("dve", "tensor_scalar", der[:, 2:3], pv[:, 39:40], -1.0, 1.0, ALU.mult, ALU.add, reads=rd(pv), writes=rd(der))
        MU = lambda g: pv[:, 32 + g:33 + g]
        NEGW0, NEGA0, OMKA = der[:, 0:1], der[:, 1:2], der[:, 2:3]
        KK, KA, RK = pv[:, 38:39], pv[:, 39:40], pv[:, 40:41]

        W = sb("W", [128, 8, 1152], BF16)
        for kc in range(8):
            stg = stage[kc % 2]
            S.dma("sp", stg[:, 0:1152], wc_d[kc * 128:(kc + 1) * 128, :], writes=rd(stg))
            eng = "dve" if kc % 2 == 0 else "act"
            if eng == "dve":
                S.op("dve", "tensor_copy", W[:, kc, :], stg[:, 0:1152], reads=rd(stg), writes=rd(W))
            else:
                S.op("act", "activation", W[:, kc, :], stg[:, 0:1152], AF.Copy, reads=rd(stg), writes=rd(W))

        kT_all = sb("kT_all", [128, S_len], BF16)
        v_all = sb("v_all", [128, NKB, 128], BF16)
        raw = [sb("raw%d" % g, [128, 513]) for g in range(4)]
        for g in range(4):
            S.op("pool", "memset", raw[g][:, 0:1], 0.0, writes=rd(raw[g]))
        Tst = sb("Tst", [128, 64])
        Tb = sb("Tb", [128, 64], BF16)
        S.op("pool", "memset", Tst[:], 0.0, writes=rd(Tst))
        S.op("pool", "memset", Tb[:], 0.0, writes=rd(Tb))

        xt = [sb("xt%d" % i, [128, 1024]) for i in range(2)]
        junk = sb("junk", [128, 1024], BF16)
        ss = sb("ss", [128, 4])
        rstd = sb("rstd", [128, 4])
        xn = [sb("xn%d" % i, [128, 1024], BF16) for i in range(2)]
        tmpf = sb("tmpf", [128, 8, 128])
        hT = [sb("hT%d" % i, [128, 8, 512], BF16) for i in range(1)]
        sh = [sb("sh%d" % g, [128, 512]) for g in range(4)]
        dif = sb("dif", [128, 512])
        gate = sb("gate", [128, 4, 256])
        gtmp = sb("gtmp", [128, 256])
        th = dif
        e1 = sb("e1", [128, 512])
        ew = sb("ew", [128, 512])
        alpha = sb("alpha", [128, 512])
        kks = sb("kks", [128, 512])
        sq = e1
        rn = dif
        kkn = sb("kkn", [128, 512])
        bv = sb("bv", [128, 512])
        kmod = sb("kmod", [128, 512])
        cwn = sb("cwn", [128, 512])
        gam = sb("gam", [128, 512])
        ginv = sb("ginv", [128, 512])
        gprev = e1
        gco = sb("gco", [128, 512])
        ncC = sb("ncC", [128, 4])
        RtTh = [sb("RtTh%d" % h, [128, 512], BF16) for h in range(2)]
        AtTh = [sb("AtTh%d" % h, [128, 512], BF16) for h in range(2)]
        rkTh = [sb("rkTh%d" % h, [128, 512], BF16) for h in range(2)]
        qTh2 = [[sb("qTh%d_%d" % (i, h), [128, 512], BF16) for h in range(2)] for i in range(2)]
        kTbuf = [Buf("kT%d" % i) for i in range(NB)]
        vbuf = [Buf("v%d" % i) for i in range(NB)]
        for _t in RtTh + AtTh + rkTh + qTh2[0] + qTh2[1]:
            S.op("pool", "memset", _t[:], 0.0, writes=rd(_t))
        KtT = sb("KtT", [128, 512], BF16)
        BtT = sb("BtT", [128, 512], BF16)
        KcT = sb("KcT", [128, 512], BF16)
        BcT = sb("BcT", [128, 512], BF16)
        VT = sb("VT", [128, 512], BF16)
        tokm = [sb("tokm%d" % c, [128, 3, 128], BF16) for c in range(4)]
        Lab = [[sb("Lab%d_%d" % (c, i), [128, 4, 128], BF16) for i in range(2)] for c in range(4)]
        Mx = [sb("Mx%d" % c, [128, 6, 128], BF16) for c in range(4)]
        MT = [[sb("MT%d_%d" % (c, i), [128, 2, 128], BF16) for i in range(2)] for c in range(4)]
        Xb = sb("Xb", [128, 128], BF16)
        Ub = sb("Ub", [128, 128], BF16)
        yr = sb("yr", [128, 4, 128])
        yb = sb("yb", [128, 4, 128])
        st1 = sb("st1", [128, 8])
        st2 = sb("st2", [128, 8])
        st3 = sb("st3", [128, 8])
        ystage = [sb("ystage%d" % i, [128, 4, 128]) for i in range(1)]
        NSB = 2
        e_t = [sb("e_t%d" % i, [128, 512]) for i in range(3)]
        sp_t = [sb("sp_t%d" % i, [128, 512], BF16) for i in range(3)]
        E_t = [sb("E_t%d" % i, [128, 512], BF16) for i in range(2)]
        sc_t = [sb("sc_t%d" % i, [128, 4]) for i in range(NSB)]
        oacc = [sb("oacc%d" % h, [128, 4, 64]) for h in range(2)]
        osq = sb("osq", [128, 4, 64])
        ost = sb("ost", [128, 4])

        B_T = 0
        B_IP = [1, 2]
        B_SQ = [1, 2]
        B_MM = 0
        ZC = [5, 6, 7]
        PP = [3, 4]

        ipn = [0]

        def ipbank():
            b = B_IP[ipn[0] % 2]
            ipn[0] += 1
            return bank[b]

        def stageA(tb):
            hTb = hT[0]
            yst = ystage[0]
            qTh = qTh2[tb % 2]
            xts = [xt[0], xt[1], xt[0], xt[1]]

            def xload(ts):
                S.dma("sp", xts[ts][:], x_d[tb * 512 + ts * 128: tb * 512 + (ts + 1) * 128, :], writes=rd(xts[ts]))
            xload(0)
            xload(1)
            for ts in range(4):
                S.op("act", "activation", junk[:], xts[ts][:], AF.Square, scale=1.0 / 32.0,
                                                          accum_out=ss[:, ts:ts + 1], reads=rd(xts[ts]), writes=rd(junk, ss))
                S.op("act", "activation", rstd[:, ts:ts + 1], ss[:, ts:ts + 1], AF.Ln, bias=NORM_EPS, reads=rd(ss), writes=rd(rstd))
                S.op("act", "activation", rstd[:, ts:ts + 1], rstd[:, ts:ts + 1], AF.Exp, scale=-0.5, reads=rd(rstd), writes=rd(rstd))
                xnb = xn[ts % 2]
                S.op("dve", "tensor_scalar", xnb[:], xts[ts][:], rstd[:, ts:ts + 1], None, ALU.mult,
                     reads=rd(xts[ts], rstd), writes=rd(xnb))
                pT = bank[B_T]
                pTv = bview(B_T, "p (k t) -> p k t", dt=BF16, t=128)
                for kc in range(8):
                    S.op("pe", "transpose", pTv[:, kc, :], xnb[:, kc * 128:(kc + 1) * 128], ident[:],
                         reads=rd(xnb, ident), writes=rd(pT))
                S.op("dve", "tensor_tensor", tmpf[:], pTv[:, 0:8, :], Aco[:].unsqueeze(2).to_broadcast([128, 8, 128]), ALU.mult,
                     reads=rd(pT, Aco), writes=rd(tmpf))
                S.op("pool", "tensor_tensor", hTb[:, :, ts * 128:(ts + 1) * 128], tmpf[:],
                                                               Bco[:].unsqueeze(2).to_broadcast([128, 8, 128]), ALU.add,
                     reads=rd(tmpf, Bco), writes=rd(hTb))
                if ts + 2 < 4:
                    xload(ts + 2)
                yield
            for g in range(6):
                pb = ipbank()
                for kc in range(8):
                    S.op("pe", "matmul", pb[:], W[:, kc, g * 128:(g + 1) * 128], hTb[:, kc, :],
                                                                     start=(kc == 0), stop=(kc == 7),
                         reads=rd(W, hTb), writes=rd(pb))
                if g < 4:
                    S.op("act", "activation", raw[g][:, 1:513], pb[:], AF.Copy, reads=rd(pb), writes=rd(raw[g]))
                    S.op("dve", "tensor_tensor", dif[:], raw[g][:, 0:512], raw[g][:, 1:513], ALU.subtract,
                         reads=rd(raw[g]), writes=rd(dif))
                    S.op("dve", "scalar_tensor_tensor", sh[g][:], dif[:], MU(g), raw[g][:, 1:513], ALU.mult, ALU.add,
                         reads=rd(dif, raw[g], pv), writes=rd(sh[g]))
                    S.op("pool", "tensor_copy", raw[g][:, 0:1], raw[g][:, 512:513], reads=rd(raw[g]), writes=rd(raw[g]))
                elif g == 4:
                    for h in range(2):
                        hs = slice(64 * h, 64 * h + 64)
                        S.op("act", "activation", qTh[h][hs, :], pb[hs, :], AF.Copy, reads=rd(pb), writes=rd(qTh[h]))
                else:
                    S.op("act", "activation", kT_all[:, tb * 512:(tb + 1) * 512], pb[:], AF.Copy,
                         reads=rd(pb), writes=[kTbuf[tb]])
                yield
            for ts in range(4):
                pb = ipbank()
                for kc in range(8):
                    S.op("pe", "matmul", pb[:, 0:384], hTb[:, kc, ts * 128:(ts + 1) * 128],
                                                                       W[:, kc, 768:1152], start=(kc == 0), stop=(kc == 7),
                         reads=rd(W, hTb), writes=rd(pb))
                S.op("act", "activation", v_all[:, tb * 4 + ts, :], pb[:, 0:128], AF.Copy,
                     reads=rd(pb), writes=[vbuf[tb]])
                S.op("act", "activation", gtmp[:], pb[:, 128:384], AF.Exp, scale=-1.0, reads=rd(pb), writes=rd(gtmp))
                S.op("dve", "tensor_scalar", gtmp[:], gtmp[:], 1.0, None, ALU.add, reads=rd(gtmp), writes=rd(gtmp))
                S.op("dve", "reciprocal", gtmp[:], gtmp[:], reads=rd(gtmp), writes=rd(gtmp))
                S.op("dve", "tensor_tensor", gate[:, ts, :], gtmp[:], pb[:, 128:384], ALU.mult,
                     reads=rd(gtmp, pb), writes=rd(gate))
                yield

            shr, shk, shv, shz = sh
            S.op("act", "activation", th[0:64, :], shz[0:64, :], AF.Exp, scale=2.0, reads=rd(shz), writes=rd(th))
            S.op("dve", "tensor_scalar", th[0:64, :], th[0:64, :], 1.0, None, ALU.add, reads=rd(th), writes=rd(th))
            S.op("dve", "reciprocal", th[0:64, :], th[0:64, :], reads=rd(th), writes=rd(th))
            S.op("dve", "tensor_scalar", th[0:64, :], th[0:64, :], -2.0, 1.0, ALU.mult, ALU.add, reads=rd(th), writes=rd(th))
            pdl = ipbank()
            S.op("pe", "matmul", pdl[:], w2a2[0:64, :], th[0:64, :], start=True, stop=True, reads=rd(w2a2, th), writes=rd(pdl))
            S.op("act", "activation", e1[:], pdl[:], AF.Exp, scale=-1.0, bias=NEGW0, reads=rd(pdl, der), writes=rd(e1))
            S.op("act", "activation", e1[:], e1[:], AF.Ln, bias=1.0, reads=rd(e1), writes=rd(e1))
            S.op("act", "activation", ew[:], e1[:], AF.Exp, scale=-1.0, bias=-0.5, reads=rd(e1), writes=rd(ew))
            yield
            pda = ipbank()
            S.op("pe", "matmul", pda[:], w2a2[64:128, :], shz[64:128, :], start=True, stop=True, reads=rd(w2a2, shz), writes=rd(pda))
            S.op("act", "activation", alpha[:], pda[:], AF.Exp, scale=-1.0, bias=NEGA0, reads=rd(pda, der), writes=rd(alpha))
            S.op("dve", "tensor_scalar", alpha[:], alpha[:], 1.0, None, ALU.add, reads=rd(alpha), writes=rd(alpha))
            S.op("dve", "reciprocal", alpha[:], alpha[:], reads=rd(alpha), writes=rd(alpha))
            yield
            S.op("dve", "tensor_scalar", kks[:], shk[:], KK, None, ALU.mult, reads=rd(shk, pv), writes=rd(kks))
            S.op("pool", "tensor_tensor", sq[:], kks[:], kks[:], ALU.mult, reads=rd(kks), writes=rd(sq))
            pss = ipbank()
            S.op("pe", "matmul", pss[:], blockones[:], sq[:], start=True, stop=True, reads=rd(blockones, sq), writes=rd(pss))
            yield
            S.op("dve", "tensor_scalar", rn[:], pss[:], 1e-24, None, ALU.max, reads=rd(pss), writes=rd(rn))
            S.op("act", "activation", rn[:], rn[:], AF.Ln, reads=rd(rn), writes=rd(rn))
            S.op("act", "activation", rn[:], rn[:], AF.Exp, scale=-0.5, reads=rd(rn), writes=rd(rn))
            S.op("dve", "tensor_tensor", kkn[:], kks[:], rn[:], ALU.mult, reads=rd(kks, rn), writes=rd(kkn))
            S.op("pool", "tensor_tensor", bv[:], kkn[:], alpha[:], ALU.mult, reads=rd(kkn, alpha), writes=rd(bv))
            S.op("dve", "tensor_scalar", kmod[:], alpha[:], KA, OMKA, ALU.mult, ALU.add, reads=rd(alpha, pv, der), writes=rd(kmod))
            S.op("dve", "tensor_tensor", kmod[:], kmod[:], shk[:], ALU.mult, reads=rd(kmod, shk), writes=rd(kmod))
            yield
            S.op("dve", "tensor_tensor_scan", cwn[:], resetm[:], ew[:], 0.0, ALU.mult, ALU.add, reads=rd(resetm, ew), writes=rd(cwn))
            S.op("act", "activation", gam[:], cwn[:], AF.Exp, scale=-1.0, reads=rd(cwn), writes=rd(gam))
            S.op("act", "activation", ginv[:], cwn[:], AF.Exp, reads=rd(cwn), writes=rd(ginv))
            S.op("pool", "tensor_tensor", gprev[:], cwn[:], ew[:], ALU.subtract, reads=rd(cwn, ew), writes=rd(gprev))
            S.op("act", "activation", gprev[:], gprev[:], AF.Exp, scale=-1.0, reads=rd(gprev), writes=rd(gprev))
            yield
            cwn3 = cwn[:].rearrange("p (c t) -> p c t", t=128)
            S.op("dve", "tensor_scalar", ncC[:], cwn3[:, :, 127], -1.0, None, ALU.mult, reads=rd(cwn), writes=rd(ncC))
            for c in range(4):
                S.op("act", "activation", gco[:, c * 128:(c + 1) * 128], cwn[:, c * 128:(c + 1) * 128], AF.Exp,
                                                        bias=ncC[:, c:c + 1], reads=rd(cwn, ncC), writes=rd(gco))
            for h in range(2):
                hs = slice(64 * h, 64 * h + 64)
                S.op("dve", "tensor_tensor", RtTh[h][hs, :], shr[hs, :], gam[hs, :], ALU.mult, reads=rd(shr, gam), writes=rd(RtTh[h]))
                S.op("dve", "scalar_tensor_tensor", AtTh[h][hs, :], kkn[hs, :], -1.0, gprev[hs, :], ALU.mult, ALU.mult, reads=rd(kkn, gprev), writes=rd(AtTh[h]))
            S.op("dve", "tensor_tensor", KtT[:], kmod[:], ginv[:], ALU.mult, reads=rd(kmod, ginv), writes=rd(KtT))
            S.op("pool", "tensor_tensor", BtT[:], bv[:], ginv[:], ALU.mult, reads=rd(bv, ginv), writes=rd(BtT))
            S.op("dve", "tensor_tensor", KcT[:], kmod[:], gco[:], ALU.mult, reads=rd(kmod, gco), writes=rd(KcT))
            S.op("pool", "tensor_tensor", BcT[:], bv[:], gco[:], ALU.mult, reads=rd(bv, gco), writes=rd(BcT))
            S.op("act", "activation", VT[:], shv[:], AF.Copy, reads=rd(shv), writes=rd(VT))
            yield
            for h in range(2):
                hs = slice(64 * h, 64 * h + 64)
                S.op("dve", "scalar_tensor_tensor", rkTh[h][hs, :], shr[hs, :], pv[hs, 40:41], kmod[hs, :], ALU.mult, ALU.mult, reads=rd(shr, kmod, pv), writes=rd(rkTh[h]))
            for c in range(4):
                cs = slice(c * 128, (c + 1) * 128)
                pT = bank[B_T]
                pTv = bview(B_T, "p (k t) -> p k t", dt=BF16, t=128)
                for i, src in enumerate([KcT, BcT, VT]):
                    S.op("pe", "transpose", pTv[:, i, :], src[:, cs], ident[:], reads=rd(src, ident), writes=rd(pT))
                S.op("act", "activation", tokm[c][:], pTv[:, 0:3, :], AF.Copy, reads=rd(pT), writes=rd(tokm[c]))
                yield
            pbn = ipbank()
            for c in range(4):
                cs = slice(c * 128, (c + 1) * 128)
                for h in range(2):
                    hs = slice(64 * h, 64 * h + 64)
                    S.op("pe", "matmul", pbn[:, c * 128 + 64 * h: c * 128 + 64 * h + 64], rkTh[h][:, cs], ones_bf[:, :],
                                                                           start=True, stop=True, reads=rd(rkTh[h], ones_bf), writes=rd(pbn))
            for c in range(4):
                S.op("dve", "tensor_tensor", yb[:, c, :], pbn[:, c * 128:(c + 1) * 128], tokm[c][:, 2, :], ALU.mult,
                     reads=rd(pbn, tokm[c]), writes=rd(yb))
                yield

            for c in range(4):
                cs = slice(c * 128, (c + 1) * 128)
                pa = bank[B_SQ[c % 2]]
                pav = bview(B_SQ[c % 2], "p (k t) -> p k t", t=128)
                for h in range(2):
                    hs = slice(64 * h, 64 * h + 64)
                    S.op("pe", "matmul", pav[:, 2 * h, :], AtTh[h][:, cs], BtT[:, cs], start=True, stop=True,
                         reads=rd(AtTh[h], BtT), writes=rd(pa))
                    S.op("pe", "matmul", pav[:, 2 * h + 1, :], BtT[:, cs], AtTh[h][:, cs], start=True, stop=True,
                         reads=rd(AtTh[h], BtT), writes=rd(pa))
                L0 = Lab[c][0]
                S.op("dve", "tensor_tensor", L0[:], pav[:, :, :], maskA[:], ALU.mult, reads=rd(pa, maskA), writes=rd(L0))
                yield
                pb_ = ipbank()
                pbv = pb_[:].rearrange("p (k t) -> p k t", t=128)
                combos = [(KtT, AtTh[0]), (KtT, AtTh[1]), (BtT, RtTh[0]), (BtT, RtTh[1])]
                for i in range(4):
                    h = i % 2
                    hs = slice(64 * h, 64 * h + 64)
                    l, r = combos[i]
                    S.op("pe", "matmul", pbv[:, i, :], l[:, cs], r[:, cs], start=True, stop=True,
                         reads=rd(l, r), writes=rd(pb_))
                S.op("dve", "tensor_tensor", Mx[c][:, 0:4, :], pbv[:, :, :], maskB[:, 0:4, :], ALU.mult,
                     reads=rd(pb_, maskB), writes=rd(Mx[c]))
                pc_ = ipbank()
                pcv = pc_[:].rearrange("p (k t) -> p k t", t=128)
                for h in range(2):
                    hs = slice(64 * h, 64 * h + 64)
                    S.op("pe", "matmul", pcv[:, h, :], KtT[:, cs], RtTh[h][:, cs], start=True, stop=True,
                         reads=rd(KtT, RtTh[h]), writes=rd(pc_))
                S.op("dve", "tensor_tensor", Mx[c][:, 4:6, :], pcv[:, 0:2, :], maskB[:, 4:6, :], ALU.mult,
                     reads=rd(pc_, maskB), writes=rd(Mx[c]))
                yield
                M0 = MT[c][0]
                for h in range(2):
                    S.op("pool", "tensor_tensor", M0[:, h, :], L0[:, 2 * h + 1, :], ident[:], ALU.add,
                         reads=rd(L0, ident), writes=rd(M0))
            for k in range(1, 7):
                for c in range(4):
                    Lp = Lab[c][(k - 1) % 2]
                    Ln_ = Lab[c][k % 2]
                    Mp = MT[c][(k - 1) % 2]
                    Mn = MT[c][k % 2]
                    pa = bank[B_SQ[c % 2]]
                    pav = bview(B_SQ[c % 2], "p (k t) -> p k t", t=128)
                    for h in range(2):
                        S.op("pe", "matmul", pav[:, 2 * h, :], Lp[:, 2 * h + 1, :], Lp[:, 2 * h, :], start=True, stop=True,
                             reads=rd(Lp), writes=rd(pa))
                        S.op("pe", "matmul", pav[:, 2 * h + 1, :], Lp[:, 2 * h, :], Lp[:, 2 * h + 1, :], start=True, stop=True,
                             reads=rd(Lp), writes=rd(pa))
                    S.op("act", "activation", Ln_[:], pav[:, :, :], AF.Copy, reads=rd(pa), writes=rd(Ln_))
                    yield
                    pm = bank[B_MM]
                    pmv = bview(B_MM, "p (k t) -> p k t", t=128)
                    off = 2 * (c % 2)
                    for h in range(2):
                        S.op("pe", "matmul", pmv[:, off + h, :], Ln_[:, 2 * h, :], Mp[:, h, :], start=True, stop=True,
                             reads=rd(Ln_, Mp), writes=rd(pm))
                    S.op("dve", "tensor_tensor", Mn[:], pmv[:, off:off + 2, :], Mp[:], ALU.add,
                         reads=rd(pm, Mp), writes=rd(Mn))
                yield
            MTf = [MT[c][0] for c in range(4)]

            gam3 = gam[:].rearrange("p (c t) -> p c t", t=128)
            for c in range(4):
                cs = slice(c * 128, (c + 1) * 128)
                pm = bank[B_MM]
                Vt = tokm[c]
                for h in range(2):
                    hs = slice(64 * h, 64 * h + 64)
                    S.op("pe", "matmul", pm[:, hs], AtTh[h][:, cs], Tb[:, :], start=True, stop=False,
                         reads=rd(AtTh[h], Tb), writes=rd(pm))
                    S.op("pe", "matmul", pm[:, hs], Mx[c][:, h, :], Vt[:, 2, hs], start=False, stop=True,
                         reads=rd(Mx[c], Vt), writes=rd(pm))
                S.op("dve", "tensor_copy", Xb[:], pm[:, 0:128], reads=rd(pm), writes=rd(Xb))
                yield
                for h in range(2):
                    hs = slice(64 * h, 64 * h + 64)
                    S.op("pe", "matmul", pm[:, 128 + 64 * h:128 + 64 * h + 64], MTf[c][:, h, :], Xb[:, hs], start=True, stop=True,
                         reads=rd(MTf[c], Xb), writes=rd(pm))
                S.op("act", "activation", Ub[:], pm[:, 128:256], AF.Copy, reads=rd(pm), writes=rd(Ub))
                yield
                for h in range(2):
                    hs = slice(64 * h, 64 * h + 64)
                    ys = slice(256 + 64 * h, 256 + 64 * h + 64)
                    S.op("pe", "matmul", pm[:, ys], RtTh[h][:, cs], Tb[:, :], start=True, stop=False,
                         reads=rd(RtTh[h], Tb), writes=rd(pm))
                    S.op("pe", "matmul", pm[:, ys], Mx[c][:, 2 + h, :], Ub[:, hs], start=False, stop=False,
                         reads=rd(Mx[c], Ub), writes=rd(pm))
                    S.op("pe", "matmul", pm[:, ys], Mx[c][:, 4 + h, :], Vt[:, 2, hs], start=False, stop=True,
                         reads=rd(Mx[c], Vt), writes=rd(pm))
                S.op("act", "activation", yr[:, c, :], pm[:, 256:384], AF.Copy, reads=rd(pm), writes=rd(yr))
                for h in range(2):
                    hs = slice(64 * h, 64 * h + 64)
                    S.op("pe", "matmul", pm[hs, 384:448], Vt[:, 1, hs], Ub[:, hs], start=True, stop=False,
                         reads=rd(Vt, Ub), writes=rd(pm))
                    S.op("pe", "matmul", pm[hs, 384:448], Vt[:, 0, hs], Vt[:, 2, hs], start=False, stop=True,
                         reads=rd(Vt), writes=rd(pm))
                S.op("dve", "scalar_tensor_tensor", Tst[:], Tst[:], gam3[:, c, 127:128], pm[:, 384:448], ALU.mult, ALU.add,
                     reads=rd(Tst, gam, pm), writes=rd(Tst))
                S.op("act", "activation", Tb[:], Tst[:], AF.Copy, reads=rd(Tst), writes=rd(Tb))
                yield

            yr8 = yr[:].rearrange("p c (h i) -> p (c h) i", i=64)
            ysq = gco
            ysq8 = ysq[:].rearrange("p (c h i) -> p (c h) i", h=2, i=64)
            S.op("dve", "tensor_reduce", st1[:], yr8, axis=AX.X, op=ALU.add, reads=rd(yr), writes=rd(st1))
            S.op("pool", "tensor_tensor", ysq[:], yr[:].rearrange("p c x -> p (c x)"), yr[:].rearrange("p c x -> p (c x)"), ALU.mult, reads=rd(yr), writes=rd(ysq))
            S.op("dve", "tensor_reduce", st2[:], ysq8, axis=AX.X, op=ALU.add, reads=rd(ysq), writes=rd(st2))
            S.op("dve", "tensor_scalar", st1[:], st1[:], 1.0 / 64, None, ALU.mult, reads=rd(st1), writes=rd(st1))
            S.op("dve", "tensor_tensor", st3[:], st1[:], st1[:], ALU.mult, reads=rd(st1), writes=rd(st3))
            S.op("dve", "scalar_tensor_tensor", st2[:], st2[:], 1.0 / 64, st3[:], ALU.mult, ALU.subtract, reads=rd(st2, st3), writes=rd(st2))
            S.op("act", "activation", st2[:], st2[:], AF.Ln, bias=GN_EPS, reads=rd(st2), writes=rd(st2))
            S.op("act", "activation", st2[:], st2[:], AF.Exp, scale=-0.5, reads=rd(st2), writes=rd(st2))
            S.op("dve", "tensor_tensor", yr8, yr8, st1[:].unsqueeze(2).to_broadcast([128, 8, 64]), ALU.subtract, reads=rd(yr, st1), writes=rd(yr))
            S.op("dve", "tensor_tensor", yr8, yr8, st2[:].unsqueeze(2).to_broadcast([128, 8, 64]), ALU.mult, reads=rd(yr, st2), writes=rd(yr))
            for c in range(4):
                S.op("pool", "tensor_tensor", yr[:, c, :], yr[:, c, :], bct[:, 0, :], ALU.mult, reads=rd(yr, bct), writes=rd(yr))
                S.op("pool", "tensor_tensor", yr[:, c, :], yr[:, c, :], bct[:, 1, :], ALU.add, reads=rd(yr, bct), writes=rd(yr))
            S.op("dve", "tensor_tensor", yr[:], yr[:], yb[:], ALU.add, reads=rd(yr, yb), writes=rd(yr))
            S.op("dve", "tensor_tensor", yst[:, :, 0:128], yr[:], gate[:, :, 0:128], ALU.mult, reads=rd(yr, gate), writes=rd(yst))

            S.dma("sp", y_d[tb * 512:(tb + 1) * 512, 0:128].rearrange("(ts p) c -> p ts c", p=128), yst[:], reads=rd(yst))
            S.dma("sp", g_d[tb * 512:(tb + 1) * 512, :].rearrange("(ts p) c -> p ts c", p=128), gate[:, :, 128:256], reads=rd(gate))
            yield

        sbn = [0]

        def stageS(tb):
            qTh = qTh2[tb % 2]
            tiles = [(h, kb) for h in range(2) for kb in range(4 * tb + 4)]
            n = len(tiles)
            g0 = sbn[0]
            sbn[0] += n

            def res(i):
                g = g0 + i
                h, kb = tiles[i]
                dk = max(0, kb - 4 * tb)
                return dict(h=h, kb=kb, dk=dk, q0=dk * 128, qsl=slice(dk * 128, 512), hs=slice(64 * h, 64 * h + 64),
                            zc=bank[ZC[g % 3]], pp=bank[PP[g % 2]], et=e_t[g % 3], sa=sp_t[g % 3], Et=E_t[g % 2], sct=sc_t[g % 2],
                            oa=oacc[h])

            def front(i):
                r = res(i)
                S.op("pe", "matmul", r['zc'][:, r['qsl']], kT_all[:, r['kb'] * 128:(r['kb'] + 1) * 128], qTh[r['h']][:, r['qsl']],
                     start=True, stop=True, reads=[kTbuf[r['kb'] // 4]] + rd(qTh[r['h']]), writes=rd(r['zc']))

            def mid(i):
                r = res(i)
                qsl, et, sa, zc, pp, dk, q0 = r['qsl'], r['et'], r['sa'], r['zc'], r['pp'], r['dk'], r['q0']
                S.op("act", "activation", et[:, qsl], zc[:, qsl], AF.Exp, scale=0.125, reads=rd(zc), writes=rd(et))
                if r['kb'] >= 4 * tb:
                    S.op("pool", "affine_select", et[:, qsl], et[:, qsl], pattern=[[1, 512 - q0]], compare_op=ALU.is_gt, fill=0.0,
                         base=0, channel_multiplier=-1, reads=rd(et), writes=rd(et))
                S.op("act", "activation", sa[:, qsl], et[:, qsl], AF.Ln, bias=1.0, reads=rd(et), writes=rd(sa))
                S.op("pe", "matmul", zc[:, qsl], tri[:], sa[:, qsl], start=True, stop=True, reads=rd(tri, sa), writes=rd(zc))
                for qs in range(dk, 4):
                    S.op("pe", "matmul", pp[:, 256 + qs:257 + qs], sa[:, qs * 128:(qs + 1) * 128], ones_bf[:, 0:1],
                         start=True, stop=True, reads=rd(sa, ones_bf), writes=rd(pp))

            def back(i):
                r = res(i)
                qsl, et, sa, zc, pp, dk, Et, sct, oa, hs, kb, h = (r['qsl'], r['et'], r['sa'], r['zc'], r['pp'], r['dk'], r['Et'],
                                                                   r['sct'], r['oa'], r['hs'], r['kb'], r['h'])
                S.op("act", "activation", Et[:, qsl], zc[:, qsl], AF.Exp, scale=-1.0, reads=rd(zc), writes=rd(Et))
                if kb > 0:
                    S.op("act", "activation", sct[:, dk:4], pp[:, 256 + dk:260], AF.Exp, scale=-1.0, reads=rd(pp), writes=rd(sct))
                S.op("dve", "tensor_tensor", sa[:, qsl], et[:, qsl], Et[:, qsl], ALU.mult, reads=rd(et, Et), writes=rd(sa))
                for qs in range(dk, 4):
                    S.op("pe", "matmul", pp[:, qs * 64:(qs + 1) * 64], sa[:, qs * 128:(qs + 1) * 128], v_all[:, kb, hs],
                         start=True, stop=True, reads=rd(sa) + [vbuf[kb // 4]], writes=rd(pp))
                ppv = pp[:, 0:256].rearrange("p (q d) -> p q d", d=64)
                if kb == 0:
                    S.op("dve", "tensor_copy", oa[:], ppv, reads=rd(pp), writes=rd(oa))
                else:
                    S.op("pool", "tensor_tensor", oa[:, dk:4, :], oa[:, dk:4, :],
                         sct[:, dk:4].unsqueeze(2).to_broadcast([128, 4 - dk, 64]), ALU.mult, reads=rd(oa, sct), writes=rd(oa))
                    S.op("dve", "tensor_tensor", oa[:, dk:4, :], oa[:, dk:4, :], ppv[:, dk:4, :], ALU.add, reads=rd(oa, pp), writes=rd(oa))
                if kb == 4 * tb + 3:
                    S.op("pool", "tensor_tensor", osq[:], oa[:], oa[:], ALU.mult, reads=rd(oa), writes=rd(osq))
                    S.op("dve", "tensor_reduce", ost[:], osq[:], axis=AX.X, op=ALU.add, reads=rd(osq), writes=rd(ost))
                    S.op("act", "activation", ost[:], ost[:], AF.Ln, scale=1.0 / 64, bias=NORM_EPS, reads=rd(ost), writes=rd(ost))
                    S.op("act", "activation", ost[:], ost[:], AF.Exp, scale=-0.5, reads=rd(ost), writes=rd(ost))
                    S.op("dve", "tensor_tensor", oa[:], oa[:], ost[:].unsqueeze(2).to_broadcast([128, 4, 64]), ALU.mult,
                         reads=rd(oa, ost), writes=rd(oa))
                    for qs in range(4):
                        S.op("pool", "tensor_tensor", oa[:, qs, :], oa[:, qs, :], bct[:, 2, hs], ALU.mult, reads=rd(oa, bct), writes=rd(oa))
                    S.dma("sp", y_d.rearrange("(q nb) c -> q nb c", nb=NKB)[:, 4 * tb:4 * tb + 4, 128 + 64 * h:128 + 64 * h + 64],
                          oa[:], reads=rd(oa))

            front(0)
            if n > 1:
                front(1)
            mid(0)
            for i in range(n):
                if i + 2 < n:
                    front(i + 2)
                if i + 1 < n:
                    mid(i + 1)
                back(i)
                yield

        def drive(gens):
            lists = []
            for g in gens:
                lists.append(g)
            alive = list(lists)
            while alive:
                for g in list(alive):
                    try:
                        next(g)
                    except StopIteration:
                        alive.remove(g)

        def count(genf, tb):
            en = S.enabled
            S.enabled = False
            saved = sbn[0]
            c = sum(1 for _ in genf(tb))
            sbn[0] = saved
            S.enabled = en
            return c

        def merged(gS, nS, gA, nA):
            iS = iA = 0
            while iS < nS or iA < nA:
                if iA < nA and (iS >= nS or iA * max(nS, 1) <= iS * nA):
                    next(gA, None)
                    iA += 1
                else:
                    next(gS, None)
                    iS += 1
            for _ in gS:
                pass
            for _ in gA:
                pass

        for _ in stageA(0):
            pass
        for tb in range(NB):
            nS = count(stageS, tb)
            if tb + 1 < NB:
                nA = count(stageA, tb + 1)
                merged(stageS(tb), nS, stageA(tb + 1), nA)
            else:
                for _ in stageS(tb):
                    pass

        for ev in S.dma_events[-S.NDMA:]:
            S.wait_event("sp", ev)
        S.emit(st)
    return nc


def build_phase_b(ntok, final):
    NT = ntok // 128
    nc = bass.Bass("TRN2", target_bir_lowering=False)
    x_d = nc.dram_tensor("x", [ntok, 1024], F32, kind="ExternalInput").ap()
    y_d = nc.dram_tensor("y", [ntok, 1024], F32, kind="ExternalInput").ap()
    g_d = nc.dram_tensor("gs", [ntok, 512], F32, kind="ExternalInput").ap()
    wo_d = nc.dram_tensor("wo", [1024, 1024], F32, kind="ExternalInput").ap()
    adaw_d = nc.dram_tensor("adaw", [1024, 1024], F32, kind="ExternalInput").ap()
    pv_d = nc.dram_tensor("pv", [128, 8], F32, kind="ExternalInput").ap()
    brow_d = nc.dram_tensor("brow", [1, 1024], F32, kind="ExternalInput").ap()
    fg_d = nc.dram_tensor("fg", [128, 1024], F32, kind="ExternalInput").ap()
    o_d = nc.dram_tensor("xo", [ntok, 1024], F32, kind="ExternalOutput").ap()

    S = Sched(nc)
    st = ExitStack()
    with st:
        def sb(name, shape, dt=F32):
            return TL(st.enter_context(nc.sbuf_tensor("s_" + name, shape, dt)), name)

        def ps(name, shape, dt=F32):
            return TL(st.enter_context(nc.psum_tensor("p_" + name, shape, dt)), name)

        def rd(*ts):
            return [t.b for t in ts]

        bank = [ps("bank%d" % i, [128, 512], F32) for i in range(8)]
        pv = sb("pv", [128, 8])
        brow = sb("brow", [1, 1024])
        fg = sb("fg", [128, 1024])
        S.dma("sp", pv[:], pv_d[:, :], writes=rd(pv))
        S.dma("sp", brow[:], brow_d[:, :], writes=rd(brow))
        S.dma("sp", fg[:], fg_d[:, :], writes=rd(fg))
        identf = sb("identf", [128, 128])
        ident = sb("ident", [128, 128], BF16)
        S.op("pool", "memset", identf[:], 1.0, writes=rd(identf))
        S.op("pool", "affine_select", identf[:], identf[:], pattern=[[-1, 128]], compare_op=ALU.is_equal,
             fill=0.0, base=0, channel_multiplier=1, reads=rd(identf), writes=rd(identf))
        S.op("dve", "tensor_copy", ident[:], identf[:], reads=rd(identf), writes=rd(ident))
        onesf = sb("onesf", [1, 128])
        S.op("pool", "memset", onesf[:], 1.0, writes=rd(onesf))
        cact = sb("cact", [128, 8])
        tmp8 = sb("tmp8", [128, 8])
        S.op("act", "activation", tmp8[:], pv[:, 0:8], AF.Exp, scale=-1.0, reads=rd(pv), writes=rd(tmp8))
        S.op("dve", "tensor_scalar", tmp8[:], tmp8[:], 1.0, None, ALU.add, reads=rd(tmp8), writes=rd(tmp8))
        S.op("dve", "reciprocal", tmp8[:], tmp8[:], reads=rd(tmp8), writes=rd(tmp8))
        S.op("dve", "tensor_tensor", cact[:], tmp8[:], pv[:, 0:8], ALU.mult, reads=rd(tmp8, pv), writes=rd(cact))
        stage = [sb("stage%d" % i, [128, 1024]) for i in range(2)]
        grow = sb("grow", [1, 1024])
        for kc in range(8):
            stg = stage[kc % 2]
            S.dma("sp", stg[:], adaw_d[kc * 128:(kc + 1) * 128, :], writes=rd(stg))
            for hf in range(2):
                S.op("pe", "matmul", bank[hf][0:1, :], cact[:, kc:kc + 1], stg[:, hf * 512:(hf + 1) * 512],
                     start=(kc == 0), stop=(kc == 7), reads=rd(cact, stg), writes=rd(bank[hf]))
        for hf in range(2):
            S.op("dve", "tensor_tensor", grow[:, hf * 512:(hf + 1) * 512], bank[hf][0:1, :], brow[:, hf * 512:(hf + 1) * 512], ALU.add,
                 reads=rd(bank[hf], brow), writes=rd(grow))
        gbc = sb("gbc", [128, 1024])
        for hf in range(2):
            S.op("pe", "matmul", bank[2 + hf][:], onesf[0:1, :], grow[0:1, hf * 512:(hf + 1) * 512], start=True, stop=True,
                 reads=rd(onesf, grow), writes=rd(bank[2 + hf]))
            S.op("act", "activation", gbc[:, hf * 512:(hf + 1) * 512], bank[2 + hf][:], AF.Copy, reads=rd(bank[2 + hf]), writes=rd(gbc))
        Wo = sb("Wo", [128, 8, 1024], BF16)
        for kc in range(8):
            stg = stage[kc % 2]
            S.dma("sp", stg[:], wo_d[kc * 128:(kc + 1) * 128, :], writes=rd(stg))
            S.op("dve", "tensor_tensor", Wo[:, kc, :], stg[:], gbc[:], ALU.mult, reads=rd(stg, gbc), writes=rd(Wo))

        xt = [sb("xt%d" % i, [128, 1024]) for i in range(2)]
        yt = [sb("yt%d" % i, [128, 1024]) for i in range(2)]
        gt = [sb("gt%d" % i, [128, 512]) for i in range(2)]
        yb = [sb("yb%d" % i, [128, 1024], BF16) for i in range(2)]
        yT = [sb("yT%d" % i, [128, 8, 128], BF16) for i in range(2)]
        xo = [sb("xo%d" % i, [128, 1024]) for i in range(2)]
        junk = sb("junk", [128, 1024], BF16)
        ss = sb("ss", [128, 1])
        for t in range(NT):
            i = t % 2
            rows = slice(t * 128, (t + 1) * 128)
            S.dma("sp", xt[i][:], x_d[rows, :], writes=rd(xt[i]))
            S.dma("sp", yt[i][:], y_d[rows, :], writes=rd(yt[i]))
            S.dma("sp", gt[i][:], g_d[rows, :], writes=rd(gt[i]))
            S.op("act", "activation", yb[i][:, 0:512], yt[i][:, 0:512], AF.Copy, reads=rd(yt[i]), writes=rd(yb[i]))
            S.op("dve", "tensor_tensor", yb[i][:, 512:1024], yt[i][:, 512:1024], gt[i][:], ALU.mult, reads=rd(yt[i], gt[i]), writes=rd(yb[i]))
            pT = bank[4 + i]
            pTv = pT[:].bitcast(BF16).rearrange("p (k t) -> p k t", t=128)
            for kc in range(8):
                S.op("pe", "transpose", pTv[:, kc, :], yb[i][:, kc * 128:(kc + 1) * 128], ident[:], reads=rd(yb[i], ident), writes=rd(pT))
            S.op("act", "activation", yT[i][:], pTv[:, 0:8, :], AF.Copy, reads=rd(pT), writes=rd(yT[i]))
            for hf in range(2):
                pb = bank[hf * 2 + i]
                for kc in range(8):
                    S.op("pe", "matmul", pb[:], yT[i][:, kc, :], Wo[:, kc, hf * 512:(hf + 1) * 512], start=(kc == 0), stop=(kc == 7),
                         reads=rd(yT[i], Wo), writes=rd(pb))
                S.op("dve", "tensor_tensor", xo[i][:, hf * 512:(hf + 1) * 512], pb[:], xt[i][:, hf * 512:(hf + 1) * 512], ALU.add,
                     reads=rd(pb, xt[i]), writes=rd(xo[i]))
            if final:
                S.op("act", "activation", junk[:], xo[i][:], AF.Square, scale=1.0 / 32.0, accum_out=ss[:], reads=rd(xo[i]), writes=rd(junk, ss))
                S.op("act", "activation", ss[:], ss[:], AF.Ln, bias=NORM_EPS, reads=rd(ss), writes=rd(ss))
                S.op("act", "activation", ss[:], ss[:], AF.Exp, scale=-0.5, reads=rd(ss), writes=rd(ss))
                S.op("dve", "scalar_tensor_tensor", xo[i][:], xo[i][:], ss[:, 0:1], fg[:], ALU.mult, ALU.mult, reads=rd(xo[i], ss, fg), writes=rd(xo[i]))
            S.dma("sp", o_d[rows, :], xo[i][:], reads=rd(xo[i]))
        for ev in S.dma_events[-S.NDMA:]:
            S.wait_event("sp", ev)
        S.emit(st)
    return nc


D=1024; RW=512; SHIFT=1664; RWKV_COLS=2176

def cols_for(hp):
    r = np.arange(128)+128*hp
    return dict(r=r, k=512+r, v=1024+r, zz=np.arange(1536,1664), g=1664+r,
                q=RWKV_COLS+r, ks=RWKV_COLS+512+r, vs=RWKV_COLS+1024+r, gs=RWKV_COLS+1536+r)

def prep_a(inp, l, xcur, S_len):
    maps = []
    for core in range(8):
        b, hp = core // 4, core % 4
        cs = cols_for(hp)
        order = [cs['r'], cs['k'], cs['v'], cs['zz'], cs['q'], cs['ks'], cs['vs'], cs['g'], cs['gs']]
        wc = np.ascontiguousarray(inp['w_in'][l][:, np.concatenate(order)])
        adaw = np.ascontiguousarray(inp['ada_w'][l][:, :2048])
        pv = np.zeros((128, 48), np.float32)
        t8 = lambda v: np.ascontiguousarray(v.reshape(8, 128).T)
        pv[:, 0:8] = t8(inp['c'][b])
        pv[:, 8:16] = t8(inp['norm_g'][l])
        pv[:, 16:24] = t8(inp['ada_b'][l][0:1024])
        pv[:, 24:32] = t8(inp['ada_b'][l][1024:2048])
        mu = inp['tshift_mu'][l]
        pv[:, 32] = mu[cs['r']]; pv[:, 33] = mu[cs['k']]; pv[:, 34] = mu[cs['v']]; pv[:, 35] = mu[cs['zz']]
        hc = cs['r']
        pv[:, 36] = inp['decay_w0'][l][hc]; pv[:, 37] = inp['iclr_a0'][l][hc]
        pv[:, 38] = inp['k_k'][l][hc]; pv[:, 39] = inp['k_a'][l][hc]
        pv[:, 40] = inp['r_k'][l].reshape(512)[hc]
        w2a2 = np.concatenate([inp['decay_w2'][l][:, hc], inp['iclr_a2'][l][:, hc]], axis=0).astype(np.float32)
        bc = np.stack([np.broadcast_to(inp['rwkv_ln_w'][l][hc], (128, 128)),
                       np.broadcast_to(inp['rwkv_ln_b'][l][hc], (128, 128)),
                       np.broadcast_to(inp['sb_norm_g'][l][hc], (128, 128))], axis=1).astype(np.float32)
        maps.append(dict(x=np.ascontiguousarray(xcur[b, :S_len]), wc=wc, adaw=adaw, pv=pv,
                         w2a2=np.ascontiguousarray(w2a2), bc=np.ascontiguousarray(bc)))
    return maps

def assemble_y(results, S_len):
    y = np.zeros((2, S_len, 1024), np.float32)
    g = np.zeros((2, S_len, 512), np.float32)
    for core in range(8):
        b, hp = core // 4, core % 4
        yc = results[core]['y']
        y[b, :, 128*hp:128*hp+128] = yc[:, 0:128]
        y[b, :, 512+128*hp:512+128*hp+128] = yc[:, 128:256]
        g[b, :, 128*hp:128*hp+128] = results[core]['gs']
    return y, g


_CACHE = {}


def _get(kind, *args):
    key = (kind,) + args
    if key not in _CACHE:
        _CACHE[key] = build_phase_a(*args) if kind == "a" else build_phase_b(*args)
    return _CACHE[key]


def prep_b(inp, l, xcur, y, g):
    maps = []
    t8 = lambda v: np.ascontiguousarray(v.reshape(8, 128).T)
    xf = xcur.reshape(16384, 1024)
    yf = y.reshape(16384, 1024)
    gf = g.reshape(16384, 512)
    for core in range(8):
        b = core // 4
        rows = slice(core * 2048, (core + 1) * 2048)
        maps.append(dict(
            x=np.ascontiguousarray(xf[rows]), y=np.ascontiguousarray(yf[rows]), gs=np.ascontiguousarray(gf[rows]),
            wo=np.ascontiguousarray(inp['w_out'][l]), adaw=np.ascontiguousarray(inp['ada_w'][l][:, 2048:3072]),
            pv=t8(inp['c'][b]), brow=np.ascontiguousarray(inp['ada_b'][l][2048:3072].reshape(1, 1024)),
            fg=np.ascontiguousarray(np.broadcast_to(inp['final_g'], (128, 1024)))))
    return maps


def kernel(**inputs):
    inp = {k: np.asarray(v, dtype=np.float32) for k, v in inputs.items()}
    S_len = inp['x'].shape[1]
    depth = inp['w_in'].shape[0]
    xcur = inp['x']
    for l in range(depth):
        nca = _get("a", S_len)
        res = run_bass_kernel_spmd(nca, prep_a(inp, l, xcur, S_len), core_ids=list(range(8)))
        y, g = assemble_y(res.results, S_len)
        ncb = _get("b", 2048, l == depth - 1)
        res = run_bass_kernel_spmd(ncb, prep_b(inp, l, xcur, y, g), core_ids=list(range(8)))
        xcur = np.concatenate([r['xo'] for r in res.results], axis=0).reshape(2, S_len, 1024)
    return np.ascontiguousarray(xcur.astype(np.float32))
```

```python
from contextlib import ExitStack
import heapq
from concourse.bass_utils import run_bass_kernel_spmd
import numpy as np
import concourse.bass as bass
import concourse.mybir as mybir

F32 = mybir.dt.float32
BF16 = mybir.dt.bfloat16
ALU = mybir.AluOpType
AF = mybir.ActivationFunctionType
AX = mybir.AxisListType


class Buf:
    __slots__ = ("name", "w", "r")

    def __init__(self, name):
        self.name = name
        self.w = None
        self.r = []


def _fsize(ap):
    n = 1
    for s in tuple(ap.shape)[1:]:
        n *= int(s)
    return n


class Sched:
    CENG = ("pe", "act", "dve", "pool")
    ENGS = CENG + ("sp",)
    NDMA = 24
    LAT = 0.12

    def __init__(self, nc, reorder=True):
        self.nc = nc
        self.nodes = []
        self.enabled = True
        self.reorder = reorder

    def _record(self, eng, kind, meth, args, kw, reads, writes, dur):
        deps = set()
        for b in reads:
            if b.w is not None:
                deps.add(b.w)
        for b in writes:
            if b.w is not None:
                deps.add(b.w)
            deps.update(b.r)
        nid = len(self.nodes)
        self.nodes.append(dict(eng=eng, kind=kind, meth=meth, args=args, kw=kw, deps=deps, dur=dur))
        for b in reads:
            b.r.append(nid)
        for b in writes:
            b.w = nid
            b.r = []
        return nid

    def _est(self, eng, meth, args):
        if eng == "pe":
            if meth == "transpose":
                return 0.11
            n = _fsize(args[2])
            c = max(64, n) / 1400.0
            if args[1].dtype == F32:
                c *= 4
            return c + 0.03
        n = _fsize(args[0])
        if eng == "act":
            return 0.17 + n / 1400.0
        if eng == "dve":
            return 0.07 + n / 960.0
        return 0.12 + n / 640.0

    def op(self, eng, meth, *args, reads=(), writes=(), **kw):
        if not self.enabled:
            return None
        return self._record(eng, "op", meth, args, kw, reads, writes, self._est(eng, meth, args))

    def dma(self, q, out, in_, reads=(), writes=()):
        if not self.enabled:
            return None
        nbytes = _fsize(out) * int(tuple(out.shape)[0]) * 4
        return self._record(q, "dma", None, (out, in_), {}, reads, writes, 2.0 + nbytes / 60000.0)

    def _schedule(self):
        nodes = self.nodes
        n = len(nodes)
        order = {e: [] for e in self.ENGS}
        if not self.reorder:
            for i, nd in enumerate(nodes):
                order[nd["eng"]].append(i)
            return order
        succ = [[] for _ in range(n)]
        npend = [0] * n
        for i, nd in enumerate(nodes):
            npend[i] = len(nd["deps"])
            for d in nd["deps"]:
                succ[d].append(i)
        finish = [0.0] * n
        heaps = {e: [] for e in self.ENGS}
        t_e = {e: 0.0 for e in self.ENGS}

        def push(i):
            nd = nodes[i]
            rdy = 0.0
            for d in nd["deps"]:
                f = finish[d] + (0.0 if nodes[d]["eng"] == nd["eng"] == "pe" else self.LAT)
                if f > rdy:
                    rdy = f
            heapq.heappush(heaps[nd["eng"]], (rdy, i))
        for i in range(n):
            if npend[i] == 0:
                push(i)
        done = 0
        while done < n:
            best = None
            for e in self.ENGS:
                h = heaps[e]
                if not h:
                    continue
                rdy, i = h[0]
                start = max(t_e[e], rdy)
                key = (start, i)
                if best is None or key < best[0]:
                    best = (key, e)
            (start, i), e = best
            h = heaps[e]
            cands = []
            while h and max(t_e[e], h[0][0]) <= start + 1e-9:
                cands.append(heapq.heappop(h))
            cands.sort(key=lambda x: x[1])
            rdy, i = cands[0]
            for c in cands[1:]:
                heapq.heappush(h, c)
            nd = nodes[i]
            if nd["kind"] == "dma":
                t_e[e] = start + 0.06
                finish[i] = start + nd["dur"]
            else:
                t_e[e] = start + nd["dur"]
                finish[i] = t_e[e]
            order[e].append(i)
            done += 1
            for s in succ[i]:
                npend[s] -= 1
                if npend[s] == 0:
                    push(s)
        self.est_makespan = max(t_e.values())
        return order

    def emit(self, stack):
        nc = self.nc
        nodes = self.nodes
        order = self._schedule()
        event = [None] * len(nodes)
        for e in self.CENG:
            cnt = 0
            for i in order[e]:
                if nodes[i]["kind"] == "op":
                    cnt += 1
                    event[i] = (e, cnt)
        ndma = 0
        slot_of = {}
        for e in self.ENGS:
            for i in order[e]:
                if nodes[i]["kind"] == "dma":
                    assert e == "sp"
                    slot = ndma % self.NDMA
                    event[i] = (("d", slot), 16 * (ndma // self.NDMA + 1))
                    slot_of[i] = (slot, 16 * (ndma // self.NDMA))
                    ndma += 1
        streams = {e: [] for e in self.ENGS}
        for e in self.ENGS:
            known = {}
            for i in order[e]:
                nd = nodes[i]
                need = {}
                if nd["kind"] == "dma":
                    slot, prev = slot_of[i]
                    if prev > 0:
                        need[("d", slot)] = prev
                for d in nd["deps"]:
                    k, v = event[d]
                    if k == e == "pe":
                        continue
                    if need.get(k, 0) < v:
                        need[k] = v
                for k, v in need.items():
                    if known.get(k, 0) >= v:
                        continue
                    known[k] = v
                    streams[e].append(("wait", k, v))
                if nd["kind"] == "dma":
                    streams[e].append(("dma", nd["args"][0], nd["args"][1], slot_of[i][0]))
                else:
                    streams[e].append(("op", nd["meth"], nd["args"], nd["kw"]))
            if e == "sp":
                last = {}
                for i in order[e]:
                    if nodes[i]["kind"] == "dma":
                        k, v = event[i]
                        last[k] = max(last.get(k, 0), v)
                for k, v in last.items():
                    if known.get(k, 0) < v:
                        streams[e].append(("wait", k, v))

        sems = {}
        for e in self.CENG:
            sems[e] = stack.enter_context(nc.semaphore("c_" + e))
        for s in range(self.NDMA):
            sems[("d", s)] = stack.enter_context(nc.semaphore("d%d" % s))
        block = stack.enter_context(nc.Block())

        def run(engname, eng):
            for item in streams[engname]:
                if item[0] == "wait":
                    eng.wait_ge(sems[item[1]], item[2])
                elif item[0] == "op":
                    getattr(eng, item[1])(*item[2], **item[3]).then_inc(sems[engname], 1)
                else:
                    _, out, in_, slot = item
                    eng.dma_start(out=out, in_=in_).then_inc(sems[("d", slot)], 16)

        @block.tensor
        def _(e):
            run("pe", e)

        @block.scalar
        def _(e):
            run("act", e)

        @block.vector
        def _(e):
            run("dve", e)

        @block.gpsimd
        def _(e):
            run("pool", e)

        @block.sync
        def _(e):
            run("sp", e)


NORM_EPS = 1e-6
GN_EPS = 64e-5


class TL:
    def __init__(self, h, name):
        self.h = h
        self.b = Buf(name)

    def __getitem__(self, k):
        return self.h[k]


def build_phase_a(S_len):
    NB = S_len // 512
    NKB = S_len // 128
    nc = bass.Bass("TRN2", target_bir_lowering=False)
    x_d = nc.dram_tensor("x", [S_len, 1024], F32, kind="ExternalInput").ap()
    wc_d = nc.dram_tensor("wc", [1024, 1152], F32, kind="ExternalInput").ap()
    adaw_d = nc.dram_tensor("adaw", [1024, 2048], F32, kind="ExternalInput").ap()
    pv_d = nc.dram_tensor("pv", [128, 48], F32, kind="ExternalInput").ap()
    w2a2_d = nc.dram_tensor("w2a2", [128, 128], F32, kind="ExternalInput").ap()
    bc_d = nc.dram_tensor("bc", [128, 3, 128], F32, kind="ExternalInput").ap()
    y_d = nc.dram_tensor("y", [S_len, 256], F32, kind="ExternalOutput").ap()
    g_d = nc.dram_tensor("gs", [S_len, 128], F32, kind="ExternalOutput").ap()

    S = Sched(nc)
    st = ExitStack()
    with st:
        def sb(name, shape, dt=F32):
            return TL(st.enter_context(nc.sbuf_tensor("s_" + name, shape, dt)), name)

        def ps(name, shape, dt=F32):
            return TL(st.enter_context(nc.psum_tensor("p_" + name, shape, dt)), name)

        def rd(*ts):
            return [t.b for t in ts]

        bank = [ps("bank%d" % i, [128, 512], F32) for i in range(8)]

        def bview(i, shape_str=None, dt=None, **kw):
            ap = bank[i][:]
            if dt is not None:
                ap = ap.bitcast(dt)
            if shape_str:
                ap = ap.rearrange(shape_str, **kw)
            return ap

        pv = sb("pv", [128, 48])
        w2a2 = sb("w2a2", [128, 128])
        bct = sb("bct", [128, 3, 128])
        S.dma("sp", pv[:], pv_d[:, :], writes=rd(pv))
        S.dma("sp", w2a2[:], w2a2_d[:, :], writes=rd(w2a2))
        S.dma("sp", bct[:], bc_d[:, :, :], writes=rd(bct))

        identf = sb("identf", [128, 128])
        ident = sb("ident", [128, 128], BF16)
        S.op("pool", "memset", identf[:], 1.0, writes=rd(identf))
        S.op("pool", "affine_select", identf[:], identf[:], pattern=[[-1, 128]], compare_op=ALU.is_equal,
                                               fill=0.0, base=0, channel_multiplier=1, reads=rd(identf), writes=rd(identf))
        S.op("dve", "tensor_copy", ident[:], identf[:], reads=rd(identf), writes=rd(ident))

        def mkmask(name, pat, cm, op):
            m = sb(name, [128, 128])
            S.op("pool", "memset", m[:], 1.0, writes=rd(m))
            S.op("pool", "affine_select", m[:], m[:], pattern=[[pat, 128]], compare_op=op, fill=0.0,
                                                   base=0, channel_multiplier=cm, reads=rd(m), writes=rd(m))
            return m
        m_su = mkmask("m_su", 1, -1, ALU.is_gt)
        m_iu = mkmask("m_iu", 1, -1, ALU.is_ge)
        m_sl = mkmask("m_sl", -1, 1, ALU.is_gt)
        m_il = mkmask("m_il", -1, 1, ALU.is_ge)
        maskA = sb("maskA", [128, 4, 128])
        maskB = sb("maskB", [128, 6, 128])
        for i, m in enumerate([m_sl, m_su, m_sl, m_su]):
            S.op("pool", "tensor_copy", maskA[:, i, :], m[:], reads=rd(m), writes=rd(maskA))
        for i, m in enumerate([m_su, m_su, m_iu, m_iu, m_iu, m_iu]):
            S.op("pool", "tensor_copy", maskB[:, i, :], m[:], reads=rd(m), writes=rd(maskB))
        tri = sb("tri", [128, 128], BF16)
        S.op("dve", "tensor_copy", tri[:], m_il[:], reads=rd(m_il), writes=rd(tri))
        ones_bf = sb("ones_bf", [128, 64], BF16)
        S.op("pool", "memset", ones_bf[:], 1.0, writes=rd(ones_bf))
        blockones = sb("blockones", [128, 128])
        S.op("pool", "memset", blockones[:], 0.0, writes=rd(blockones))
        S.op("pool", "memset", blockones[0:64, 0:64], 1.0, writes=rd(blockones))
        S.op("pool", "memset", blockones[64:128, 64:128], 1.0, writes=rd(blockones))
        resetm = sb("resetm", [128, 512])
        S.op("pool", "memset", resetm[:], 1.0, writes=rd(resetm))
        S.op("pool", "memset", resetm[:].rearrange("p (c t) -> p c t", t=128)[:, :, 0:1], 0.0, writes=rd(resetm))

        cact = sb("cact", [128, 8])
        tmp8 = sb("tmp8", [128, 8])
        S.op("act", "activation", tmp8[:], pv[:, 0:8], AF.Exp, scale=-1.0, reads=rd(pv), writes=rd(tmp8))
        S.op("dve", "tensor_scalar", tmp8[:], tmp8[:], 1.0, None, ALU.add, reads=rd(tmp8), writes=rd(tmp8))
        S.op("dve", "reciprocal", tmp8[:], tmp8[:], reads=rd(tmp8), writes=rd(tmp8))
        S.op("dve", "tensor_tensor", cact[:], tmp8[:], pv[:, 0:8], ALU.mult, reads=rd(tmp8, pv), writes=rd(cact))

        stage = [sb("stage0", [128, 1152])] * 2
        modp = bank[0]
        for kc in range(8):
            for hf in range(2):
                stg = stage[hf]
                S.dma("sp", stg[:, 0:1024], adaw_d[kc * 128:(kc + 1) * 128, hf * 1024:(hf + 1) * 1024], writes=rd(stg))
                for jj in range(8):
                    j = hf * 8 + jj
                    S.op("pe", "matmul",
                        modp[:, kc * 16 + j: kc * 16 + j + 1], stg[:, jj * 128:(jj + 1) * 128], cact[:, kc:kc + 1],
                        start=True, stop=True, reads=rd(stg, cact), writes=rd(modp))
        modT = sb("modT", [128, 16])
        S.op("dve", "tensor_reduce", modT[:], modp[:, 0:128].rearrange("p (k j) -> p j k", j=16),
                                              axis=AX.X, op=ALU.add, reads=rd(modp), writes=rd(modT))
        Bco = sb("Bco", [128, 8])
        Aco = sb("Aco", [128, 8])
        S.op("dve", "tensor_tensor", Bco[:], modT[:, 0:8], pv[:, 16:24], ALU.add, reads=rd(modT, pv), writes=rd(Bco))
        S.op("dve", "tensor_tensor", Aco[:], modT[:, 8:16], pv[:, 24:32], ALU.add, reads=rd(modT, pv), writes=rd(Aco))
        S.op("dve", "scalar_tensor_tensor", Aco[:], Aco[:], 1.0, pv[:, 8:16], ALU.add, ALU.mult,
             reads=rd(Aco, pv), writes=rd(Aco))
        der = sb("der", [128, 4])
        S.op("dve", "tensor_scalar", der[:, 0:2], pv[:, 36:38], -1.0, None, ALU.mult, reads=rd(pv), writes=rd(der))
        S.op("dve", "tensor_scalar", der[:, 2:3], pv[:, 39:40], -1.0, 1.0, ALU.mult, ALU.add, reads=rd(pv), writes=rd(der))
        MU = lambda g: pv[:, 32 + g:33 + g]
        NEGW0, NEGA0, OMKA = der[:, 0:1], der[:, 1:2], der[:, 2:3]
        KK, KA, RK = pv[:, 38:39], pv[:, 39:40], pv[:, 40:41]

        W = sb("W", [128, 8, 1152], BF16)
        for kc in range(8):
            stg = stage[kc % 2]
            S.dma("sp", stg[:, 0:1152], wc_d[kc * 128:(kc + 1) * 128, :], writes=rd(stg))
            eng = "dve" if kc % 2 == 0 else "act"
            if eng == "dve":
                S.op("dve", "tensor_copy", W[:, kc, :], stg[:, 0:1152], reads=rd(stg), writes=rd(W))
            else:
                S.op("act", "activation", W[:, kc, :], stg[:, 0:1152], AF.Copy, reads=rd(stg), writes=rd(W))

        kT_all = sb("kT_all", [128, S_len], BF16)
        v_all = sb("v_all", [128, NKB, 128], BF16)
        raw = [sb("raw%d" % g, [128, 513]) for g in range(4)]
        for g in range(4):
            S.op("pool", "memset", raw[g][:, 0:1], 0.0, writes=rd(raw[g]))
        Tst = sb("Tst", [128, 64])
        Tb = sb("Tb", [128, 64], BF16)
        S.op("pool", "memset", Tst[:], 0.0, writes=rd(Tst))
        S.op("pool", "memset", Tb[:], 0.0, writes=rd(Tb))

        xt = [sb("xt%d" % i, [128, 1024]) for i in range(2)]
        junk = sb("junk", [128, 1024], BF16)
        ss = sb("ss", [128, 4])
        rstd = sb("rstd", [128, 4])
        xn = [sb("xn%d" % i, [128, 1024], BF16) for i in range(2)]
        tmpf = sb("tmpf", [128, 8, 128])
        hT = [sb("hT%d" % i, [128, 8, 512], BF16) for i in range(1)]
        sh = [sb("sh%d" % g, [128, 512]) for g in range(4)]
        dif = sb("dif", [128, 512])
        gate = sb("gate", [128, 4, 256])
        gtmp = sb("gtmp", [128, 256])
        th = dif
        e1 = sb("e1", [128, 512])
        ew = sb("ew", [128, 512])
        alpha = sb("alpha", [128, 512])
        kks = sb("kks", [128, 512])
        sq = e1
        rn = dif
        kkn = sb("kkn", [128, 512])
        bv = sb("bv", [128, 512])
        kmod = sb("kmod", [128, 512])
        cwn = sb("cwn", [128, 512])
        gam = sb("gam", [128, 512])
        ginv = sb("ginv", [128, 512])
        gprev = e1
        gco = sb("gco", [128, 512])
        ncC = sb("ncC", [128, 4])
        RtTh = [sb("RtTh%d" % h, [128, 512], BF16) for h in range(2)]
        AtTh = [sb("AtTh%d" % h, [128, 512], BF16) for h in range(2)]
        rkTh = [sb("rkTh%d" % h, [128, 512], BF16) for h in range(2)]
        qTh2 = [[sb("qTh%d_%d" % (i, h), [128, 512], BF16) for h in range(2)] for i in range(2)]
        kTbuf = [Buf("kT%d" % i) for i in range(NB)]
        vbuf = [Buf("v%d" % i) for i in range(NB)]
        for _t in RtTh + AtTh + rkTh + qTh2[0] + qTh2[1]:
            S.op("pool", "memset", _t[:], 0.0, writes=rd(_t))
        KtT = sb("KtT", [128, 512], BF16)
        BtT = sb("BtT", [128, 512], BF16)
        KcT = sb("KcT", [128, 512], BF16)
        BcT = sb("BcT", [128, 512], BF16)
        VT = sb("VT", [128, 512], BF16)
        tokm = [sb("tokm%d" % c, [128, 3, 128], BF16) for c in range(4)]
        Lab = [[sb("Lab%d_%d" % (c, i), [128, 4, 128], BF16) for i in range(2)] for c in range(4)]
        Mx = [sb("Mx%d" % c, [128, 6, 128], BF16) for c in range(4)]
        MT = [[sb("MT%d_%d" % (c, i), [128, 2, 128], BF16) for i in range(2)] for c in range(4)]
        Xb = sb("Xb", [128, 128], BF16)
        Ub = sb("Ub", [128, 128], BF16)
        yr = sb("yr", [128, 4, 128])
        yb = sb("yb", [128, 4, 128])
        st1 = sb("st1", [128, 8])
        st2 = sb("st2", [128, 8])
        st3 = sb("st3", [128, 8])
        ystage = [sb("ystage%d" % i, [128, 4, 128]) for i in range(1)]
        NSB = 2
        e_t = [sb("e_t%d" % i, [128, 512]) for i in range(3)]
        sp_t = [sb("sp_t%d" % i, [128, 512], BF16) for i in range(3)]
        E_t = [sb("E_t%d" % i, [128, 512], BF16) for i in range(2)]
        sc_t = [sb("sc_t%d" % i, [128, 4]) for i in range(NSB)]
        oacc = [sb("oacc%d" % h, [128, 4, 64]) for h in range(2)]
        osq = sb("osq", [128, 4, 64])
        ost = sb("ost", [128, 4])

        B_T = 0
        B_IP = [1, 2]
        B_SQ = [1, 2]
        B_MM = 0
        ZC = [5, 6, 7]
        PP = [3, 4]

        ipn = [0]

        def ipbank():
            b = B_IP[ipn[0] % 2]
            ipn[0] += 1
            return bank[b]

        def stageA(tb):
            hTb = hT[0]
            yst = ystage[0]
            qTh = qTh2[tb % 2]
            xts = [xt[0], xt[1], xt[0], xt[1]]

            def xload(ts):
                S.dma("sp", xts[ts][:], x_d[tb * 512 + ts * 128: tb * 512 + (ts + 1) * 128, :], writes=rd(xts[ts]))
            xload(0)
            xload(1)
            for ts in range(4):
                S.op("act", "activation", junk[:], xts[ts][:], AF.Square, scale=1.0 / 32.0,
                                                          accum_out=ss[:, ts:ts + 1], reads=rd(xts[ts]), writes=rd(junk, ss))
                S.op("act", "activation", rstd[:, ts:ts + 1], ss[:, ts:ts + 1], AF.Ln, bias=NORM_EPS, reads=rd(ss), writes=rd(rstd))
                S.op("act", "activation", rstd[:, ts:ts + 1], rstd[:, ts:ts + 1], AF.Exp, scale=-0.5, reads=rd(rstd), writes=rd(rstd))
                xnb = xn[ts % 2]
                S.op("dve", "tensor_scalar", xnb[:], xts[ts][:], rstd[:, ts:ts + 1], None, ALU.mult,
                     reads=rd(xts[ts], rstd), writes=rd(xnb))
                pT = bank[B_T]
                pTv = bview(B_T, "p (k t) -> p k t", dt=BF16, t=128)
                for kc in range(8):
                    S.op("pe", "transpose", pTv[:, kc, :], xnb[:, kc * 128:(kc + 1) * 128], ident[:],
                         reads=rd(xnb, ident), writes=rd(pT))
                S.op("dve", "tensor_tensor", tmpf[:], pTv[:, 0:8, :], Aco[:].unsqueeze(2).to_broadcast([128, 8, 128]), ALU.mult,
                     reads=rd(pT, Aco), writes=rd(tmpf))
                S.op("pool", "tensor_tensor", hTb[:, :, ts * 128:(ts + 1) * 128], tmpf[:],
                                                               Bco[:].unsqueeze(2).to_broadcast([128, 8, 128]), ALU.add,
                     reads=rd(tmpf, Bco), writes=rd(hTb))
                if ts + 2 < 4:
                    xload(ts + 2)
                yield
            for g in range(6):
                pb = ipbank()
                for kc in range(8):
                    S.op("pe", "matmul", pb[:], W[:, kc, g * 128:(g + 1) * 128], hTb[:, kc, :],
                                                                     start=(kc == 0), stop=(kc == 7),
                         reads=rd(W, hTb), writes=rd(pb))
                if g < 4:
                    S.op("act", "activation", raw[g][:, 1:513], pb[:], AF.Copy, reads=rd(pb), writes=rd(raw[g]))
                    S.op("dve", "tensor_tensor", dif[:], raw[g][:, 0:512], raw[g][:, 1:513], ALU.subtract,
                         reads=rd(raw[g]), writes=rd(dif))
                    S.op("dve", "scalar_tensor_tensor", sh[g][:], dif[:], MU(g), raw[g][:, 1:513], ALU.mult, ALU.add,
                         reads=rd(dif, raw[g], pv), writes=rd(sh[g]))
                    S.op("pool", "tensor_copy", raw[g][:, 0:1], raw[g][:, 512:513], reads=rd(raw[g]), writes=rd(raw[g]))
                elif g == 4:
                    for h in range(2):
                        hs = slice(64 * h, 64 * h + 64)
                        S.op("act", "activation", qTh[h][hs, :], pb[hs, :], AF.Copy, reads=rd(pb), writes=rd(qTh[h]))
                else:
                    S.op("act", "activation", kT_all[:, tb * 512:(tb + 1) * 512], pb[:], AF.Copy,
                         reads=rd(pb), writes=[kTbuf[tb]])
                yield
            for ts in range(4):
                pb = ipbank()
                for kc in range(8):
                    S.op("pe", "matmul", pb[:, 0:384], hTb[:, kc, ts * 128:(ts + 1) * 128],
                                                                       W[:, kc, 768:1152], start=(kc == 0), stop=(kc == 7),
                         reads=rd(W, hTb), writes=rd(pb))
                S.op("act", "activation", v_all[:, tb * 4 + ts, :], pb[:, 0:128], AF.Copy,
                     reads=rd(pb), writes=[vbuf[tb]])
                S.op("act", "activation", gtmp[:], pb[:, 128:384], AF.Exp, scale=-1.0, reads=rd(pb), writes=rd(gtmp))
                S.op("dve", "tensor_scalar", gtmp[:], gtmp[:], 1.0, None, ALU.add, reads=rd(gtmp), writes=rd(gtmp))
                S.op("dve", "reciprocal", gtmp[:], gtmp[:], reads=rd(gtmp), writes=rd(gtmp))
                S.op("dve", "tensor_tensor", gate[:, ts, :], gtmp[:], pb[:, 128:384], ALU.mult,
                     reads=rd(gtmp, pb), writes=rd(gate))
                yield

            shr, shk, shv, shz = sh
            S.op("act", "activation", th[0:64, :], shz[0:64, :], AF.Exp, scale=2.0, reads=rd(shz), writes=rd(th))
            S.op("dve", "tensor_scalar", th[0:64, :], th[0:64, :], 1.0, None, ALU.add, reads=rd(th), writes=rd(th))
            S.op("dve", "reciprocal", th[0:64, :], th[0:64, :], reads=rd(th), writes=rd(th))
            S.op("dve", "tensor_scalar", th[0:64, :], th[0:64, :], -2.0, 1.0, ALU.mult, ALU.add, reads=rd(th), writes=rd(th))
            pdl = ipbank()
            S.op("pe", "matmul", pdl[:], w2a2[0:64, :], th[0:64, :], start=True, stop=True, reads=rd(w2a2, th), writes=rd(pdl))
            S.op("act", "activation", e1[:], pdl[:], AF.Exp, scale=-1.0, bias=NEGW0, reads=rd(pdl, der), writes=rd(e1))
            S.op("act", "activation", e1[:], e1[:], AF.Ln, bias=1.0, reads=rd(e1), writes=rd(e1))
            S.op("act", "activation", ew[:], e1[:], AF.Exp, scale=-1.0, bias=-0.5, reads=rd(e1), writes=rd(ew))
            yield
            pda = ipbank()
            S.op("pe", "matmul", pda[:], w2a2[64:128, :], shz[64:128, :], start=True, stop=True, reads=rd(w2a2, shz), writes=rd(pda))
            S.op("act", "activation", alpha[:], pda[:], AF.Exp, scale=-1.0, bias=NEGA0, reads=rd(pda, der), writes=rd(alpha))
            S.op("dve", "tensor_scalar", alpha[:], alpha[:], 1.0, None, ALU.add, reads=rd(alpha), writes=rd(alpha))
            S.op("dve", "reciprocal", alpha[:], alpha[:], reads=rd(alpha), writes=rd(alpha))
            yield
            S.op("dve", "tensor_scalar", kks[:], shk[:], KK, None, ALU.mult, reads=rd(shk, pv), writes=rd(kks))
            S.op("pool", "tensor_tensor", sq[:], kks[:], kks[:], ALU.mult, reads=rd(kks), writes=rd(sq))
            pss = ipbank()
            S.op("pe", "matmul", pss[:], blockones[:], sq[:], start=True, stop=True, reads=rd(blockones, sq), writes=rd(pss))
            yield
            S.op("dve", "tensor_scalar", rn[:], pss[:], 1e-24, None, ALU.max, reads=rd(pss), writes=rd(rn))
            S.op("act", "activation", rn[:], rn[:], AF.Ln, reads=rd(rn), writes=rd(rn))
            S.op("act", "activation", rn[:], rn[:], AF.Exp, scale=-0.5, reads=rd(rn), writes=rd(rn))
            S.op("dve", "tensor_tensor", kkn[:], kks[:], rn[:], ALU.mult, reads=rd(kks, rn), writes=rd(kkn))
            S.op("pool", "tensor_tensor", bv[:], kkn[:], alpha[:], ALU.mult, reads=rd(kkn, alpha), writes=rd(bv))
            S.op("dve", "tensor_scalar", kmod[:], alpha[:], KA, OMKA, ALU.mult, ALU.add, reads=rd(alpha, pv, der), writes=rd(kmod))
            S.op("dve", "tensor_tensor", kmod[:], kmod[:], shk[:], ALU.mult, reads=rd(kmod, shk), writes=rd(kmod))
            yield
            S.op("dve", "tensor_tensor_scan", cwn[:], resetm[:], ew[:], 0.0, ALU.mult, ALU.add, reads=rd(resetm, ew), writes=rd(cwn))
            S.op("act", "activation", gam[:], cwn[:], AF.Exp, scale=-1.0, reads=rd(cwn), writes=rd(gam))
            S.op("act", "activation", ginv[:], cwn[:], AF.Exp, reads=rd(cwn), writes=rd(ginv))
            S.op("pool", "tensor_tensor", gprev[:], cwn[:], ew[:], ALU.subtract, reads=rd(cwn, ew), writes=rd(gprev))
            S.op("act", "activation", gprev[:], gprev[:], AF.Exp, scale=-1.0, reads=rd(gprev), writes=rd(gprev))
            yield
            cwn3 = cwn[:].rearrange("p (c t) -> p c t", t=128)
            S.op("dve", "tensor_scalar", ncC[:], cwn3[:, :, 127], -1.0, None, ALU.mult, reads=rd(cwn), writes=rd(ncC))
            for c in range(4):
                S.op("act", "activation", gco[:, c * 128:(c + 1) * 128], cwn[:, c * 128:(c + 1) * 128], AF.Exp,
                                                        bias=ncC[:, c:c + 1], reads=rd(cwn, ncC), writes=rd(gco))
            for h in range(2):
                hs = slice(64 * h, 64 * h + 64)
                S.op("dve", "tensor_tensor", RtTh[h][hs, :], shr[hs, :], gam[hs, :], ALU.mult, reads=rd(shr, gam), writes=rd(RtTh[h]))
                S.op("dve", "scalar_tensor_tensor", AtTh[h][hs, :], kkn[hs, :], -1.0, gprev[hs, :], ALU.mult, ALU.mult, reads=rd(kkn, gprev), writes=rd(AtTh[h]))
            S.op("dve", "tensor_tensor", KtT[:], kmod[:], ginv[:], ALU.mult, reads=rd(kmod, ginv), writes=rd(KtT))
            S.op("pool", "tensor_tensor", BtT[:], bv[:], ginv[:], ALU.mult, reads=rd(bv, ginv), writes=rd(BtT))
            S.op("dve", "tensor_tensor", KcT[:], kmod[:], gco[:], ALU.mult, reads=rd(kmod, gco), writes=rd(KcT))
            S.op("pool", "tensor_tensor", BcT[:], bv[:], gco[:], ALU.mult, reads=rd(bv, gco), writes=rd(BcT))
            S.op("act", "activation", VT[:], shv[:], AF.Copy, reads=rd(shv), writes=rd(VT))
            yield
            for h in range(2):
                hs = slice(64 * h, 64 * h + 64)
                S.op("dve", "scalar_tensor_tensor", rkTh[h][hs, :], shr[hs, :], pv[hs, 40:41], kmod[hs, :], ALU.mult, ALU.mult, reads=rd(shr, kmod, pv), writes=rd(rkTh[h]))
            for c in range(4):
                cs = slice(c * 128, (c + 1) * 128)
                pT = bank[B_T]
                pTv = bview(B_T, "p (k t) -> p k t", dt=BF16, t=128)
                for i, src in enumerate([KcT, BcT, VT]):
                    S.op("pe", "transpose", pTv[:, i, :], src[:, cs], ident[:], reads=rd(src, ident), writes=rd(pT))
                S.op("act", "activation", tokm[c][:], pTv[:, 0:3, :], AF.Copy, reads=rd(pT), writes=rd(tokm[c]))
                yield
            pbn = ipbank()
            for c in range(4):
                cs = slice(c * 128, (c + 1) * 128)
                for h in range(2):
                    hs = slice(64 * h, 64 * h + 64)
                    S.op("pe", "matmul", pbn[:, c * 128 + 64 * h: c * 128 + 64 * h + 64], rkTh[h][:, cs], ones_bf[:, :],
                                                                           start=True, stop=True, reads=rd(rkTh[h], ones_bf), writes=rd(pbn))
            for c in range(4):
                S.op("dve", "tensor_tensor", yb[:, c, :], pbn[:, c * 128:(c + 1) * 128], tokm[c][:, 2, :], ALU.mult,
                     reads=rd(pbn, tokm[c]), writes=rd(yb))
                yield

            for c in range(4):
                cs = slice(c * 128, (c + 1) * 128)
                pa = bank[B_SQ[c % 2]]
                pav = bview(B_SQ[c % 2], "p (k t) -> p k t", t=128)
                for h in range(2):
                    hs = slice(64 * h, 64 * h + 64)
                    S.op("pe", "matmul", pav[:, 2 * h, :], AtTh[h][:, cs], BtT[:, cs], start=True, stop=True,
                         reads=rd(AtTh[h], BtT), writes=rd(pa))
                    S.op("pe", "matmul", pav[:, 2 * h + 1, :], BtT[:, cs], AtTh[h][:, cs], start=True, stop=True,
                         reads=rd(AtTh[h], BtT), writes=rd(pa))
                L0 = Lab[c][0]
                S.op("dve", "tensor_tensor", L0[:], pav[:, :, :], maskA[:], ALU.mult, reads=rd(pa, maskA), writes=rd(L0))
                yield
                pb_ = ipbank()
                pbv = pb_[:].rearrange("p (k t) -> p k t", t=128)
                combos = [(KtT, AtTh[0]), (KtT, AtTh[1]), (BtT, RtTh[0]), (BtT, RtTh[1])]
                for i in range(4):
                    h = i % 2
                    hs = slice(64 * h, 64 * h + 64)
                    l, r = combos[i]
                    S.op("pe", "matmul", pbv[:, i, :], l[:, cs], r[:, cs], start=True, stop=True,
                         reads=rd(l, r), writes=rd(pb_))
                S.op("dve", "tensor_tensor", Mx[c][:, 0:4, :], pbv[:, :, :], maskB[:, 0:4, :], ALU.mult,
                     reads=rd(pb_, maskB), writes=rd(Mx[c]))
                pc_ = ipbank()
                pcv = pc_[:].rearrange("p (k t) -> p k t", t=128)
                for h in range(2):
                    hs = slice(64 * h, 64 * h + 64)
                    S.op("pe", "matmul", pcv[:, h, :], KtT[:, cs], RtTh[h][:, cs], start=True, stop=True,
                         reads=rd(KtT, RtTh[h]), writes=rd(pc_))
                S.op("dve", "tensor_tensor", Mx[c][:, 4:6, :], pcv[:, 0:2, :], maskB[:, 4:6, :], ALU.mult,
                     reads=rd(pc_, maskB), writes=rd(Mx[c]))
                yield
                M0 = MT[c][0]
                for h in range(2):
                    S.op("pool", "tensor_tensor", M0[:, h, :], L0[:, 2 * h + 1, :], ident[:], ALU.add,
                         reads=rd(L0, ident), writes=rd(M0))
            for k in range(1, 7):
                for c in range(4):
                    Lp = Lab[c][(k - 1) % 2]
                    Ln_ = Lab[c][k % 2]
                    Mp = MT[c][(k - 1) % 2]
                    Mn = MT[c][k % 2]
                    pa = bank[B_SQ[c % 2]]
                    pav = bview(B_SQ[c % 2], "p (k t) -> p k t", t=128)
                    for h in range(2):
                        S.op("pe", "matmul", pav[:, 2 * h, :], Lp[:, 2 * h + 1, :], Lp[:, 2 * h, :], start=True, stop=True,
                             reads=rd(Lp), writes=rd(pa))
                        S.op("pe", "matmul", pav[:, 2 * h + 1, :], Lp[:, 2 * h, :], Lp[:, 2 * h + 1, :], start=True, stop=True,
                             reads=rd(Lp), writes=rd(pa))
                    S.op("act", "activation", Ln_[:], pav[:, :, :], AF.Copy, reads=rd(pa), writes=rd(Ln_))
                    yield
                    pm = bank[B_MM]
                    pmv = bview(B_MM, "p (k t) -> p k t", t=128)
                    off = 2 * (c % 2)
                    for h in range(2):
                        S.op("pe", "matmul", pmv[:, off + h, :], Ln_[:, 2 * h, :], Mp[:, h, :], start=True, stop=True,
                             reads=rd(Ln_, Mp), writes=rd(pm))
                    S.op("dve", "tensor_tensor", Mn[:], pmv[:, off:off + 2, :], Mp[:], ALU.add,
                         reads=rd(pm, Mp), writes=rd(Mn))
                yield
            MTf = [MT[c][0] for c in range(4)]

            gam3 = gam[:].rearrange("p (c t) -> p c t", t=128)
            for c in range(4):
                cs = slice(c * 128, (c + 1) * 128)
                pm = bank[B_MM]
                Vt = tokm[c]
                for h in range(2):
                    hs = slice(64 * h, 64 * h + 64)
                    S.op("pe", "matmul", pm[:, hs], AtTh[h][:, cs], Tb[:, :], start=True, stop=False,
                         reads=rd(AtTh[h], Tb), writes=rd(pm))
                    S.op("pe", "matmul", pm[:, hs], Mx[c][:, h, :], Vt[:, 2, hs], start=False, stop=True,
                         reads=rd(Mx[c], Vt), writes=rd(pm))
                S.op("dve", "tensor_copy", Xb[:], pm[:, 0:128], reads=rd(pm), writes=rd(Xb))
                yield
                for h in range(2):
                    hs = slice(64 * h, 64 * h + 64)
                    S.op("pe", "matmul", pm[:, 128 + 64 * h:128 + 64 * h + 64], MTf[c][:, h, :], Xb[:, hs], start=True, stop=True,
                         reads=rd(MTf[c], Xb), writes=rd(pm))
                S.op("act", "activation", Ub[:], pm[:, 128:256], AF.Copy, reads=rd(pm), writes=rd(Ub))
                yield
                for h in range(2):
                    hs = slice(64 * h, 64 * h + 64)
                    ys = slice(256 + 64 * h, 256 + 64 * h + 64)
                    S.op("pe", "matmul", pm[:, ys], RtTh[h][:, cs], Tb[:, :], start=True, stop=False,
                         reads=rd(RtTh[h], Tb), writes=rd(pm))
                    S.op("pe", "matmul", pm[:, ys], Mx[c][:, 2 + h, :], Ub[:, hs], start=False, stop=False,
                         reads=rd(Mx[c], Ub), writes=rd(pm))
                    S.op("pe", "matmul", pm[:, ys], Mx[c][:, 4 + h, :], Vt[:, 2, hs], start=False, stop=True,
                         reads=rd(Mx[c], Vt), writes=rd(pm))
                S.op("act", "activation", yr[:, c, :], pm[:, 256:384], AF.Copy, reads=rd(pm), writes=rd(yr))
                for h in range(2):
                    hs = slice(64 * h, 64 * h + 64)
                    S.op("pe", "matmul", pm[hs, 384:448], Vt[:, 1, hs], Ub[:, hs], start=True, stop=False,
                         reads=rd(Vt, Ub), writes=rd(pm))
                    S.op("pe", "matmul", pm[hs, 384:448], Vt[:, 0, hs], Vt[:, 2, hs], start=False, stop=True,
                         reads=rd(Vt), writes=rd(pm))
                S.op("dve", "scalar_tensor_tensor", Tst[:], Tst[:], gam3[:, c, 127:128], pm[:, 384:448], ALU.mult, ALU.add,
                     reads=rd(Tst, gam, pm), writes=rd(Tst))
                S.op("act", "activation", Tb[:], Tst[:], AF.Copy, reads=rd(Tst), writes=rd(Tb))
                yield

            yr8 = yr[:].rearrange("p c (h i) -> p (c h) i", i=64)
            ysq = gco
            ysq8 = ysq[:].rearrange("p (c h i) -> p (c h) i", h=2, i=64)
            S.op("dve", "tensor_reduce", st1[:], yr8, axis=AX.X, op=ALU.add, reads=rd(yr), writes=rd(st1))
            S.op("pool", "tensor_tensor", ysq[:], yr[:].rearrange("p c x -> p (c x)"), yr[:].rearrange("p c x -> p (c x)"), ALU.mult, reads=rd(yr), writes=rd(ysq))
            S.op("dve", "tensor_reduce", st2[:], ysq8, axis=AX.X, op=ALU.add, reads=rd(ysq), writes=rd(st2))
            S.op("dve", "tensor_scalar", st1[:], st1[:], 1.0 / 64, None, ALU.mult, reads=rd(st1), writes=rd(st1))
            S.op("dve", "tensor_tensor", st3[:], st1[:], st1[:], ALU.mult, reads=rd(st1), writes=rd(st3))
            S.op("dve", "scalar_tensor_tensor", st2[:], st2[:], 1.0 / 64, st3[:], ALU.mult, ALU.subtract, reads=rd(st2, st3), writes=rd(st2))
            S.op("act", "activation", st2[:], st2[:], AF.Ln, bias=GN_EPS, reads=rd(st2), writes=rd(st2))
            S.op("act", "activation", st2[:], st2[:], AF.Exp, scale=-0.5, reads=rd(st2), writes=rd(st2))
            S.op("dve", "tensor_tensor", yr8, yr8, st1[:].unsqueeze(2).to_broadcast([128, 8, 64]), ALU.subtract, reads=rd(yr, st1), writes=rd(yr))
            S.op("dve", "tensor_tensor", yr8, yr8, st2[:].unsqueeze(2).to_broadcast([128, 8, 64]), ALU.mult, reads=rd(yr, st2), writes=rd(yr))
            for c in range(4):
                S.op("pool", "tensor_tensor", yr[:, c, :], yr[:, c, :], bct[:, 0, :], ALU.mult, reads=rd(yr, bct), writes=rd(yr))
                S.op("pool", "tensor_tensor", yr[:, c, :], yr[:, c, :], bct[:, 1, :], ALU.add, reads=rd(yr, bct), writes=rd(yr))
            S.op("dve", "tensor_tensor", yr[:], yr[:], yb[:], ALU.add, reads=rd(yr, yb), writes=rd(yr))
            S.op("dve", "tensor_tensor", yst[:, :, 0:128], yr[:], gate[:, :, 0:128], ALU.mult, reads=rd(yr, gate), writes=rd(yst))

            S.dma("sp", y_d[tb * 512:(tb + 1) * 512, 0:128].rearrange("(ts p) c -> p ts c", p=128), yst[:], reads=rd(yst))
            S.dma("sp", g_d[tb * 512:(tb + 1) * 512, :].rearrange("(ts p) c -> p ts c", p=128), gate[:, :, 128:256], reads=rd(gate))
            yield

        sbn = [0]

        def stageS(tb):
            qTh = qTh2[tb % 2]
            tiles = [(h, kb) for h in range(2) for kb in range(4 * tb + 4)]
            n = len(tiles)
            g0 = sbn[0]
            sbn[0] += n

            def res(i):
                g = g0 + i
                h, kb = tiles[i]
                dk = max(0, kb - 4 * tb)
                return dict(h=h, kb=kb, dk=dk, q0=dk * 128, qsl=slice(dk * 128, 512), hs=slice(64 * h, 64 * h + 64),
                            zc=bank[ZC[g % 3]], pp=bank[PP[g % 2]], et=e_t[g % 3], sa=sp_t[g % 3], Et=E_t[g % 2], sct=sc_t[g % 2],
                            oa=oacc[h])

            def front(i):
                r = res(i)
                S.op("pe", "matmul", r['zc'][:, r['qsl']], kT_all[:, r['kb'] * 128:(r['kb'] + 1) * 128], qTh[r['h']][:, r['qsl']],
                     start=True, stop=True, reads=[kTbuf[r['kb'] // 4]] + rd(qTh[r['h']]), writes=rd(r['zc']))

            def mid(i):
                r = res(i)
                qsl, et, sa, zc, pp, dk, q0 = r['qsl'], r['et'], r['sa'], r['zc'], r['pp'], r['dk'], r['q0']
                S.op("act", "activation", et[:, qsl], zc[:, qsl], AF.Exp, scale=0.125, reads=rd(zc), writes=rd(et))
                if r['kb'] >= 4 * tb:
                    S.op("pool", "affine_select", et[:, qsl], et[:, qsl], pattern=[[1, 512 - q0]], compare_op=ALU.is_gt, fill=0.0,
                         base=0, channel_multiplier=-1, reads=rd(et), writes=rd(et))
                S.op("act", "activation", sa[:, qsl], et[:, qsl], AF.Ln, bias=1.0, reads=rd(et), writes=rd(sa))
                S.op("pe", "matmul", zc[:, qsl], tri[:], sa[:, qsl], start=True, stop=True, reads=rd(tri, sa), writes=rd(zc))
                for qs in range(dk, 4):
                    S.op("pe", "matmul", pp[:, 256 + qs:257 + qs], sa[:, qs * 128:(qs + 1) * 128], ones_bf[:, 0:1],
                         start=True, stop=True, reads=rd(sa, ones_bf), writes=rd(pp))

            def back(i):
                r = res(i)
                qsl, et, sa, zc, pp, dk, Et, sct, oa, hs, kb, h = (r['qsl'], r['et'], r['sa'], r['zc'], r['pp'], r['dk'], r['Et'],
                                                                   r['sct'], r['oa'], r['hs'], r['kb'], r['h'])
                S.op("act", "activation", Et[:, qsl], zc[:, qsl], AF.Exp, scale=-1.0, reads=rd(zc), writes=rd(Et))
                if kb > 0:
                    S.op("act", "activation", sct[:, dk:4], pp[:, 256 + dk:260], AF.Exp, scale=-1.0, reads=rd(pp), writes=rd(sct))
                S.op("dve", "tensor_tensor", sa[:, qsl], et[:, qsl], Et[:, qsl], ALU.mult, reads=rd(et, Et), writes=rd(sa))
                for qs in range(dk, 4):
                    S.op("pe", "matmul", pp[:, qs * 64:(qs + 1) * 64], sa[:, qs * 128:(qs + 1) * 128], v_all[:, kb, hs],
                         start=True, stop=True, reads=rd(sa) + [vbuf[kb // 4]], writes=rd(pp))
                ppv = pp[:, 0:256].rearrange("p (q d) -> p q d", d=64)
                if kb == 0:
                    S.op("dve", "tensor_copy", oa[:], ppv, reads=rd(pp), writes=rd(oa))
                else:
                    S.op("pool", "tensor_tensor", oa[:, dk:4, :], oa[:, dk:4, :],
                         sct[:, dk:4].unsqueeze(2).to_broadcast([128, 4 - dk, 64]), ALU.mult, reads=rd(oa, sct), writes=rd(oa))
                    S.op("dve", "tensor_tensor", oa[:, dk:4, :], oa[:, dk:4, :], ppv[:, dk:4, :], ALU.add, reads=rd(oa, pp), writes=rd(oa))
                if kb == 4 * tb + 3:
                    S.op("pool", "tensor_tensor", osq[:], oa[:], oa[:], ALU.mult, reads=rd(oa), writes=rd(osq))
                    S.op("dve", "tensor_reduce", ost[:], osq[:], axis=AX.X, op=ALU.add, reads=rd(osq), writes=rd(ost))
                    S.op("act", "activation", ost[:], ost[:], AF.Ln, scale=1.0 / 64, bias=NORM_EPS, reads=rd(ost), writes=rd(ost))
                    S.op("act", "activation", ost[:], ost[:], AF.Exp, scale=-0.5, reads=rd(ost), writes=rd(ost))
                    S.op("dve", "tensor_tensor", oa[:], oa[:], ost[:].unsqueeze(2).to_broadcast([128, 4, 64]), ALU.mult,
                         reads=rd(oa, ost), writes=rd(oa))
                    for qs in range(4):
                        S.op("pool", "tensor_tensor", oa[:, qs, :], oa[:, qs, :], bct[:, 2, hs], ALU.mult, reads=rd(oa, bct), writes=rd(oa))
                    S.dma("sp", y_d.rearrange("(q nb) c -> q nb c", nb=NKB)[:, 4 * tb:4 * tb + 4, 128 + 64 * h:128 + 64 * h + 64],
                          oa[:], reads=rd(oa))

            front(0)
            if n > 1:
                front(1)
            mid(0)
            for i in range(n):
                if i + 2 < n:
                    front(i + 2)
                if i + 1 < n:
                    mid(i + 1)
                back(i)
                yield

        def drive(gens):
            lists = []
            for g in gens:
                lists.append(g)
            alive = list(lists)
            while alive:
                for g in list(alive):
                    try:
                        next(g)
                    except StopIteration:
                        alive.remove(g)

        def count(genf, tb):
            en = S.enabled
            S.enabled = False
            saved = sbn[0]
            c = sum(1 for _ in genf(tb))
            sbn[0] = saved
            S.enabled = en
            return c

        def merged(gS, nS, gA, nA):
            iS = iA = 0
            while iS < nS or iA < nA:
                if iA < nA and (iS >= nS or iA * max(nS, 1) <= iS * nA):
                    next(gA, None)
                    iA += 1
                else:
                    next(gS, None)
                    iS += 1
            for _ in gS:
                pass
            for _ in gA:
                pass

        for _ in stageA(0):
            pass
        for tb in range(NB):
            nS = count(stageS, tb)
            if tb + 1 < NB:
                nA = count(stageA, tb + 1)
                merged(stageS(tb), nS, stageA(tb + 1), nA)
            else:
                for _ in stageS(tb):
                    pass

        S.emit(st)
    return nc


def build_phase_b(ntok, final):
    NT = ntok // 128
    nc = bass.Bass("TRN2", target_bir_lowering=False)
    x_d = nc.dram_tensor("x", [ntok, 1024], F32, kind="ExternalInput").ap()
    y_d = nc.dram_tensor("y", [ntok, 1024], F32, kind="ExternalInput").ap()
    g_d = nc.dram_tensor("gs", [ntok, 512], F32, kind="ExternalInput").ap()
    wo_d = nc.dram_tensor("wo", [1024, 1024], F32, kind="ExternalInput").ap()
    adaw_d = nc.dram_tensor("adaw", [1024, 1024], F32, kind="ExternalInput").ap()
    pv_d = nc.dram_tensor("pv", [128, 8], F32, kind="ExternalInput").ap()
    brow_d = nc.dram_tensor("brow", [1, 1024], F32, kind="ExternalInput").ap()
    fg_d = nc.dram_tensor("fg", [128, 1024], F32, kind="ExternalInput").ap()
    o_d = nc.dram_tensor("xo", [ntok, 1024], F32, kind="ExternalOutput").ap()

    S = Sched(nc)
    st = ExitStack()
    with st:
        def sb(name, shape, dt=F32):
            return TL(st.enter_context(nc.sbuf_tensor("s_" + name, shape, dt)), name)

        def ps(name, shape, dt=F32):
            return TL(st.enter_context(nc.psum_tensor("p_" + name, shape, dt)), name)

        def rd(*ts):
            return [t.b for t in ts]

        bank = [ps("bank%d" % i, [128, 512], F32) for i in range(8)]
        pv = sb("pv", [128, 8])
        brow = sb("brow", [1, 1024])
        fg = sb("fg", [128, 1024])
        S.dma("sp", pv[:], pv_d[:, :], writes=rd(pv))
        S.dma("sp", brow[:], brow_d[:, :], writes=rd(brow))
        S.dma("sp", fg[:], fg_d[:, :], writes=rd(fg))
        identf = sb("identf", [128, 128])
        ident = sb("ident", [128, 128], BF16)
        S.op("pool", "memset", identf[:], 1.0, writes=rd(identf))
        S.op("pool", "affine_select", identf[:], identf[:], pattern=[[-1, 128]], compare_op=ALU.is_equal,
             fill=0.0, base=0, channel_multiplier=1, reads=rd(identf), writes=rd(identf))
        S.op("dve", "tensor_copy", ident[:], identf[:], reads=rd(identf), writes=rd(ident))
        onesf = sb("onesf", [1, 128])
        S.op("pool", "memset", onesf[:], 1.0, writes=rd(onesf))
        cact = sb("cact", [128, 8])
        tmp8 = sb("tmp8", [128, 8])
        S.op("act", "activation", tmp8[:], pv[:, 0:8], AF.Exp, scale=-1.0, reads=rd(pv), writes=rd(tmp8))
        S.op("dve", "tensor_scalar", tmp8[:], tmp8[:], 1.0, None, ALU.add, reads=rd(tmp8), writes=rd(tmp8))
        S.op("dve", "reciprocal", tmp8[:], tmp8[:], reads=rd(tmp8), writes=rd(tmp8))
        S.op("dve", "tensor_tensor", cact[:], tmp8[:], pv[:, 0:8], ALU.mult, reads=rd(tmp8, pv), writes=rd(cact))
        stage = [sb("stage%d" % i, [128, 1024]) for i in range(2)]
        grow = sb("grow", [1, 1024])
        for kc in range(8):
            stg = stage[kc % 2]
            S.dma("sp", stg[:], adaw_d[kc * 128:(kc + 1) * 128, :], writes=rd(stg))
            for hf in range(2):
                S.op("pe", "matmul", bank[hf][0:1, :], cact[:, kc:kc + 1], stg[:, hf * 512:(hf + 1) * 512],
                     start=(kc == 0), stop=(kc == 7), reads=rd(cact, stg), writes=rd(bank[hf]))
        for hf in range(2):
            S.op("dve", "tensor_tensor", grow[:, hf * 512:(hf + 1) * 512], bank[hf][0:1, :], brow[:, hf * 512:(hf + 1) * 512], ALU.add,
                 reads=rd(bank[hf], brow), writes=rd(grow))
        gbc = sb("gbc", [128, 1024])
        for hf in range(2):
            S.op("pe", "matmul", bank[2 + hf][:], onesf[0:1, :], grow[0:1, hf * 512:(hf + 1) * 512], start=True, stop=True,
                 reads=rd(onesf, grow), writes=rd(bank[2 + hf]))
            S.op("act", "activation", gbc[:, hf * 512:(hf + 1) * 512], bank[2 + hf][:], AF.Copy, reads=rd(bank[2 + hf]), writes=rd(gbc))
        Wo = sb("Wo", [128, 8, 1024], BF16)
        for kc in range(8):
            stg = stage[kc % 2]
            S.dma("sp", stg[:], wo_d[kc * 128:(kc + 1) * 128, :], writes=rd(stg))
            S.op("dve", "tensor_tensor", Wo[:, kc, :], stg[:], gbc[:], ALU.mult, reads=rd(stg, gbc), writes=rd(Wo))

        xt = [sb("xt%d" % i, [128, 1024]) for i in range(2)]
        yt = [sb("yt%d" % i, [128, 1024]) for i in range(2)]
        gt = [sb("gt%d" % i, [128, 512]) for i in range(2)]
        yb = [sb("yb%d" % i, [128, 1024], BF16) for i in range(2)]
        yT = [sb("yT%d" % i, [128, 8, 128], BF16) for i in range(2)]
        xo = [sb("xo%d" % i, [128, 1024]) for i in range(2)]
        junk = sb("junk", [128, 1024], BF16)
        ss = sb("ss", [128, 1])
        for t in range(NT):
            i = t % 2
            rows = slice(t * 128, (t + 1) * 128)
            S.dma("sp", xt[i][:], x_d[rows, :], writes=rd(xt[i]))
            S.dma("sp", yt[i][:], y_d[rows, :], writes=rd(yt[i]))
            S.dma("sp", gt[i][:], g_d[rows, :], writes=rd(gt[i]))
            S.op("act", "activation", yb[i][:, 0:512], yt[i][:, 0:512], AF.Copy, reads=rd(yt[i]), writes=rd(yb[i]))
            S.op("dve", "tensor_tensor", yb[i][:, 512:1024], yt[i][:, 512:1024], gt[i][:], ALU.mult, reads=rd(yt[i], gt[i]), writes=rd(yb[i]))
            pT = bank[4 + i]
            pTv = pT[:].bitcast(BF16).rearrange("p (k t) -> p k t", t=128)
            for kc in range(8):
                S.op("pe", "transpose", pTv[:, kc, :], yb[i][:, kc * 128:(kc + 1) * 128], ident[:], reads=rd(yb[i], ident), writes=rd(pT))
            S.op("act", "activation", yT[i][:], pTv[:, 0:8, :], AF.Copy, reads=rd(pT), writes=rd(yT[i]))
            for hf in range(2):
                pb = bank[hf * 2 + i]
                for kc in range(8):
                    S.op("pe", "matmul", pb[:], yT[i][:, kc, :], Wo[:, kc, hf * 512:(hf + 1) * 512], start=(kc == 0), stop=(kc == 7),
                         reads=rd(yT[i], Wo), writes=rd(pb))
                S.op("dve", "tensor_tensor", xo[i][:, hf * 512:(hf + 1) * 512], pb[:], xt[i][:, hf * 512:(hf + 1) * 512], ALU.add,
                     reads=rd(pb, xt[i]), writes=rd(xo[i]))
            if final:
                S.op("act", "activation", junk[:], xo[i][:], AF.Square, scale=1.0 / 32.0, accum_out=ss[:], reads=rd(xo[i]), writes=rd(junk, ss))
                S.op("act", "activation", ss[:], ss[:], AF.Ln, bias=NORM_EPS, reads=rd(ss), writes=rd(ss))
                S.op("act", "activation", ss[:], ss[:], AF.Exp, scale=-0.5, reads=rd(ss), writes=rd(ss))
                S.op("dve", "scalar_tensor_tensor", xo[i][:], xo[i][:], ss[:, 0:1], fg[:], ALU.mult, ALU.mult, reads=rd(xo[i], ss, fg), writes=rd(xo[i]))
            S.dma("sp", o_d[rows, :], xo[i][:], reads=rd(xo[i]))
        S.emit(st)
    return nc


D=1024; RW=512; SHIFT=1664; RWKV_COLS=2176

def cols_for(hp):
    r = np.arange(128)+128*hp
    return dict(r=r, k=512+r, v=1024+r, zz=np.arange(1536,1664), g=1664+r,
                q=RWKV_COLS+r, ks=RWKV_COLS+512+r, vs=RWKV_COLS+1024+r, gs=RWKV_COLS+1536+r)

def prep_a(inp, l, xcur, S_len):
    maps = []
    for core in range(8):
        b, hp = core // 4, core % 4
        cs = cols_for(hp)
        order = [cs['r'], cs['k'], cs['v'], cs['zz'], cs['q'], cs['ks'], cs['vs'], cs['g'], cs['gs']]
        wc = np.ascontiguousarray(inp['w_in'][l][:, np.concatenate(order)])
        adaw = np.ascontiguousarray(inp['ada_w'][l][:, :2048])
        pv = np.zeros((128, 48), np.float32)
        t8 = lambda v: np.ascontiguousarray(v.reshape(8, 128).T)
        pv[:, 0:8] = t8(inp['c'][b])
        pv[:, 8:16] = t8(inp['norm_g'][l])
        pv[:, 16:24] = t8(inp['ada_b'][l][0:1024])
        pv[:, 24:32] = t8(inp['ada_b'][l][1024:2048])
        mu = inp['tshift_mu'][l]
        pv[:, 32] = mu[cs['r']]; pv[:, 33] = mu[cs['k']]; pv[:, 34] = mu[cs['v']]; pv[:, 35] = mu[cs['zz']]
        hc = cs['r']
        pv[:, 36] = inp['decay_w0'][l][hc]; pv[:, 37] = inp['iclr_a0'][l][hc]
        pv[:, 38] = inp['k_k'][l][hc]; pv[:, 39] = inp['k_a'][l][hc]
        pv[:, 40] = inp['r_k'][l].reshape(512)[hc]
        w2a2 = np.concatenate([inp['decay_w2'][l][:, hc], inp['iclr_a2'][l][:, hc]], axis=0).astype(np.float32)
        bc = np.stack([np.broadcast_to(inp['rwkv_ln_w'][l][hc], (128, 128)),
                       np.broadcast_to(inp['rwkv_ln_b'][l][hc], (128, 128)),
                       np.broadcast_to(inp['sb_norm_g'][l][hc], (128, 128))], axis=1).astype(np.float32)
        maps.append(dict(x=np.ascontiguousarray(xcur[b, :S_len]), wc=wc, adaw=adaw, pv=pv,
                         w2a2=np.ascontiguousarray(w2a2), bc=np.ascontiguousarray(bc)))
    return maps

def assemble_y(results, S_len):
    y = np.zeros((2, S_len, 1024), np.float32)
    g = np.zeros((2, S_len, 512), np.float32)
    for core in range(8):
        b, hp = core // 4, core % 4
        yc = results[core]['y']
        y[b, :, 128*hp:128*hp+128] = yc[:, 0:128]
        y[b, :, 512+128*hp:512+128*hp+128] = yc[:, 128:256]
        g[b, :, 128*hp:128*hp+128] = results[core]['gs']
    return y, g


_CACHE = {}


def _get(kind, *args):
    key = (kind,) + args
    if key not in _CACHE:
        _CACHE[key] = build_phase_a(*args) if kind == "a" else build_phase_b(*args)
    return _CACHE[key]


def prep_b(inp, l, xcur, y, g):
    maps = []
    t8 = lambda v: np.ascontiguousarray(v.reshape(8, 128).T)
    xf = xcur.reshape(16384, 1024)
    yf = y.reshape(16384, 1024)
    gf = g.reshape(16384, 512)
    for core in range(8):
        b = core // 4
        rows = slice(core * 2048, (core + 1) * 2048)
        maps.append(dict(
            x=np.ascontiguousarray(xf[rows]), y=np.ascontiguousarray(yf[rows]), gs=np.ascontiguousarray(gf[rows]),
            wo=np.ascontiguousarray(inp['w_out'][l]), adaw=np.ascontiguousarray(inp['ada_w'][l][:, 2048:3072]),
            pv=t8(inp['c'][b]), brow=np.ascontiguousarray(inp['ada_b'][l][2048:3072].reshape(1, 1024)),
            fg=np.ascontiguousarray(np.broadcast_to(inp['final_g'], (128, 1024)))))
    return maps


def kernel(**inputs):
    inp = {k: np.asarray(v, dtype=np.float32) for k, v in inputs.items()}
    S_len = inp['x'].shape[1]
    depth = inp['w_in'].shape[0]
    xcur = inp['x']
    for l in range(depth):
        nca = _get("a", S_len)
        res = run_bass_kernel_spmd(nca, prep_a(inp, l, xcur, S_len), core_ids=list(range(8)))
        y, g = assemble_y(res.results, S_len)
        ncb = _get("b", 2048, l == depth - 1)
        res = run_bass_kernel_spmd(ncb, prep_b(inp, l, xcur, y, g), core_ids=list(range(8)))
        xcur = np.concatenate([r['xo'] for r in res.results], axis=0).reshape(2, S_len, 1024)
    return np.ascontiguousarray(xcur.astype(np.float32))
```

```python
from contextlib import ExitStack
import heapq
from concourse.bass_utils import run_bass_kernel_spmd
import numpy as np
import concourse.bass as bass
import concourse.mybir as mybir

F32 = mybir.dt.float32
BF16 = mybir.dt.bfloat16
ALU = mybir.AluOpType
AF = mybir.ActivationFunctionType
AX = mybir.AxisListType


class Buf:
    __slots__ = ("name", "w", "r")

    def __init__(self, name):
        self.name = name
        self.w = None
        self.r = []


def _fsize(ap):
    n = 1
    for s in tuple(ap.shape)[1:]:
        n *= int(s)
    return n


class Sched:
    CENG = ("pe", "act", "dve", "pool")
    ENGS = CENG + ("sp",)
    NDMA = 24
    LAT = 0.12
    PRIO = "cp"

    def __init__(self, nc, reorder=True):
        self.nc = nc
        self.nodes = []
        self.enabled = True
        self.reorder = reorder

    def _record(self, eng, kind, meth, args, kw, reads, writes, dur):
        deps = set()
        for b in reads:
            if b.w is not None:
                deps.add(b.w)
        for b in writes:
            if b.w is not None:
                deps.add(b.w)
            deps.update(b.r)
        nid = len(self.nodes)
        self.nodes.append(dict(eng=eng, kind=kind, meth=meth, args=args, kw=kw, deps=deps, dur=dur))
        for b in reads:
            b.r.append(nid)
        for b in writes:
            b.w = nid
            b.r = []
        return nid

    def _est(self, eng, meth, args):
        if eng == "pe":
            if meth == "transpose":
                return 0.11
            n = _fsize(args[2])
            c = max(64, n) / 1400.0
            if args[1].dtype == F32:
                c *= 4
            return c + 0.03
        n = _fsize(args[0])
        if eng == "act":
            return 0.17 + n / 1400.0
        if eng == "dve":
            return 0.07 + n / 960.0
        return 0.12 + n / 640.0

    def op(self, eng, meth, *args, reads=(), writes=(), **kw):
        if not self.enabled:
            return None
        return self._record(eng, "op", meth, args, kw, reads, writes, self._est(eng, meth, args))

    def dma(self, q, out, in_, reads=(), writes=()):
        if not self.enabled:
            return None
        nbytes = _fsize(out) * int(tuple(out.shape)[0]) * 4
        return self._record(q, "dma", None, (out, in_), {}, reads, writes, 2.0 + nbytes / 60000.0)

    def _schedule(self):
        nodes = self.nodes
        n = len(nodes)
        order = {e: [] for e in self.ENGS}
        if not self.reorder:
            for i, nd in enumerate(nodes):
                order[nd["eng"]].append(i)
            return order
        succ = [[] for _ in range(n)]
        npend = [0] * n
        for i, nd in enumerate(nodes):
            npend[i] = len(nd["deps"])
            for d in nd["deps"]:
                succ[d].append(i)
        finish = [0.0] * n
        cp = [0.0] * n
        if self.PRIO == "cp":
            for i in range(n - 1, -1, -1):
                m = 0.0
                for s_ in succ[i]:
                    if cp[s_] > m:
                        m = cp[s_]
                cp[i] = m + nodes[i]["dur"] + self.LAT
        heaps = {e: [] for e in self.ENGS}
        t_e = {e: 0.0 for e in self.ENGS}

        def push(i):
            nd = nodes[i]
            rdy = 0.0
            for d in nd["deps"]:
                f = finish[d] + (0.0 if nodes[d]["eng"] == nd["eng"] == "pe" else self.LAT)
                if f > rdy:
                    rdy = f
            heapq.heappush(heaps[nd["eng"]], (rdy, i))
        for i in range(n):
            if npend[i] == 0:
                push(i)
        done = 0
        while done < n:
            best = None
            for e in self.ENGS:
                h = heaps[e]
                if not h:
                    continue
                rdy, i = h[0]
                start = max(t_e[e], rdy)
                key = (start, i)
                if best is None or key < best[0]:
                    best = (key, e)
            (start, i), e = best
            h = heaps[e]
            cands = []
            while h and max(t_e[e], h[0][0]) <= start + 1e-9:
                cands.append(heapq.heappop(h))
            if self.PRIO == "cp":
                cands.sort(key=lambda x: (-cp[x[1]], x[1]))
            else:
                cands.sort(key=lambda x: x[1])
            rdy, i = cands[0]
            for c in cands[1:]:
                heapq.heappush(h, c)
            nd = nodes[i]
            if nd["kind"] == "dma":
                t_e[e] = start + 0.06
                finish[i] = start + nd["dur"]
            else:
                t_e[e] = start + nd["dur"]
                finish[i] = t_e[e]
            order[e].append(i)
            done += 1
            for s in succ[i]:
                npend[s] -= 1
                if npend[s] == 0:
                    push(s)
        self.est_makespan = max(t_e.values())
        return order

    def emit(self, stack):
        nc = self.nc
        nodes = self.nodes
        order = self._schedule()
        event = [None] * len(nodes)
        for e in self.CENG:
            cnt = 0
            for i in order[e]:
                if nodes[i]["kind"] == "op":
                    cnt += 1
                    event[i] = (e, cnt)
        ndma = 0
        slot_of = {}
        for e in self.ENGS:
            for i in order[e]:
                if nodes[i]["kind"] == "dma":
                    assert e == "sp"
                    slot = ndma % self.NDMA
                    event[i] = (("d", slot), 16 * (ndma // self.NDMA + 1))
                    slot_of[i] = (slot, 16 * (ndma // self.NDMA))
                    ndma += 1
        streams = {e: [] for e in self.ENGS}
        for e in self.ENGS:
            known = {}
            for i in order[e]:
                nd = nodes[i]
                need = {}
                if nd["kind"] == "dma":
                    slot, prev = slot_of[i]
                    if prev > 0:
                        need[("d", slot)] = prev
                for d in nd["deps"]:
                    k, v = event[d]
                    if k == e == "pe":
                        continue
                    if need.get(k, 0) < v:
                        need[k] = v
                for k, v in need.items():
                    if known.get(k, 0) >= v:
                        continue
                    known[k] = v
                    streams[e].append(("wait", k, v))
                if nd["kind"] == "dma":
                    streams[e].append(("dma", nd["args"][0], nd["args"][1], slot_of[i][0]))
                else:
                    streams[e].append(("op", nd["meth"], nd["args"], nd["kw"]))
            if e == "sp":
                last = {}
                for i in order[e]:
                    if nodes[i]["kind"] == "dma":
                        k, v = event[i]
                        last[k] = max(last.get(k, 0), v)
                for k, v in last.items():
                    if known.get(k, 0) < v:
                        streams[e].append(("wait", k, v))

        sems = {}
        for e in self.CENG:
            sems[e] = stack.enter_context(nc.semaphore("c_" + e))
        for s in range(self.NDMA):
            sems[("d", s)] = stack.enter_context(nc.semaphore("d%d" % s))
        block = stack.enter_context(nc.Block())

        def run(engname, eng):
            for item in streams[engname]:
                if item[0] == "wait":
                    eng.wait_ge(sems[item[1]], item[2])
                elif item[0] == "op":
                    getattr(eng, item[1])(*item[2], **item[3]).then_inc(sems[engname], 1)
                else:
                    _, out, in_, slot = item
                    eng.dma_start(out=out, in_=in_).then_inc(sems[("d", slot)], 16)

        @block.tensor
        def _(e):
            run("pe", e)

        @block.scalar
        def _(e):
            run("act", e)

        @block.vector
        def _(e):
            run("dve", e)

        @block.gpsimd
        def _(e):
            run("pool", e)

        @block.sync
        def _(e):
            run("sp", e)


NORM_EPS = 1e-6
GN_EPS = 64e-5
NSTAGE = 2
NHT = 1
INV_ENG = "dve"
NET = 3


class TL:
    def __init__(self, h, name):
        self.h = h
        self.b = Buf(name)

    def __getitem__(self, k):
        return self.h[k]


def build_phase_a(S_len):
    NB = S_len // 512
    NKB = S_len // 128
    nc = bass.Bass("TRN2", target_bir_lowering=False)
    x_d = nc.dram_tensor("x", [S_len, 1024], F32, kind="ExternalInput").ap()
    wc_d = nc.dram_tensor("wc", [1024, 1152], F32, kind="ExternalInput").ap()
    adaw_d = nc.dram_tensor("adaw", [1024, 2048], F32, kind="ExternalInput").ap()
    pv_d = nc.dram_tensor("pv", [128, 48], F32, kind="ExternalInput").ap()
    w2a2_d = nc.dram_tensor("w2a2", [128, 128], F32, kind="ExternalInput").ap()
    bc_d = nc.dram_tensor("bc", [128, 3, 128], F32, kind="ExternalInput").ap()
    y_d = nc.dram_tensor("y", [S_len, 256], F32, kind="ExternalOutput").ap()
    g_d = nc.dram_tensor("gs", [S_len, 128], F32, kind="ExternalOutput").ap()

    S = Sched(nc)
    st = ExitStack()
    with st:
        def sb(name, shape, dt=F32):
            return TL(st.enter_context(nc.sbuf_tensor("s_" + name, shape, dt)), name)

        def ps(name, shape, dt=F32):
            return TL(st.enter_context(nc.psum_tensor("p_" + name, shape, dt)), name)

        def rd(*ts):
            return [t.b for t in ts]

        bank = [ps("bank%d" % i, [128, 512], F32) for i in range(8)]

        def bview(i, shape_str=None, dt=None, **kw):
            ap = bank[i][:]
            if dt is not None:
                ap = ap.bitcast(dt)
            if shape_str:
                ap = ap.rearrange(shape_str, **kw)
            return ap

        pv = sb("pv", [128, 48])
        w2a2 = sb("w2a2", [128, 128])
        bct = sb("bct", [128, 3, 128])
        S.dma("sp", pv[:], pv_d[:, :], writes=rd(pv))
        S.dma("sp", w2a2[:], w2a2_d[:, :], writes=rd(w2a2))
        S.dma("sp", bct[:], bc_d[:, :, :], writes=rd(bct))

        identf = sb("identf", [128, 128])
        ident = sb("ident", [128, 128], BF16)
        S.op("pool", "memset", identf[:], 1.0, writes=rd(identf))
        S.op("pool", "affine_select", identf[:], identf[:], pattern=[[-1, 128]], compare_op=ALU.is_equal,
                                               fill=0.0, base=0, channel_multiplier=1, reads=rd(identf), writes=rd(identf))
        S.op("dve", "tensor_copy", ident[:], identf[:], reads=rd(identf), writes=rd(ident))

        def mkmask(name, pat, cm, op):
            m = sb(name, [128, 128])
            S.op("pool", "memset", m[:], 1.0, writes=rd(m))
            S.op("pool", "affine_select", m[:], m[:], pattern=[[pat, 128]], compare_op=op, fill=0.0,
                                                   base=0, channel_multiplier=cm, reads=rd(m), writes=rd(m))
            return m
        m_su = mkmask("m_su", 1, -1, ALU.is_gt)
        m_iu = mkmask("m_iu", 1, -1, ALU.is_ge)
        m_sl = mkmask("m_sl", -1, 1, ALU.is_gt)
        m_il = mkmask("m_il", -1, 1, ALU.is_ge)
        maskA = sb("maskA", [128, 4, 128])
        maskB = sb("maskB", [128, 6, 128])
        for i, m in enumerate([m_sl, m_su, m_sl, m_su]):
            S.op("pool", "tensor_copy", maskA[:, i, :], m[:], reads=rd(m), writes=rd(maskA))
        for i, m in enumerate([m_su, m_su, m_iu, m_iu, m_iu, m_iu]):
            S.op("pool", "tensor_copy", maskB[:, i, :], m[:], reads=rd(m), writes=rd(maskB))
        tri = sb("tri", [128, 128], BF16)
        S.op("dve", "tensor_copy", tri[:], m_il[:], reads=rd(m_il), writes=rd(tri))
        ones_bf = sb("ones_bf", [128, 64], BF16)
        S.op("pool", "memset", ones_bf[:], 1.0, writes=rd(ones_bf))
        blockones = sb("blockones", [128, 128])
        S.op("pool", "memset", blockones[:], 0.0, writes=rd(blockones))
        S.op("pool", "memset", blockones[0:64, 0:64], 1.0, writes=rd(blockones))
        S.op("pool", "memset", blockones[64:128, 64:128], 1.0, writes=rd(blockones))
        resetm = sb("resetm", [128, 512])
        S.op("pool", "memset", resetm[:], 1.0, writes=rd(resetm))
        S.op("pool", "memset", resetm[:].rearrange("p (c t) -> p c t", t=128)[:, :, 0:1], 0.0, writes=rd(resetm))

        cact = sb("cact", [128, 8])
        tmp8 = sb("tmp8", [128, 8])
        S.op("act", "activation", tmp8[:], pv[:, 0:8], AF.Exp, scale=-1.0, reads=rd(pv), writes=rd(tmp8))
        S.op("dve", "tensor_scalar", tmp8[:], tmp8[:], 1.0, None, ALU.add, reads=rd(tmp8), writes=rd(tmp8))
        S.op("dve", "reciprocal", tmp8[:], tmp8[:], reads=rd(tmp8), writes=rd(tmp8))
        S.op("dve", "tensor_tensor", cact[:], tmp8[:], pv[:, 0:8], ALU.mult, reads=rd(tmp8, pv), writes=rd(cact))

        stage = [sb("stage%d" % i, [128, 1152]) for i in range(NSTAGE)]
        stage = [stage[i % NSTAGE] for i in range(2)]
        modp = bank[0]
        for kc in range(8):
            for hf in range(2):
                stg = stage[hf]
                S.dma("sp", stg[:, 0:1024], adaw_d[kc * 128:(kc + 1) * 128, hf * 1024:(hf + 1) * 1024], writes=rd(stg))
                for jj in range(8):
                    j = hf * 8 + jj
                    S.op("pe", "matmul",
                        modp[:, kc * 16 + j: kc * 16 + j + 1], stg[:, jj * 128:(jj + 1) * 128], cact[:, kc:kc + 1],
                        start=True, stop=True, reads=rd(stg, cact), writes=rd(modp))
        modT = sb("modT", [128, 16])
        S.op("dve", "tensor_reduce", modT[:], modp[:, 0:128].rearrange("p (k j) -> p j k", j=16),
                                              axis=AX.X, op=ALU.add, reads=rd(modp), writes=rd(modT))
        Bco = sb("Bco", [128, 8])
        Aco = sb("Aco", [128, 8])
        S.op("dve", "tensor_tensor", Bco[:], modT[:, 0:8], pv[:, 16:24], ALU.add, reads=rd(modT, pv), writes=rd(Bco))
        S.op("dve", "tensor_tensor", Aco[:], modT[:, 8:16], pv[:, 24:32], ALU.add, reads=rd(modT, pv), writes=rd(Aco))
        S.op("dve", "scalar_tensor_tensor", Aco[:], Aco[:], 1.0, pv[:, 8:16], ALU.add, ALU.mult,
             reads=rd(Aco, pv), writes=rd(Aco))
        der = sb("der", [128, 4])
        S.op("dve", "tensor_scalar", der[:, 0:2], pv[:, 36:38], -1.0, None, ALU.mult, reads=rd(pv), writes=rd(der))
        S.op("dve", "tensor_scalar", der[:, 2:3], pv[:, 39:40], -1.0, 1.0, ALU.mult, ALU.add, reads=rd(pv), writes=rd(der))
        MU = lambda g: pv[:, 32 + g:33 + g]
        NEGW0, NEGA0, OMKA = der[:, 0:1], der[:, 1:2], der[:, 2:3]
        KK, KA, RK = pv[:, 38:39], pv[:, 39:40], pv[:, 40:41]

        W = sb("W", [128, 8, 1152], BF16)
        for kc in range(8):
            stg = stage[kc % 2]
            S.dma("sp", stg[:, 0:1152], wc_d[kc * 128:(kc + 1) * 128, :], writes=rd(stg))
            eng = "dve" if kc % 2 == 0 else "act"
            if eng == "dve":
                S.op("dve", "tensor_copy", W[:, kc, :], stg[:, 0:1152], reads=rd(stg), writes=rd(W))
            else:
                S.op("act", "activation", W[:, kc, :], stg[:, 0:1152], AF.Copy, reads=rd(stg), writes=rd(W))

        kT_all = sb("kT_all", [128, S_len], BF16)
        v_all = sb("v_all", [128, NKB, 128], BF16)
        raw = [sb("raw%d" % g, [128, 513]) for g in range(4)]
        for g in range(4):
            S.op("pool", "memset", raw[g][:, 0:1], 0.0, writes=rd(raw[g]))
        Tst = sb("Tst", [128, 64])
        Tb = sb("Tb", [128, 64], BF16)
        S.op("pool", "memset", Tst[:], 0.0, writes=rd(Tst))
        S.op("pool", "memset", Tb[:], 0.0, writes=rd(Tb))

        xt = [sb("xt%d" % i, [128, 1024]) for i in range(2)]
        junk = sb("junk", [128, 1024], BF16)
        ss = sb("ss", [128, 4])
        rstd = sb("rstd", [128, 4])
        xn = [sb("xn%d" % i, [128, 1024], BF16) for i in range(2)]
        tmpf = sb("tmpf", [128, 8, 128])
        hT = [sb("hT%d" % i, [128, 8, 512], BF16) for i in range(NHT)]
        sh = [sb("sh%d" % g, [128, 512]) for g in range(4)]
        dif = sb("dif", [128, 512])
        gate = sb("gate", [128, 4, 256])
        gtmp = sb("gtmp", [128, 256])
        th = dif
        e1 = sb("e1", [128, 512])
        ew = sb("ew", [128, 512])
        alpha = sb("alpha", [128, 512])
        kks = sb("kks", [128, 512])
        sq = e1
        rn = dif
        kkn = sb("kkn", [128, 512])
        bv = sb("bv", [128, 512])
        kmod = sb("kmod", [128, 512])
        cwn = sb("cwn", [128, 512])
        gam = sb("gam", [128, 512])
        ginv = sb("ginv", [128, 512])
        gprev = e1
        gco = sb("gco", [128, 512])
        ncC = sb("ncC", [128, 4])
        RtTh = [sb("RtTh%d" % h, [128, 512], BF16) for h in range(2)]
        AtTh = [sb("AtTh%d" % h, [128, 512], BF16) for h in range(2)]
        rkTh = [sb("rkTh%d" % h, [128, 512], BF16) for h in range(2)]
        qTh2 = [[sb("qTh%d_%d" % (i, h), [128, 512], BF16) for h in range(2)] for i in range(2)]
        kTbuf = [Buf("kT%d" % i) for i in range(NB)]
        vbuf = [Buf("v%d" % i) for i in range(NB)]
        for _t in RtTh + AtTh + rkTh + qTh2[0] + qTh2[1]:
            S.op("pool", "memset", _t[:], 0.0, writes=rd(_t))
        KtT = sb("KtT", [128, 512], BF16)
        BtT = sb("BtT", [128, 512], BF16)
        KcT = sb("KcT", [128, 512], BF16)
        BcT = sb("BcT", [128, 512], BF16)
        VT = sb("VT", [128, 512], BF16)
        tokm = [sb("tokm%d" % c, [128, 3, 128], BF16) for c in range(4)]
        Lab = [[sb("Lab%d_%d" % (c, i), [128, 4, 128], BF16) for i in range(2)] for c in range(4)]
        Mx = [sb("Mx%d" % c, [128, 6, 128], BF16) for c in range(4)]
        MT = [[sb("MT%d_%d" % (c, i), [128, 2, 128], BF16) for i in range(2)] for c in range(4)]
        Xb = sb("Xb", [128, 128], BF16)
        Ub = sb("Ub", [128, 128], BF16)
        yr = sb("yr", [128, 4, 128])
        yb = sb("yb", [128, 4, 128])
        st1 = sb("st1", [128, 8])
        st2 = sb("st2", [128, 8])
        st3 = sb("st3", [128, 8])
        ystage = [sb("ystage%d" % i, [128, 4, 128]) for i in range(1)]
        NSB = 2
        e_t = [sb("e_t%d" % i, [128, 512]) for i in range(NET)]
        sp_t = [sb("sp_t%d" % i, [128, 512], BF16) for i in range(NET)]
        E_t = [sb("E_t%d" % i, [128, 512], BF16) for i in range(2)]
        sc_t = [sb("sc_t%d" % i, [128, 4]) for i in range(NSB)]
        oacc = [sb("oacc%d" % h, [128, 4, 64]) for h in range(2)]
        osq = sb("osq", [128, 4, 64])
        ost = sb("ost", [128, 4])

        B_T = 0
        B_IP = [1, 2]
        B_SQ = [1, 2]
        B_MM = 0
        ZC = [5, 6, 7]
        PP = [3, 4]

        ipn = [0]

        def ipbank():
            b = B_IP[ipn[0] % 2]
            ipn[0] += 1
            return bank[b]

        def stageA(tb):
            hTb = hT[tb % NHT]
            yst = ystage[0]
            qTh = qTh2[tb % 2]
            xts = [xt[0], xt[1], xt[0], xt[1]]

            def xload(ts):
                S.dma("sp", xts[ts][:], x_d[tb * 512 + ts * 128: tb * 512 + (ts + 1) * 128, :], writes=rd(xts[ts]))
            xload(0)
            xload(1)
            for ts in range(4):
                S.op("act", "activation", junk[:], xts[ts][:], AF.Square, scale=1.0 / 32.0,
                                                          accum_out=ss[:, ts:ts + 1], reads=rd(xts[ts]), writes=rd(junk, ss))
                S.op("act", "activation", rstd[:, ts:ts + 1], ss[:, ts:ts + 1], AF.Ln, bias=NORM_EPS, reads=rd(ss), writes=rd(rstd))
                S.op("act", "activation", rstd[:, ts:ts + 1], rstd[:, ts:ts + 1], AF.Exp, scale=-0.5, reads=rd(rstd), writes=rd(rstd))
                xnb = xn[ts % 2]
                S.op("dve", "tensor_scalar", xnb[:], xts[ts][:], rstd[:, ts:ts + 1], None, ALU.mult,
                     reads=rd(xts[ts], rstd), writes=rd(xnb))
                pT = bank[B_T]
                pTv = bview(B_T, "p (k t) -> p k t", dt=BF16, t=128)
                for kc in range(8):
                    S.op("pe", "transpose", pTv[:, kc, :], xnb[:, kc * 128:(kc + 1) * 128], ident[:],
                         reads=rd(xnb, ident), writes=rd(pT))
                S.op("dve", "tensor_tensor", tmpf[:], pTv[:, 0:8, :], Aco[:].unsqueeze(2).to_broadcast([128, 8, 128]), ALU.mult,
                     reads=rd(pT, Aco), writes=rd(tmpf))
                S.op("pool", "tensor_tensor", hTb[:, :, ts * 128:(ts + 1) * 128], tmpf[:],
                                                               Bco[:].unsqueeze(2).to_broadcast([128, 8, 128]), ALU.add,
                     reads=rd(tmpf, Bco), writes=rd(hTb))
                if ts + 2 < 4:
                    xload(ts + 2)
                yield
            for g in range(6):
                pb = ipbank()
                for kc in range(8):
                    S.op("pe", "matmul", pb[:], W[:, kc, g * 128:(g + 1) * 128], hTb[:, kc, :],
                                                                     start=(kc == 0), stop=(kc == 7),
                         reads=rd(W, hTb), writes=rd(pb))
                if g < 4:
                    S.op("act", "activation", raw[g][:, 1:513], pb[:], AF.Copy, reads=rd(pb), writes=rd(raw[g]))
                    S.op("dve", "tensor_tensor", dif[:], raw[g][:, 0:512], raw[g][:, 1:513], ALU.subtract,
                         reads=rd(raw[g]), writes=rd(dif))
                    S.op("dve", "scalar_tensor_tensor", sh[g][:], dif[:], MU(g), raw[g][:, 1:513], ALU.mult, ALU.add,
                         reads=rd(dif, raw[g], pv), writes=rd(sh[g]))
                    S.op("pool", "tensor_copy", raw[g][:, 0:1], raw[g][:, 512:513], reads=rd(raw[g]), writes=rd(raw[g]))
                elif g == 4:
                    for h in range(2):
                        hs = slice(64 * h, 64 * h + 64)
                        S.op("act", "activation", qTh[h][hs, :], pb[hs, :], AF.Copy, reads=rd(pb), writes=rd(qTh[h]))
                else:
                    S.op("act", "activation", kT_all[:, tb * 512:(tb + 1) * 512], pb[:], AF.Copy,
                         reads=rd(pb), writes=[kTbuf[tb]])
                yield
            for ts in range(4):
                pb = ipbank()
                for kc in range(8):
                    S.op("pe", "matmul", pb[:, 0:384], hTb[:, kc, ts * 128:(ts + 1) * 128],
                                                                       W[:, kc, 768:1152], start=(kc == 0), stop=(kc == 7),
                         reads=rd(W, hTb), writes=rd(pb))
                S.op("act", "activation", v_all[:, tb * 4 + ts, :], pb[:, 0:128], AF.Copy,
                     reads=rd(pb), writes=[vbuf[tb]])
                S.op("act", "activation", gtmp[:], pb[:, 128:384], AF.Exp, scale=-1.0, reads=rd(pb), writes=rd(gtmp))
                S.op("dve", "tensor_scalar", gtmp[:], gtmp[:], 1.0, None, ALU.add, reads=rd(gtmp), writes=rd(gtmp))
                S.op("dve", "reciprocal", gtmp[:], gtmp[:], reads=rd(gtmp), writes=rd(gtmp))
                S.op("dve", "tensor_tensor", gate[:, ts, :], gtmp[:], pb[:, 128:384], ALU.mult,
                     reads=rd(gtmp, pb), writes=rd(gate))
                yield

            shr, shk, shv, shz = sh
            S.op("act", "activation", th[0:64, :], shz[0:64, :], AF.Exp, scale=2.0, reads=rd(shz), writes=rd(th))
            S.op("dve", "tensor_scalar", th[0:64, :], th[0:64, :], 1.0, None, ALU.add, reads=rd(th), writes=rd(th))
            S.op("dve", "reciprocal", th[0:64, :], th[0:64, :], reads=rd(th), writes=rd(th))
            S.op("dve", "tensor_scalar", th[0:64, :], th[0:64, :], -2.0, 1.0, ALU.mult, ALU.add, reads=rd(th), writes=rd(th))
            pdl = ipbank()
            S.op("pe", "matmul", pdl[:], w2a2[0:64, :], th[0:64, :], start=True, stop=True, reads=rd(w2a2, th), writes=rd(pdl))
            S.op("act", "activation", e1[:], pdl[:], AF.Exp, scale=-1.0, bias=NEGW0, reads=rd(pdl, der), writes=rd(e1))
            S.op("act", "activation", e1[:], e1[:], AF.Ln, bias=1.0, reads=rd(e1), writes=rd(e1))
            S.op("act", "activation", ew[:], e1[:], AF.Exp, scale=-1.0, bias=-0.5, reads=rd(e1), writes=rd(ew))
            yield
            pda = ipbank()
            S.op("pe", "matmul", pda[:], w2a2[64:128, :], shz[64:128, :], start=True, stop=True, reads=rd(w2a2, shz), writes=rd(pda))
            S.op("act", "activation", alpha[:], pda[:], AF.Exp, scale=-1.0, bias=NEGA0, reads=rd(pda, der), writes=rd(alpha))
            S.op("dve", "tensor_scalar", alpha[:], alpha[:], 1.0, None, ALU.add, reads=rd(alpha), writes=rd(alpha))
            S.op("dve", "reciprocal", alpha[:], alpha[:], reads=rd(alpha), writes=rd(alpha))
            yield
            S.op("dve", "tensor_scalar", kks[:], shk[:], KK, None, ALU.mult, reads=rd(shk, pv), writes=rd(kks))
            S.op("pool", "tensor_tensor", sq[:], kks[:], kks[:], ALU.mult, reads=rd(kks), writes=rd(sq))
            pss = ipbank()
            S.op("pe", "matmul", pss[:], blockones[:], sq[:], start=True, stop=True, reads=rd(blockones, sq), writes=rd(pss))
            yield
            S.op("dve", "tensor_scalar", rn[:], pss[:], 1e-24, None, ALU.max, reads=rd(pss), writes=rd(rn))
            S.op("act", "activation", rn[:], rn[:], AF.Ln, reads=rd(rn), writes=rd(rn))
            S.op("act", "activation", rn[:], rn[:], AF.Exp, scale=-0.5, reads=rd(rn), writes=rd(rn))
            S.op("dve", "tensor_tensor", kkn[:], kks[:], rn[:], ALU.mult, reads=rd(kks, rn), writes=rd(kkn))
            S.op("pool", "tensor_tensor", bv[:], kkn[:], alpha[:], ALU.mult, reads=rd(kkn, alpha), writes=rd(bv))
            S.op("dve", "tensor_scalar", kmod[:], alpha[:], KA, OMKA, ALU.mult, ALU.add, reads=rd(alpha, pv, der), writes=rd(kmod))
            S.op("dve", "tensor_tensor", kmod[:], kmod[:], shk[:], ALU.mult, reads=rd(kmod, shk), writes=rd(kmod))
            yield
            S.op("dve", "tensor_tensor_scan", cwn[:], resetm[:], ew[:], 0.0, ALU.mult, ALU.add, reads=rd(resetm, ew), writes=rd(cwn))
            S.op("act", "activation", gam[:], cwn[:], AF.Exp, scale=-1.0, reads=rd(cwn), writes=rd(gam))
            S.op("act", "activation", ginv[:], cwn[:], AF.Exp, reads=rd(cwn), writes=rd(ginv))
            S.op("pool", "tensor_tensor", gprev[:], cwn[:], ew[:], ALU.subtract, reads=rd(cwn, ew), writes=rd(gprev))
            S.op("act", "activation", gprev[:], gprev[:], AF.Exp, scale=-1.0, reads=rd(gprev), writes=rd(gprev))
            yield
            cwn3 = cwn[:].rearrange("p (c t) -> p c t", t=128)
            S.op("dve", "tensor_scalar", ncC[:], cwn3[:, :, 127], -1.0, None, ALU.mult, reads=rd(cwn), writes=rd(ncC))
            for c in range(4):
                S.op("act", "activation", gco[:, c * 128:(c + 1) * 128], cwn[:, c * 128:(c + 1) * 128], AF.Exp,
                                                        bias=ncC[:, c:c + 1], reads=rd(cwn, ncC), writes=rd(gco))
            for h in range(2):
                hs = slice(64 * h, 64 * h + 64)
                S.op("dve", "tensor_tensor", RtTh[h][hs, :], shr[hs, :], gam[hs, :], ALU.mult, reads=rd(shr, gam), writes=rd(RtTh[h]))
                S.op("dve", "scalar_tensor_tensor", AtTh[h][hs, :], kkn[hs, :], -1.0, gprev[hs, :], ALU.mult, ALU.mult, reads=rd(kkn, gprev), writes=rd(AtTh[h]))
            S.op("dve", "tensor_tensor", KtT[:], kmod[:], ginv[:], ALU.mult, reads=rd(kmod, ginv), writes=rd(KtT))
            S.op("pool", "tensor_tensor", BtT[:], bv[:], ginv[:], ALU.mult, reads=rd(bv, ginv), writes=rd(BtT))
            S.op("dve", "tensor_tensor", KcT[:], kmod[:], gco[:], ALU.mult, reads=rd(kmod, gco), writes=rd(KcT))
            S.op("pool", "tensor_tensor", BcT[:], bv[:], gco[:], ALU.mult, reads=rd(bv, gco), writes=rd(BcT))
            S.op("act", "activation", VT[:], shv[:], AF.Copy, reads=rd(shv), writes=rd(VT))
            yield
            for h in range(2):
                hs = slice(64 * h, 64 * h + 64)
                S.op("dve", "scalar_tensor_tensor", rkTh[h][hs, :], shr[hs, :], pv[hs, 40:41], kmod[hs, :], ALU.mult, ALU.mult, reads=rd(shr, kmod, pv), writes=rd(rkTh[h]))
            for c in range(4):
                cs = slice(c * 128, (c + 1) * 128)
                pT = bank[B_T]
                pTv = bview(B_T, "p (k t) -> p k t", dt=BF16, t=128)
                for i, src in enumerate([KcT, BcT, VT]):
                    S.op("pe", "transpose", pTv[:, i, :], src[:, cs], ident[:], reads=rd(src, ident), writes=rd(pT))
                S.op("act", "activation", tokm[c][:], pTv[:, 0:3, :], AF.Copy, reads=rd(pT), writes=rd(tokm[c]))
                yield
            pbn = ipbank()
            for c in range(4):
                cs = slice(c * 128, (c + 1) * 128)
                for h in range(2):
                    hs = slice(64 * h, 64 * h + 64)
                    S.op("pe", "matmul", pbn[:, c * 128 + 64 * h: c * 128 + 64 * h + 64], rkTh[h][:, cs], ones_bf[:, :],
                                                                           start=True, stop=True, reads=rd(rkTh[h], ones_bf), writes=rd(pbn))
            for c in range(4):
                S.op("dve", "tensor_tensor", yb[:, c, :], pbn[:, c * 128:(c + 1) * 128], tokm[c][:, 2, :], ALU.mult,
                     reads=rd(pbn, tokm[c]), writes=rd(yb))
                yield

            for c in range(4):
                cs = slice(c * 128, (c + 1) * 128)
                pa = bank[B_SQ[c % 2]]
                pav = bview(B_SQ[c % 2], "p (k t) -> p k t", t=128)
                for h in range(2):
                    hs = slice(64 * h, 64 * h + 64)
                    S.op("pe", "matmul", pav[:, 2 * h, :], AtTh[h][:, cs], BtT[:, cs], start=True, stop=True,
                         reads=rd(AtTh[h], BtT), writes=rd(pa))
                    S.op("pe", "matmul", pav[:, 2 * h + 1, :], BtT[:, cs], AtTh[h][:, cs], start=True, stop=True,
                         reads=rd(AtTh[h], BtT), writes=rd(pa))
                L0 = Lab[c][0]
                S.op("dve", "tensor_tensor", L0[:], pav[:, :, :], maskA[:], ALU.mult, reads=rd(pa, maskA), writes=rd(L0))
                yield
                pb_ = ipbank()
                pbv = pb_[:].rearrange("p (k t) -> p k t", t=128)
                combos = [(KtT, AtTh[0]), (KtT, AtTh[1]), (BtT, RtTh[0]), (BtT, RtTh[1])]
                for i in range(4):
                    h = i % 2
                    hs = slice(64 * h, 64 * h + 64)
                    l, r = combos[i]
                    S.op("pe", "matmul", pbv[:, i, :], l[:, cs], r[:, cs], start=True, stop=True,
                         reads=rd(l, r), writes=rd(pb_))
                S.op("dve", "tensor_tensor", Mx[c][:, 0:4, :], pbv[:, :, :], maskB[:, 0:4, :], ALU.mult,
                     reads=rd(pb_, maskB), writes=rd(Mx[c]))
                pc_ = ipbank()
                pcv = pc_[:].rearrange("p (k t) -> p k t", t=128)
                for h in range(2):
                    hs = slice(64 * h, 64 * h + 64)
                    S.op("pe", "matmul", pcv[:, h, :], KtT[:, cs], RtTh[h][:, cs], start=True, stop=True,
                         reads=rd(KtT, RtTh[h]), writes=rd(pc_))
                S.op("dve", "tensor_tensor", Mx[c][:, 4:6, :], pcv[:, 0:2, :], maskB[:, 4:6, :], ALU.mult,
                     reads=rd(pc_, maskB), writes=rd(Mx[c]))
                yield
                M0 = MT[c][0]
                for h in range(2):
                    S.op("pool", "tensor_tensor", M0[:, h, :], L0[:, 2 * h + 1, :], ident[:], ALU.add,
                         reads=rd(L0, ident), writes=rd(M0))
            for k in range(1, 7):
                for c in range(4):
                    Lp = Lab[c][(k - 1) % 2]
                    Ln_ = Lab[c][k % 2]
                    Mp = MT[c][(k - 1) % 2]
                    Mn = MT[c][k % 2]
                    pa = bank[B_SQ[c % 2]]
                    pav = bview(B_SQ[c % 2], "p (k t) -> p k t", t=128)
                    for h in range(2):
                        S.op("pe", "matmul", pav[:, 2 * h, :], Lp[:, 2 * h + 1, :], Lp[:, 2 * h, :], start=True, stop=True,
                             reads=rd(Lp), writes=rd(pa))
                        S.op("pe", "matmul", pav[:, 2 * h + 1, :], Lp[:, 2 * h, :], Lp[:, 2 * h + 1, :], start=True, stop=True,
                             reads=rd(Lp), writes=rd(pa))
                    if INV_ENG == "act":
                        S.op("act", "activation", Ln_[:], pav[:, :, :], AF.Copy, reads=rd(pa), writes=rd(Ln_))
                    else:
                        S.op("dve", "tensor_copy", Ln_[:], pav[:, :, :], reads=rd(pa), writes=rd(Ln_))
                    yield
                    pm = bank[B_MM]
                    pmv = bview(B_MM, "p (k t) -> p k t", t=128)
                    off = 2 * (c % 2)
                    for h in range(2):
                        S.op("pe", "matmul", pmv[:, off + h, :], Ln_[:, 2 * h, :], Mp[:, h, :], start=True, stop=True,
                             reads=rd(Ln_, Mp), writes=rd(pm))
                    S.op("dve", "tensor_tensor", Mn[:], pmv[:, off:off + 2, :], Mp[:], ALU.add,
                         reads=rd(pm, Mp), writes=rd(Mn))
                yield
            MTf = [MT[c][0] for c in range(4)]

            gam3 = gam[:].rearrange("p (c t) -> p c t", t=128)
            for c in range(4):
                cs = slice(c * 128, (c + 1) * 128)
                pm = bank[B_MM]
                Vt = tokm[c]
                for h in range(2):
                    hs = slice(64 * h, 64 * h + 64)
                    S.op("pe", "matmul", pm[:, hs], AtTh[h][:, cs], Tb[:, :], start=True, stop=False,
                         reads=rd(AtTh[h], Tb), writes=rd(pm))
                    S.op("pe", "matmul", pm[:, hs], Mx[c][:, h, :], Vt[:, 2, hs], start=False, stop=True,
                         reads=rd(Mx[c], Vt), writes=rd(pm))
                S.op("dve", "tensor_copy", Xb[:], pm[:, 0:128], reads=rd(pm), writes=rd(Xb))
                yield
                for h in range(2):
                    hs = slice(64 * h, 64 * h + 64)
                    S.op("pe", "matmul", pm[:, 128 + 64 * h:128 + 64 * h + 64], MTf[c][:, h, :], Xb[:, hs], start=True, stop=True,
                         reads=rd(MTf[c], Xb), writes=rd(pm))
                S.op("act", "activation", Ub[:], pm[:, 128:256], AF.Copy, reads=rd(pm), writes=rd(Ub))
                yield
                for h in range(2):
                    hs = slice(64 * h, 64 * h + 64)
                    ys = slice(256 + 64 * h, 256 + 64 * h + 64)
                    S.op("pe", "matmul", pm[:, ys], RtTh[h][:, cs], Tb[:, :], start=True, stop=False,
                         reads=rd(RtTh[h], Tb), writes=rd(pm))
                    S.op("pe", "matmul", pm[:, ys], Mx[c][:, 2 + h, :], Ub[:, hs], start=False, stop=False,
                         reads=rd(Mx[c], Ub), writes=rd(pm))
                    S.op("pe", "matmul", pm[:, ys], Mx[c][:, 4 + h, :], Vt[:, 2, hs], start=False, stop=True,
                         reads=rd(Mx[c], Vt), writes=rd(pm))
                S.op("act", "activation", yr[:, c, :], pm[:, 256:384], AF.Copy, reads=rd(pm), writes=rd(yr))
                for h in range(2):
                    hs = slice(64 * h, 64 * h + 64)
                    S.op("pe", "matmul", pm[hs, 384:448], Vt[:, 1, hs], Ub[:, hs], start=True, stop=False,
                         reads=rd(Vt, Ub), writes=rd(pm))
                    S.op("pe", "matmul", pm[hs, 384:448], Vt[:, 0, hs], Vt[:, 2, hs], start=False, stop=True,
                         reads=rd(Vt), writes=rd(pm))
                S.op("dve", "scalar_tensor_tensor", Tst[:], Tst[:], gam3[:, c, 127:128], pm[:, 384:448], ALU.mult, ALU.add,
                     reads=rd(Tst, gam, pm), writes=rd(Tst))
                S.op("act", "activation", Tb[:], Tst[:], AF.Copy, reads=rd(Tst), writes=rd(Tb))
                yield

            yr8 = yr[:].rearrange("p c (h i) -> p (c h) i", i=64)
            ysq = gco
            ysq8 = ysq[:].rearrange("p (c h i) -> p (c h) i", h=2, i=64)
            S.op("dve", "tensor_reduce", st1[:], yr8, axis=AX.X, op=ALU.add, reads=rd(yr), writes=rd(st1))
            S.op("pool", "tensor_tensor", ysq[:], yr[:].rearrange("p c x -> p (c x)"), yr[:].rearrange("p c x -> p (c x)"), ALU.mult, reads=rd(yr), writes=rd(ysq))
            S.op("dve", "tensor_reduce", st2[:], ysq8, axis=AX.X, op=ALU.add, reads=rd(ysq), writes=rd(st2))
            S.op("dve", "tensor_scalar", st1[:], st1[:], 1.0 / 64, None, ALU.mult, reads=rd(st1), writes=rd(st1))
            S.op("dve", "tensor_tensor", st3[:], st1[:], st1[:], ALU.mult, reads=rd(st1), writes=rd(st3))
            S.op("dve", "scalar_tensor_tensor", st2[:], st2[:], 1.0 / 64, st3[:], ALU.mult, ALU.subtract, reads=rd(st2, st3), writes=rd(st2))
            S.op("act", "activation", st2[:], st2[:], AF.Ln, bias=GN_EPS, reads=rd(st2), writes=rd(st2))
            S.op("act", "activation", st2[:], st2[:], AF.Exp, scale=-0.5, reads=rd(st2), writes=rd(st2))
            S.op("dve", "tensor_tensor", yr8, yr8, st1[:].unsqueeze(2).to_broadcast([128, 8, 64]), ALU.subtract, reads=rd(yr, st1), writes=rd(yr))
            S.op("dve", "tensor_tensor", yr8, yr8, st2[:].unsqueeze(2).to_broadcast([128, 8, 64]), ALU.mult, reads=rd(yr, st2), writes=rd(yr))
            for c in range(4):
                S.op("pool", "tensor_tensor", yr[:, c, :], yr[:, c, :], bct[:, 0, :], ALU.mult, reads=rd(yr, bct), writes=rd(yr))
                S.op("pool", "tensor_tensor", yr[:, c, :], yr[:, c, :], bct[:, 1, :], ALU.add, reads=rd(yr, bct), writes=rd(yr))
            S.op("dve", "tensor_tensor", yr[:], yr[:], yb[:], ALU.add, reads=rd(yr, yb), writes=rd(yr))
            S.op("dve", "tensor_tensor", yst[:, :, 0:128], yr[:], gate[:, :, 0:128], ALU.mult, reads=rd(yr, gate), writes=rd(yst))

            S.dma("sp", y_d[tb * 512:(tb + 1) * 512, 0:128].rearrange("(ts p) c -> p ts c", p=128), yst[:], reads=rd(yst))
            S.dma("sp", g_d[tb * 512:(tb + 1) * 512, :].rearrange("(ts p) c -> p ts c", p=128), gate[:, :, 128:256], reads=rd(gate))
            yield

        sbn = [0]

        def stageS(tb):
            qTh = qTh2[tb % 2]
            tiles = [(h, kb) for h in range(2) for kb in range(4 * tb + 4)]
            n = len(tiles)
            g0 = sbn[0]
            sbn[0] += n

            def res(i):
                g = g0 + i
                h, kb = tiles[i]
                dk = max(0, kb - 4 * tb)
                return dict(h=h, kb=kb, dk=dk, q0=dk * 128, qsl=slice(dk * 128, 512), hs=slice(64 * h, 64 * h + 64),
                            zc=bank[ZC[g % len(ZC)]], pp=bank[PP[g % len(PP)]], et=e_t[g % NET], sa=sp_t[g % NET], Et=E_t[g % 2], sct=sc_t[g % 2],
                            oa=oacc[h])

            def front(i):
                r = res(i)
                S.op("pe", "matmul", r['zc'][:, r['qsl']], kT_all[:, r['kb'] * 128:(r['kb'] + 1) * 128], qTh[r['h']][:, r['qsl']],
                     start=True, stop=True, reads=[kTbuf[r['kb'] // 4]] + rd(qTh[r['h']]), writes=rd(r['zc']))

            def mid(i):
                r = res(i)
                qsl, et, sa, zc, pp, dk, q0 = r['qsl'], r['et'], r['sa'], r['zc'], r['pp'], r['dk'], r['q0']
                S.op("act", "activation", et[:, qsl], zc[:, qsl], AF.Exp, scale=0.125, reads=rd(zc), writes=rd(et))
                if r['kb'] >= 4 * tb:
                    S.op("pool", "affine_select", et[:, qsl], et[:, qsl], pattern=[[1, 512 - q0]], compare_op=ALU.is_gt, fill=0.0,
                         base=0, channel_multiplier=-1, reads=rd(et), writes=rd(et))
                S.op("act", "activation", sa[:, qsl], et[:, qsl], AF.Ln, bias=1.0, reads=rd(et), writes=rd(sa))
                S.op("pe", "matmul", zc[:, qsl], tri[:], sa[:, qsl], start=True, stop=True, reads=rd(tri, sa), writes=rd(zc))
                for qs in range(dk, 4):
                    S.op("pe", "matmul", pp[:, 256 + qs:257 + qs], sa[:, qs * 128:(qs + 1) * 128], ones_bf[:, 0:1],
                         start=True, stop=True, reads=rd(sa, ones_bf), writes=rd(pp))

            def back(i):
                r = res(i)
                qsl, et, sa, zc, pp, dk, Et, sct, oa, hs, kb, h = (r['qsl'], r['et'], r['sa'], r['zc'], r['pp'], r['dk'], r['Et'],
                                                                   r['sct'], r['oa'], r['hs'], r['kb'], r['h'])
                S.op("act", "activation", Et[:, qsl], zc[:, qsl], AF.Exp, scale=-1.0, reads=rd(zc), writes=rd(Et))
                if kb > 0:
                    S.op("act", "activation", sct[:, dk:4], pp[:, 256 + dk:260], AF.Exp, scale=-1.0, reads=rd(pp), writes=rd(sct))
                S.op("dve", "tensor_tensor", sa[:, qsl], et[:, qsl], Et[:, qsl], ALU.mult, reads=rd(et, Et), writes=rd(sa))
                for qs in range(dk, 4):
                    S.op("pe", "matmul", pp[:, qs * 64:(qs + 1) * 64], sa[:, qs * 128:(qs + 1) * 128], v_all[:, kb, hs],
                         start=True, stop=True, reads=rd(sa) + [vbuf[kb // 4]], writes=rd(pp))
                ppv = pp[:, 0:256].rearrange("p (q d) -> p q d", d=64)
                if kb == 0:
                    S.op("dve", "tensor_copy", oa[:], ppv, reads=rd(pp), writes=rd(oa))
                else:
                    S.op("pool", "tensor_tensor", oa[:, dk:4, :], oa[:, dk:4, :],
                         sct[:, dk:4].unsqueeze(2).to_broadcast([128, 4 - dk, 64]), ALU.mult, reads=rd(oa, sct), writes=rd(oa))
                    S.op("dve", "tensor_tensor", oa[:, dk:4, :], oa[:, dk:4, :], ppv[:, dk:4, :], ALU.add, reads=rd(oa, pp), writes=rd(oa))
                if kb == 4 * tb + 3:
                    S.op("pool", "tensor_tensor", osq[:], oa[:], oa[:], ALU.mult, reads=rd(oa), writes=rd(osq))
                    S.op("dve", "tensor_reduce", ost[:], osq[:], axis=AX.X, op=ALU.add, reads=rd(osq), writes=rd(ost))
                    S.op("act", "activation", ost[:], ost[:], AF.Ln, scale=1.0 / 64, bias=NORM_EPS, reads=rd(ost), writes=rd(ost))
                    S.op("act", "activation", ost[:], ost[:], AF.Exp, scale=-0.5, reads=rd(ost), writes=rd(ost))
                    S.op("dve", "tensor_tensor", oa[:], oa[:], ost[:].unsqueeze(2).to_broadcast([128, 4, 64]), ALU.mult,
                         reads=rd(oa, ost), writes=rd(oa))
                    for qs in range(4):
                        S.op("pool", "tensor_tensor", oa[:, qs, :], oa[:, qs, :], bct[:, 2, hs], ALU.mult, reads=rd(oa, bct), writes=rd(oa))
                    S.dma("sp", y_d.rearrange("(q nb) c -> q nb c", nb=NKB)[:, 4 * tb:4 * tb + 4, 128 + 64 * h:128 + 64 * h + 64],
                          oa[:], reads=rd(oa))

            front(0)
            if n > 1:
                front(1)
            mid(0)
            for i in range(n):
                if i + 2 < n:
                    front(i + 2)
                if i + 1 < n:
                    mid(i + 1)
                back(i)
                yield

        def drive(gens):
            lists = []
            for g in gens:
                lists.append(g)
            alive = list(lists)
            while alive:
                for g in list(alive):
                    try:
                        next(g)
                    except StopIteration:
                        alive.remove(g)

        def count(genf, tb):
            en = S.enabled
            S.enabled = False
            saved = sbn[0]
            c = sum(1 for _ in genf(tb))
            sbn[0] = saved
            S.enabled = en
            return c

        def merged(gS, nS, gA, nA):
            iS = iA = 0
            while iS < nS or iA < nA:
                if iA < nA and (iS >= nS or iA * max(nS, 1) <= iS * nA):
                    next(gA, None)
                    iA += 1
                else:
                    next(gS, None)
                    iS += 1
            for _ in gS:
                pass
            for _ in gA:
                pass

        for _ in stageA(0):
            pass
        for tb in range(NB):
            nS = count(stageS, tb)
            if tb + 1 < NB:
                nA = count(stageA, tb + 1)
                merged(stageS(tb), nS, stageA(tb + 1), nA)
            else:
                for _ in stageS(tb):
                    pass

        S.emit(st)
    return nc


def build_phase_b(ntok, final):
    NT = ntok // 128
    nc = bass.Bass("TRN2", target_bir_lowering=False)
    x_d = nc.dram_tensor("x", [ntok, 1024], F32, kind="ExternalInput").ap()
    y_d = nc.dram_tensor("y", [ntok, 1024], F32, kind="ExternalInput").ap()
    g_d = nc.dram_tensor("gs", [ntok, 512], F32, kind="ExternalInput").ap()
    wo_d = nc.dram_tensor("wo", [1024, 1024], F32, kind="ExternalInput").ap()
    adaw_d = nc.dram_tensor("adaw", [1024, 1024], F32, kind="ExternalInput").ap()
    pv_d = nc.dram_tensor("pv", [128, 8], F32, kind="ExternalInput").ap()
    brow_d = nc.dram_tensor("brow", [1, 1024], F32, kind="ExternalInput").ap()
    fg_d = nc.dram_tensor("fg", [128, 1024], F32, kind="ExternalInput").ap()
    o_d = nc.dram_tensor("xo", [ntok, 1024], F32, kind="ExternalOutput").ap()

    S = Sched(nc)
    st = ExitStack()
    with st:
        def sb(name, shape, dt=F32):
            return TL(st.enter_context(nc.sbuf_tensor("s_" + name, shape, dt)), name)

        def ps(name, shape, dt=F32):
            return TL(st.enter_context(nc.psum_tensor("p_" + name, shape, dt)), name)

        def rd(*ts):
            return [t.b for t in ts]

        bank = [ps("bank%d" % i, [128, 512], F32) for i in range(8)]
        pv = sb("pv", [128, 8])
        brow = sb("brow", [1, 1024])
        fg = sb("fg", [128, 1024])
        S.dma("sp", pv[:], pv_d[:, :], writes=rd(pv))
        S.dma("sp", brow[:], brow_d[:, :], writes=rd(brow))
        S.dma("sp", fg[:], fg_d[:, :], writes=rd(fg))
        identf = sb("identf", [128, 128])
        ident = sb("ident", [128, 128], BF16)
        S.op("pool", "memset", identf[:], 1.0, writes=rd(identf))
        S.op("pool", "affine_select", identf[:], identf[:], pattern=[[-1, 128]], compare_op=ALU.is_equal,
             fill=0.0, base=0, channel_multiplier=1, reads=rd(identf), writes=rd(identf))
        S.op("dve", "tensor_copy", ident[:], identf[:], reads=rd(identf), writes=rd(ident))
        onesf = sb("onesf", [1, 128])
        S.op("pool", "memset", onesf[:], 1.0, writes=rd(onesf))
        cact = sb("cact", [128, 8])
        tmp8 = sb("tmp8", [128, 8])
        S.op("act", "activation", tmp8[:], pv[:, 0:8], AF.Exp, scale=-1.0, reads=rd(pv), writes=rd(tmp8))
        S.op("dve", "tensor_scalar", tmp8[:], tmp8[:], 1.0, None, ALU.add, reads=rd(tmp8), writes=rd(tmp8))
        S.op("dve", "reciprocal", tmp8[:], tmp8[:], reads=rd(tmp8), writes=rd(tmp8))
        S.op("dve", "tensor_tensor", cact[:], tmp8[:], pv[:, 0:8], ALU.mult, reads=rd(tmp8, pv), writes=rd(cact))
        stage = [sb("stage%d" % i, [128, 1024]) for i in range(2)]
        grow = sb("grow", [1, 1024])
        for kc in range(8):
            stg = stage[kc % 2]
            S.dma("sp", stg[:], adaw_d[kc * 128:(kc + 1) * 128, :], writes=rd(stg))
            for hf in range(2):
                S.op("pe", "matmul", bank[hf][0:1, :], cact[:, kc:kc + 1], stg[:, hf * 512:(hf + 1) * 512],
                     start=(kc == 0), stop=(kc == 7), reads=rd(cact, stg), writes=rd(bank[hf]))
        for hf in range(2):
            S.op("dve", "tensor_tensor", grow[:, hf * 512:(hf + 1) * 512], bank[hf][0:1, :], brow[:, hf * 512:(hf + 1) * 512], ALU.add,
                 reads=rd(bank[hf], brow), writes=rd(grow))
        gbc = sb("gbc", [128, 1024])
        for hf in range(2):
            S.op("pe", "matmul", bank[2 + hf][:], onesf[0:1, :], grow[0:1, hf * 512:(hf + 1) * 512], start=True, stop=True,
                 reads=rd(onesf, grow), writes=rd(bank[2 + hf]))
            S.op("act", "activation", gbc[:, hf * 512:(hf + 1) * 512], bank[2 + hf][:], AF.Copy, reads=rd(bank[2 + hf]), writes=rd(gbc))
        Wo = sb("Wo", [128, 8, 1024], BF16)
        for kc in range(8):
            stg = stage[kc % 2]
            S.dma("sp", stg[:], wo_d[kc * 128:(kc + 1) * 128, :], writes=rd(stg))
            S.op("dve", "tensor_tensor", Wo[:, kc, :], stg[:], gbc[:], ALU.mult, reads=rd(stg, gbc), writes=rd(Wo))

        xt = [sb("xt%d" % i, [128, 1024]) for i in range(2)]
        yt = [sb("yt%d" % i, [128, 1024]) for i in range(2)]
        gt = [sb("gt%d" % i, [128, 512]) for i in range(2)]
        yb = [sb("yb%d" % i, [128, 1024], BF16) for i in range(2)]
        yT = [sb("yT%d" % i, [128, 8, 128], BF16) for i in range(2)]
        xo = [sb("xo%d" % i, [128, 1024]) for i in range(2)]
        junk = sb("junk", [128, 1024], BF16)
        ss = sb("ss", [128, 1])
        for t in range(NT):
            i = t % 2
            rows = slice(t * 128, (t + 1) * 128)
            S.dma("sp", xt[i][:], x_d[rows, :], writes=rd(xt[i]))
            S.dma("sp", yt[i][:], y_d[rows, :], writes=rd(yt[i]))
            S.dma("sp", gt[i][:], g_d[rows, :], writes=rd(gt[i]))
            S.op("act", "activation", yb[i][:, 0:512], yt[i][:, 0:512], AF.Copy, reads=rd(yt[i]), writes=rd(yb[i]))
            S.op("dve", "tensor_tensor", yb[i][:, 512:1024], yt[i][:, 512:1024], gt[i][:], ALU.mult, reads=rd(yt[i], gt[i]), writes=rd(yb[i]))
            pT = bank[4 + i]
            pTv = pT[:].bitcast(BF16).rearrange("p (k t) -> p k t", t=128)
            for kc in range(8):
                S.op("pe", "transpose", pTv[:, kc, :], yb[i][:, kc * 128:(kc + 1) * 128], ident[:], reads=rd(yb[i], ident), writes=rd(pT))
            S.op("act", "activation", yT[i][:], pTv[:, 0:8, :], AF.Copy, reads=rd(pT), writes=rd(yT[i]))
            for hf in range(2):
                pb = bank[hf * 2 + i]
                for kc in range(8):
                    S.op("pe", "matmul", pb[:], yT[i][:, kc, :], Wo[:, kc, hf * 512:(hf + 1) * 512], start=(kc == 0), stop=(kc == 7),
                         reads=rd(yT[i], Wo), writes=rd(pb))
                S.op("dve", "tensor_tensor", xo[i][:, hf * 512:(hf + 1) * 512], pb[:], xt[i][:, hf * 512:(hf + 1) * 512], ALU.add,
                     reads=rd(pb, xt[i]), writes=rd(xo[i]))
            if final:
                S.op("act", "activation", junk[:], xo[i][:], AF.Square, scale=1.0 / 32.0, accum_out=ss[:], reads=rd(xo[i]), writes=rd(junk, ss))
                S.op("act", "activation", ss[:], ss[:], AF.Ln, bias=NORM_EPS, reads=rd(ss), writes=rd(ss))
                S.op("act", "activation", ss[:], ss[:], AF.Exp, scale=-0.5, reads=rd(ss), writes=rd(ss))
                S.op("dve", "scalar_tensor_tensor", xo[i][:], xo[i][:], ss[:, 0:1], fg[:], ALU.mult, ALU.mult, reads=rd(xo[i], ss, fg), writes=rd(xo[i]))
            S.dma("sp", o_d[rows, :], xo[i][:], reads=rd(xo[i]))
        S.emit(st)
    return nc


D=1024; RW=512; SHIFT=1664; RWKV_COLS=2176

def cols_for(hp):
    r = np.arange(128)+128*hp
    return dict(r=r, k=512+r, v=1024+r, zz=np.arange(1536,1664), g=1664+r,
                q=RWKV_COLS+r, ks=RWKV_COLS+512+r, vs=RWKV_COLS+1024+r, gs=RWKV_COLS+1536+r)

def prep_a(inp, l, xcur, S_len):
    maps = []
    for core in range(8):
        b, hp = core // 4, core % 4
        cs = cols_for(hp)
        order = [cs['r'], cs['k'], cs['v'], cs['zz'], cs['q'], cs['ks'], cs['vs'], cs['g'], cs['gs']]
        wc = np.ascontiguousarray(inp['w_in'][l][:, np.concatenate(order)])
        adaw = np.ascontiguousarray(inp['ada_w'][l][:, :2048])
        pv = np.zeros((128, 48), np.float32)
        t8 = lambda v: np.ascontiguousarray(v.reshape(8, 128).T)
        pv[:, 0:8] = t8(inp['c'][b])
        pv[:, 8:16] = t8(inp['norm_g'][l])
        pv[:, 16:24] = t8(inp['ada_b'][l][0:1024])
        pv[:, 24:32] = t8(inp['ada_b'][l][1024:2048])
        mu = inp['tshift_mu'][l]
        pv[:, 32] = mu[cs['r']]; pv[:, 33] = mu[cs['k']]; pv[:, 34] = mu[cs['v']]; pv[:, 35] = mu[cs['zz']]
        hc = cs['r']
        pv[:, 36] = inp['decay_w0'][l][hc]; pv[:, 37] = inp['iclr_a0'][l][hc]
        pv[:, 38] = inp['k_k'][l][hc]; pv[:, 39] = inp['k_a'][l][hc]
        pv[:, 40] = inp['r_k'][l].reshape(512)[hc]
        w2a2 = np.concatenate([inp['decay_w2'][l][:, hc], inp['iclr_a2'][l][:, hc]], axis=0).astype(np.float32)
        bc = np.stack([np.broadcast_to(inp['rwkv_ln_w'][l][hc], (128, 128)),
                       np.broadcast_to(inp['rwkv_ln_b'][l][hc], (128, 128)),
                       np.broadcast_to(inp['sb_norm_g'][l][hc], (128, 128))], axis=1).astype(np.float32)
        maps.append(dict(x=np.ascontiguousarray(xcur[b, :S_len]), wc=wc, adaw=adaw, pv=pv,
                         w2a2=np.ascontiguousarray(w2a2), bc=np.ascontiguousarray(bc)))
    return maps

def assemble_y(results, S_len):
    y = np.zeros((2, S_len, 1024), np.float32)
    g = np.zeros((2, S_len, 512), np.float32)
    for core in range(8):
        b, hp = core // 4, core % 4
        yc = results[core]['y']
        y[b, :, 128*hp:128*hp+128] = yc[:, 0:128]
        y[b, :, 512+128*hp:512+128*hp+128] = yc[:, 128:256]
        g[b, :, 128*hp:128*hp+128] = results[core]['gs']
    return y, g


_CACHE = {}


def _get(kind, *args):
    key = (kind,) + args
    if key not in _CACHE:
        _CACHE[key] = build_phase_a(*args) if kind == "a" else build_phase_b(*args)
    return _CACHE[key]


def prep_b(inp, l, xcur, y, g):
    maps = []
    t8 = lambda v: np.ascontiguousarray(v.reshape(8, 128).T)
    xf = xcur.reshape(16384, 1024)
    yf = y.reshape(16384, 1024)
    gf = g.reshape(16384, 512)
    for core in range(8):
        b = core // 4
        rows = slice(core * 2048, (core + 1) * 2048)
        maps.append(dict(
            x=np.ascontiguousarray(xf[rows]), y=np.ascontiguousarray(yf[rows]), gs=np.ascontiguousarray(gf[rows]),
            wo=np.ascontiguousarray(inp['w_out'][l]), adaw=np.ascontiguousarray(inp['ada_w'][l][:, 2048:3072]),
            pv=t8(inp['c'][b]), brow=np.ascontiguousarray(inp['ada_b'][l][2048:3072].reshape(1, 1024)),
            fg=np.ascontiguousarray(np.broadcast_to(inp['final_g'], (128, 1024)))))
    return maps


def kernel(**inputs):
    inp = {k: np.asarray(v, dtype=np.float32) for k, v in inputs.items()}
    S_len = inp['x'].shape[1]
    depth = inp['w_in'].shape[0]
    xcur = inp['x']
    for l in range(depth):
        nca = _get("a", S_len)
        res = run_bass_kernel_spmd(nca, prep_a(inp, l, xcur, S_len), core_ids=list(range(8)))
        y, g = assemble_y(res.results, S_len)
        ncb = _get("b", 2048, l == depth - 1)
        res = run_bass_kernel_spmd(ncb, prep_b(inp, l, xcur, y, g), core_ids=list(range(8)))
        xcur = np.concatenate([r['xo'] for r in res.results], axis=0).reshape(2, S_len, 1024)
    return np.ascontiguousarray(xcur.astype(np.float32))
```

```python
from contextlib import ExitStack
import heapq
from concourse.bass_utils import run_bass_kernel_spmd
import numpy as np
import concourse.bass as bass
import concourse.mybir as mybir

F32 = mybir.dt.float32
BF16 = mybir.dt.bfloat16
ALU = mybir.AluOpType
AF = mybir.ActivationFunctionType
AX = mybir.AxisListType


class Buf:
    __slots__ = ("name", "w", "r")

    def __init__(self, name):
        self.name = name
        self.w = None
        self.r = []


def _fsize(ap):
    n = 1
    for s in tuple(ap.shape)[1:]:
        n *= int(s)
    return n


class Sched:
    CENG = ("pe", "act", "dve", "pool")
    ENGS = CENG + ("sp",)
    NDMA = 24
    LAT = 0.12
    PRIO = "cp"

    def __init__(self, nc, reorder=True):
        self.nc = nc
        self.nodes = []
        self.enabled = True
        self.reorder = reorder

    def _record(self, eng, kind, meth, args, kw, reads, writes, dur):
        deps = set()
        for b in reads:
            if b.w is not None:
                deps.add(b.w)
        for b in writes:
            if b.w is not None:
                deps.add(b.w)
            deps.update(b.r)
        nid = len(self.nodes)
        self.nodes.append(dict(eng=eng, kind=kind, meth=meth, args=args, kw=kw, deps=deps, dur=dur))
        for b in reads:
            b.r.append(nid)
        for b in writes:
            b.w = nid
            b.r = []
        return nid

    def _est(self, eng, meth, args):
        if eng == "pe":
            if meth == "transpose":
                return 0.11
            n = _fsize(args[2])
            c = max(64, n) / 1400.0
            if args[1].dtype == F32:
                c *= 4
            return c + 0.03
        n = _fsize(args[0])
        if eng == "act":
            return 0.17 + n / 1400.0
        if eng == "dve":
            return 0.07 + n / 960.0
        return 0.12 + n / 640.0

    def op(self, eng, meth, *args, reads=(), writes=(), **kw):
        if not self.enabled:
            return None
        return self._record(eng, "op", meth, args, kw, reads, writes, self._est(eng, meth, args))

    def dma(self, q, out, in_, reads=(), writes=()):
        if not self.enabled:
            return None
        nbytes = _fsize(out) * int(tuple(out.shape)[0]) * 4
        return self._record(q, "dma", None, (out, in_), {}, reads, writes, 2.0 + nbytes / 60000.0)

    def _schedule(self):
        nodes = self.nodes
        n = len(nodes)
        order = {e: [] for e in self.ENGS}
        if not self.reorder:
            for i, nd in enumerate(nodes):
                order[nd["eng"]].append(i)
            return order
        succ = [[] for _ in range(n)]
        npend = [0] * n
        for i, nd in enumerate(nodes):
            npend[i] = len(nd["deps"])
            for d in nd["deps"]:
                succ[d].append(i)
        finish = [0.0] * n
        cp = [0.0] * n
        if self.PRIO == "cp":
            for i in range(n - 1, -1, -1):
                m = 0.0
                for s_ in succ[i]:
                    if cp[s_] > m:
                        m = cp[s_]
                cp[i] = m + nodes[i]["dur"] + self.LAT
        heaps = {e: [] for e in self.ENGS}
        t_e = {e: 0.0 for e in self.ENGS}

        def push(i):
            nd = nodes[i]
            rdy = 0.0
            for d in nd["deps"]:
                f = finish[d] + (0.0 if nodes[d]["eng"] == nd["eng"] == "pe" else self.LAT)
                if f > rdy:
                    rdy = f
            heapq.heappush(heaps[nd["eng"]], (rdy, i))
        for i in range(n):
            if npend[i] == 0:
                push(i)
        done = 0
        while done < n:
            best = None
            for e in self.ENGS:
                h = heaps[e]
                if not h:
                    continue
                rdy, i = h[0]
                start = max(t_e[e], rdy)
                key = (start, i)
                if best is None or key < best[0]:
                    best = (key, e)
            (start, i), e = best
            h = heaps[e]
            cands = []
            while h and max(t_e[e], h[0][0]) <= start + 1e-9:
                cands.append(heapq.heappop(h))
            if self.PRIO == "cp":
                cands.sort(key=lambda x: (-cp[x[1]], x[1]))
            else:
                cands.sort(key=lambda x: x[1])
            rdy, i = cands[0]
            for c in cands[1:]:
                heapq.heappush(h, c)
            nd = nodes[i]
            if nd["kind"] == "dma":
                t_e[e] = start + 0.06
                finish[i] = start + nd["dur"]
            else:
                t_e[e] = start + nd["dur"]
                finish[i] = t_e[e]
            order[e].append(i)
            done += 1
            for s in succ[i]:
                npend[s] -= 1
                if npend[s] == 0:
                    push(s)
        self.est_makespan = max(t_e.values())
        return order

    def emit(self, stack):
        nc = self.nc
        nodes = self.nodes
        order = self._schedule()
        event = [None] * len(nodes)
        for e in self.CENG:
            cnt = 0
            for i in order[e]:
                if nodes[i]["kind"] == "op":
                    cnt += 1
                    event[i] = (e, cnt)
        ndma = 0
        slot_of = {}
        for e in self.ENGS:
            for i in order[e]:
                if nodes[i]["kind"] == "dma":
                    assert e == "sp"
                    slot = ndma % self.NDMA
                    event[i] = (("d", slot), 16 * (ndma // self.NDMA + 1))
                    slot_of[i] = (slot, 16 * (ndma // self.NDMA))
                    ndma += 1
        streams = {e: [] for e in self.ENGS}
        for e in self.ENGS:
            known = {}
            for i in order[e]:
                nd = nodes[i]
                need = {}
                if nd["kind"] == "dma":
                    slot, prev = slot_of[i]
                    if prev > 0:
                        need[("d", slot)] = prev
                for d in nd["deps"]:
                    k, v = event[d]
                    if k == e == "pe":
                        continue
                    if need.get(k, 0) < v:
                        need[k] = v
                for k, v in need.items():
                    if known.get(k, 0) >= v:
                        continue
                    known[k] = v
                    streams[e].append(("wait", k, v))
                if nd["kind"] == "dma":
                    streams[e].append(("dma", nd["args"][0], nd["args"][1], slot_of[i][0]))
                else:
                    streams[e].append(("op", nd["meth"], nd["args"], nd["kw"]))
            if e == "sp":
                last = {}
                for i in order[e]:
                    if nodes[i]["kind"] == "dma":
                        k, v = event[i]
                        last[k] = max(last.get(k, 0), v)
                for k, v in last.items():
                    if known.get(k, 0) < v:
                        streams[e].append(("wait", k, v))

        sems = {}
        for e in self.CENG:
            sems[e] = stack.enter_context(nc.semaphore("c_" + e))
        for s in range(self.NDMA):
            sems[("d", s)] = stack.enter_context(nc.semaphore("d%d" % s))
        block = stack.enter_context(nc.Block())

        def run(engname, eng):
            for item in streams[engname]:
                if item[0] == "wait":
                    eng.wait_ge(sems[item[1]], item[2])
                elif item[0] == "op":
                    getattr(eng, item[1])(*item[2], **item[3]).then_inc(sems[engname], 1)
                else:
                    _, out, in_, slot = item
                    eng.dma_start(out=out, in_=in_).then_inc(sems[("d", slot)], 16)

        @block.tensor
        def _(e):
            run("pe", e)

        @block.scalar
        def _(e):
            run("act", e)

        @block.vector
        def _(e):
            run("dve", e)

        @block.gpsimd
        def _(e):
            run("pool", e)

        @block.sync
        def _(e):
            run("sp", e)


NORM_EPS = 1e-6
GN_EPS = 64e-5
NSTAGE = 2
NHT = 1
INV_ENG = "dve"
NET = 3


class TL:
    def __init__(self, h, name):
        self.h = h
        self.b = Buf(name)

    def __getitem__(self, k):
        return self.h[k]


def build_phase_a(S_len):
    NB = S_len // 512
    NKB = S_len // 128
    nc = bass.Bass("TRN2", target_bir_lowering=False)
    x_d = nc.dram_tensor("x", [S_len, 1024], F32, kind="ExternalInput").ap()
    wc_d = nc.dram_tensor("wc", [1024, 1152], F32, kind="ExternalInput").ap()
    adaw_d = nc.dram_tensor("adaw", [1024, 2048], F32, kind="ExternalInput").ap()
    pv_d = nc.dram_tensor("pv", [128, 48], F32, kind="ExternalInput").ap()
    w2a2_d = nc.dram_tensor("w2a2", [128, 128], F32, kind="ExternalInput").ap()
    bc_d = nc.dram_tensor("bc", [128, 3, 128], F32, kind="ExternalInput").ap()
    y_d = nc.dram_tensor("y", [S_len, 256], F32, kind="ExternalOutput").ap()
    g_d = nc.dram_tensor("gs", [S_len, 128], F32, kind="ExternalOutput").ap()

    S = Sched(nc)
    st = ExitStack()
    with st:
        def sb(name, shape, dt=F32):
            return TL(st.enter_context(nc.sbuf_tensor("s_" + name, shape, dt)), name)

        def ps(name, shape, dt=F32):
            return TL(st.enter_context(nc.psum_tensor("p_" + name, shape, dt)), name)

        def rd(*ts):
            return [t.b for t in ts]

        bank = [ps("bank%d" % i, [128, 512], F32) for i in range(8)]

        def bview(i, shape_str=None, dt=None, **kw):
            ap = bank[i][:]
            if dt is not None:
                ap = ap.bitcast(dt)
            if shape_str:
                ap = ap.rearrange(shape_str, **kw)
            return ap

        pv = sb("pv", [128, 48])
        w2a2 = sb("w2a2", [128, 128])
        bct = sb("bct", [128, 3, 128])
        S.dma("sp", pv[:], pv_d[:, :], writes=rd(pv))
        S.dma("sp", w2a2[:], w2a2_d[:, :], writes=rd(w2a2))
        S.dma("sp", bct[:], bc_d[:, :, :], writes=rd(bct))

        identf = sb("identf", [128, 128])
        ident = sb("ident", [128, 128], BF16)
        S.op("pool", "memset", identf[:], 1.0, writes=rd(identf))
        S.op("pool", "affine_select", identf[:], identf[:], pattern=[[-1, 128]], compare_op=ALU.is_equal,
                                               fill=0.0, base=0, channel_multiplier=1, reads=rd(identf), writes=rd(identf))
        S.op("dve", "tensor_copy", ident[:], identf[:], reads=rd(identf), writes=rd(ident))

        def mkmask(name, pat, cm, op):
            m = sb(name, [128, 128])
            S.op("pool", "memset", m[:], 1.0, writes=rd(m))
            S.op("pool", "affine_select", m[:], m[:], pattern=[[pat, 128]], compare_op=op, fill=0.0,
                                                   base=0, channel_multiplier=cm, reads=rd(m), writes=rd(m))
            return m
        m_su = mkmask("m_su", 1, -1, ALU.is_gt)
        m_iu = mkmask("m_iu", 1, -1, ALU.is_ge)
        m_sl = mkmask("m_sl", -1, 1, ALU.is_gt)
        m_il = mkmask("m_il", -1, 1, ALU.is_ge)
        maskA = sb("maskA", [128, 4, 128])
        maskB = sb("maskB", [128, 6, 128])
        for i, m in enumerate([m_sl, m_su, m_sl, m_su]):
            S.op("pool", "tensor_copy", maskA[:, i, :], m[:], reads=rd(m), writes=rd(maskA))
        for i, m in enumerate([m_su, m_su, m_iu, m_iu, m_iu, m_iu]):
            S.op("pool", "tensor_copy", maskB[:, i, :], m[:], reads=rd(m), writes=rd(maskB))
        tri = sb("tri", [128, 128], BF16)
        S.op("dve", "tensor_copy", tri[:], m_il[:], reads=rd(m_il), writes=rd(tri))
        ones_bf = sb("ones_bf", [128, 64], BF16)
        S.op("pool", "memset", ones_bf[:], 1.0, writes=rd(ones_bf))
        blockones = sb("blockones", [128, 128])
        S.op("pool", "memset", blockones[:], 0.0, writes=rd(blockones))
        S.op("pool", "memset", blockones[0:64, 0:64], 1.0, writes=rd(blockones))
        S.op("pool", "memset", blockones[64:128, 64:128], 1.0, writes=rd(blockones))
        resetm = sb("resetm", [128, 512])
        S.op("pool", "memset", resetm[:], 1.0, writes=rd(resetm))
        S.op("pool", "memset", resetm[:].rearrange("p (c t) -> p c t", t=128)[:, :, 0:1], 0.0, writes=rd(resetm))

        cact = sb("cact", [128, 8])
        tmp8 = sb("tmp8", [128, 8])
        S.op("act", "activation", tmp8[:], pv[:, 0:8], AF.Exp, scale=-1.0, reads=rd(pv), writes=rd(tmp8))
        S.op("dve", "tensor_scalar", tmp8[:], tmp8[:], 1.0, None, ALU.add, reads=rd(tmp8), writes=rd(tmp8))
        S.op("dve", "reciprocal", tmp8[:], tmp8[:], reads=rd(tmp8), writes=rd(tmp8))
        S.op("dve", "tensor_tensor", cact[:], tmp8[:], pv[:, 0:8], ALU.mult, reads=rd(tmp8, pv), writes=rd(cact))

        stage = [sb("stage%d" % i, [128, 1152]) for i in range(NSTAGE)]
        stage = [stage[i % NSTAGE] for i in range(2)]
        modp = bank[0]
        for kc in range(8):
            for hf in range(2):
                stg = stage[hf]
                S.dma("sp", stg[:, 0:1024], adaw_d[kc * 128:(kc + 1) * 128, hf * 1024:(hf + 1) * 1024], writes=rd(stg))
                for jj in range(8):
                    j = hf * 8 + jj
                    S.op("pe", "matmul",
                        modp[:, kc * 16 + j: kc * 16 + j + 1], stg[:, jj * 128:(jj + 1) * 128], cact[:, kc:kc + 1],
                        start=True, stop=True, reads=rd(stg, cact), writes=rd(modp))
        modT = sb("modT", [128, 16])
        S.op("dve", "tensor_reduce", modT[:], modp[:, 0:128].rearrange("p (k j) -> p j k", j=16),
                                              axis=AX.X, op=ALU.add, reads=rd(modp), writes=rd(modT))
        Bco = sb("Bco", [128, 8])
        Aco = sb("Aco", [128, 8])
        S.op("dve", "tensor_tensor", Bco[:], modT[:, 0:8], pv[:, 16:24], ALU.add, reads=rd(modT, pv), writes=rd(Bco))
        S.op("dve", "tensor_tensor", Aco[:], modT[:, 8:16], pv[:, 24:32], ALU.add, reads=rd(modT, pv), writes=rd(Aco))
        S.op("dve", "scalar_tensor_tensor", Aco[:], Aco[:], 1.0, pv[:, 8:16], ALU.add, ALU.mult,
             reads=rd(Aco, pv), writes=rd(Aco))
        der = sb("der", [128, 4])
        S.op("dve", "tensor_scalar", der[:, 0:2], pv[:, 36:38], -1.0, None, ALU.mult, reads=rd(pv), writes=rd(der))
        S.op("dve", "tensor_scalar", der[:, 2:3], pv[:, 39:40], -1.0, 1.0, ALU.mult, ALU.add, reads=rd(pv), writes=rd(der))
        MU = lambda g: pv[:, 32 + g:33 + g]
        NEGW0, NEGA0, OMKA = der[:, 0:1], der[:, 1:2], der[:, 2:3]
        KK, KA, RK = pv[:, 38:39], pv[:, 39:40], pv[:, 40:41]

        W = sb("W", [128, 8, 1152], BF16)
        for kc in range(8):
            stg = stage[kc % 2]
            S.dma("sp", stg[:, 0:1152], wc_d[kc * 128:(kc + 1) * 128, :], writes=rd(stg))
            eng = "dve" if kc % 2 == 0 else "act"
            if eng == "dve":
                S.op("dve", "tensor_copy", W[:, kc, :], stg[:, 0:1152], reads=rd(stg), writes=rd(W))
            else:
                S.op("act", "activation", W[:, kc, :], stg[:, 0:1152], AF.Copy, reads=rd(stg), writes=rd(W))

        kT_all = sb("kT_all", [128, S_len], BF16)
        v_all = sb("v_all", [128, NKB, 128], BF16)
        raw = [sb("raw%d" % g, [128, 513]) for g in range(4)]
        for g in range(4):
            S.op("pool", "memset", raw[g][:, 0:1], 0.0, writes=rd(raw[g]))
        Tst = sb("Tst", [128, 64])
        Tb = sb("Tb", [128, 64], BF16)
        S.op("pool", "memset", Tst[:], 0.0, writes=rd(Tst))
        S.op("pool", "memset", Tb[:], 0.0, writes=rd(Tb))

        xt = [sb("xt%d" % i, [128, 1024]) for i in range(2)]
        junk = sb("junk", [128, 1024], BF16)
        ss = sb("ss", [128, 4])
        rstd = sb("rstd", [128, 4])
        xn = [sb("xn%d" % i, [128, 1024], BF16) for i in range(2)]
        tmpf = sb("tmpf", [128, 8, 128])
        hT = [sb("hT%d" % i, [128, 8, 512], BF16) for i in range(NHT)]
        sh = [sb("sh%d" % g, [128, 512]) for g in range(4)]
        dif = sb("dif", [128, 512])
        gate = sb("gate", [128, 4, 256])
        gtmp = sb("gtmp", [128, 256])
        th = dif
        e1 = sb("e1", [128, 512])
        ew = sb("ew", [128, 512])
        alpha = sb("alpha", [128, 512])
        kks = sb("kks", [128, 512])
        sq = e1
        rn = dif
        kkn = sb("kkn", [128, 512])
        bv = sb("bv", [128, 512])
        kmod = sb("kmod", [128, 512])
        cwn = sb("cwn", [128, 512])
        gam = sb("gam", [128, 512])
        ginv = sb("ginv", [128, 512])
        gprev = e1
        gco = sb("gco", [128, 512])
        ncC = sb("ncC", [128, 4])
        RtTh = [sb("RtTh%d" % h, [128, 512], BF16) for h in range(2)]
        AtTh = [sb("AtTh%d" % h, [128, 512], BF16) for h in range(2)]
        rkTh = [sb("rkTh%d" % h, [128, 512], BF16) for h in range(2)]
        qTh2 = [[sb("qTh%d_%d" % (i, h), [128, 512], BF16) for h in range(2)] for i in range(2)]
        kTbuf = [Buf("kT%d" % i) for i in range(NB)]
        vbuf = [Buf("v%d" % i) for i in range(NB)]
        for _t in RtTh + AtTh + rkTh + qTh2[0] + qTh2[1]:
            S.op("pool", "memset", _t[:], 0.0, writes=rd(_t))
        KtT = sb("KtT", [128, 512], BF16)
        BtT = sb("BtT", [128, 512], BF16)
        KcT = sb("KcT", [128, 512], BF16)
        BcT = sb("BcT", [128, 512], BF16)
        VT = sb("VT", [128, 512], BF16)
        tokm = [sb("tokm%d" % c, [128, 3, 128], BF16) for c in range(4)]
        Lab = [[sb("Lab%d_%d" % (c, i), [128, 4, 128], BF16) for i in range(2)] for c in range(4)]
        Mx = [sb("Mx%d" % c, [128, 6, 128], BF16) for c in range(4)]
        MT = [[sb("MT%d_%d" % (c, i), [128, 2, 128], BF16) for i in range(2)] for c in range(4)]
        Xb = sb("Xb", [128, 128], BF16)
        Ub = sb("Ub", [128, 128], BF16)
        yr = sb("yr", [128, 4, 128])
        yb = sb("yb", [128, 4, 128])
        st1 = sb("st1", [128, 8])
        st2 = sb("st2", [128, 8])
        st3 = sb("st3", [128, 8])
        ystage = [sb("ystage%d" % i, [128, 4, 128]) for i in range(1)]
        NSB = 2
        e_t = [sb("e_t%d" % i, [128, 512]) for i in range(NET)]
        sp_t = [sb("sp_t%d" % i, [128, 512], BF16) for i in range(NET)]
        E_t = [sb("E_t%d" % i, [128, 512], BF16) for i in range(2)]
        sc_t = [sb("sc_t%d" % i, [128, 4]) for i in range(NSB)]
        oacc = [sb("oacc%d" % h, [128, 4, 64]) for h in range(2)]
        osq = sb("osq", [128, 4, 64])
        ost = sb("ost", [128, 4])

        B_T = 0
        B_IP = [1, 2]
        B_SQ = [1, 2]
        B_MM = 0
        ZC = [5, 6, 7]
        PP = [3, 4]

        ipn = [0]

        def ipbank():
            b = B_IP[ipn[0] % 2]
            ipn[0] += 1
            return bank[b]

        def stageA(tb):
            hTb = hT[tb % NHT]
            yst = ystage[0]
            qTh = qTh2[tb % 2]
            xts = [xt[0], xt[1], xt[0], xt[1]]

            def xload(ts):
                S.dma("sp", xts[ts][:], x_d[tb * 512 + ts * 128: tb * 512 + (ts + 1) * 128, :], writes=rd(xts[ts]))
            xload(0)
            xload(1)
            for ts in range(4):
                S.op("act", "activation", junk[:], xts[ts][:], AF.Square, scale=1.0 / 32.0,
                                                          accum_out=ss[:, ts:ts + 1], reads=rd(xts[ts]), writes=rd(junk, ss))
                S.op("act", "activation", rstd[:, ts:ts + 1], ss[:, ts:ts + 1], AF.Ln, bias=NORM_EPS, reads=rd(ss), writes=rd(rstd))
                S.op("act", "activation", rstd[:, ts:ts + 1], rstd[:, ts:ts + 1], AF.Exp, scale=-0.5, reads=rd(rstd), writes=rd(rstd))
                xnb = xn[ts % 2]
                S.op("dve", "tensor_scalar", xnb[:], xts[ts][:], rstd[:, ts:ts + 1], None, ALU.mult,
                     reads=rd(xts[ts], rstd), writes=rd(xnb))
                pT = bank[B_T]
                pTv = bview(B_T, "p (k t) -> p k t", dt=BF16, t=128)
                for kc in range(8):
                    S.op("pe", "transpose", pTv[:, kc, :], xnb[:, kc * 128:(kc + 1) * 128], ident[:],
                         reads=rd(xnb, ident), writes=rd(pT))
                S.op("dve", "tensor_tensor", tmpf[:], pTv[:, 0:8, :], Aco[:].unsqueeze(2).to_broadcast([128, 8, 128]), ALU.mult,
                     reads=rd(pT, Aco), writes=rd(tmpf))
                S.op("pool", "tensor_tensor", hTb[:, :, ts * 128:(ts + 1) * 128], tmpf[:],
                                                               Bco[:].unsqueeze(2).to_broadcast([128, 8, 128]), ALU.add,
                     reads=rd(tmpf, Bco), writes=rd(hTb))
                if ts + 2 < 4:
                    xload(ts + 2)
                yield
            for g in range(6):
                pb = ipbank()
                for kc in range(8):
                    S.op("pe", "matmul", pb[:], W[:, kc, g * 128:(g + 1) * 128], hTb[:, kc, :],
                                                                     start=(kc == 0), stop=(kc == 7),
                         reads=rd(W, hTb), writes=rd(pb))
                if g < 4:
                    S.op("act", "activation", raw[g][:, 1:513], pb[:], AF.Copy, reads=rd(pb), writes=rd(raw[g]))
                    S.op("dve", "tensor_tensor", dif[:], raw[g][:, 0:512], raw[g][:, 1:513], ALU.subtract,
                         reads=rd(raw[g]), writes=rd(dif))
                    S.op("dve", "scalar_tensor_tensor", sh[g][:], dif[:], MU(g), raw[g][:, 1:513], ALU.mult, ALU.add,
                         reads=rd(dif, raw[g], pv), writes=rd(sh[g]))
                    S.op("pool", "tensor_copy", raw[g][:, 0:1], raw[g][:, 512:513], reads=rd(raw[g]), writes=rd(raw[g]))
                elif g == 4:
                    for h in range(2):
                        hs = slice(64 * h, 64 * h + 64)
                        S.op("act", "activation", qTh[h][hs, :], pb[hs, :], AF.Copy, reads=rd(pb), writes=rd(qTh[h]))
                else:
                    S.op("act", "activation", kT_all[:, tb * 512:(tb + 1) * 512], pb[:], AF.Copy,
                         reads=rd(pb), writes=[kTbuf[tb]])
                yield
            for ts in range(4):
                pb = ipbank()
                for kc in range(8):
                    S.op("pe", "matmul", pb[:, 0:384], hTb[:, kc, ts * 128:(ts + 1) * 128],
                                                                       W[:, kc, 768:1152], start=(kc == 0), stop=(kc == 7),
                         reads=rd(W, hTb), writes=rd(pb))
                S.op("act", "activation", v_all[:, tb * 4 + ts, :], pb[:, 0:128], AF.Copy,
                     reads=rd(pb), writes=[vbuf[tb]])
                S.op("act", "activation", gtmp[:], pb[:, 128:384], AF.Exp, scale=-1.0, reads=rd(pb), writes=rd(gtmp))
                S.op("dve", "tensor_scalar", gtmp[:], gtmp[:], 1.0, None, ALU.add, reads=rd(gtmp), writes=rd(gtmp))
                S.op("dve", "reciprocal", gtmp[:], gtmp[:], reads=rd(gtmp), writes=rd(gtmp))
                S.op("dve", "tensor_tensor", gate[:, ts, :], gtmp[:], pb[:, 128:384], ALU.mult,
                     reads=rd(gtmp, pb), writes=rd(gate))
                yield

            shr, shk, shv, shz = sh
            S.op("act", "activation", th[0:64, :], shz[0:64, :], AF.Exp, scale=2.0, reads=rd(shz), writes=rd(th))
            S.op("dve", "tensor_scalar", th[0:64, :], th[0:64, :], 1.0, None, ALU.add, reads=rd(th), writes=rd(th))
            S.op("dve", "reciprocal", th[0:64, :], th[0:64, :], reads=rd(th), writes=rd(th))
            S.op("dve", "tensor_scalar", th[0:64, :], th[0:64, :], -2.0, 1.0, ALU.mult, ALU.add, reads=rd(th), writes=rd(th))
            pdl = ipbank()
            S.op("pe", "matmul", pdl[:], w2a2[0:64, :], th[0:64, :], start=True, stop=True, reads=rd(w2a2, th), writes=rd(pdl))
            S.op("act", "activation", e1[:], pdl[:], AF.Exp, scale=-1.0, bias=NEGW0, reads=rd(pdl, der), writes=rd(e1))
            S.op("act", "activation", e1[:], e1[:], AF.Ln, bias=1.0, reads=rd(e1), writes=rd(e1))
            S.op("act", "activation", ew[:], e1[:], AF.Exp, scale=-1.0, bias=-0.5, reads=rd(e1), writes=rd(ew))
            yield
            pda = ipbank()
            S.op("pe", "matmul", pda[:], w2a2[64:128, :], shz[64:128, :], start=True, stop=True, reads=rd(w2a2, shz), writes=rd(pda))
            S.op("act", "activation", alpha[:], pda[:], AF.Exp, scale=-1.0, bias=NEGA0, reads=rd(pda, der), writes=rd(alpha))
            S.op("dve", "tensor_scalar", alpha[:], alpha[:], 1.0, None, ALU.add, reads=rd(alpha), writes=rd(alpha))
            S.op("dve", "reciprocal", alpha[:], alpha[:], reads=rd(alpha), writes=rd(alpha))
            yield
            S.op("dve", "tensor_scalar", kks[:], shk[:], KK, None, ALU.mult, reads=rd(shk, pv), writes=rd(kks))
            S.op("pool", "tensor_tensor", sq[:], kks[:], kks[:], ALU.mult, reads=rd(kks), writes=rd(sq))
            pss = ipbank()
            S.op("pe", "matmul", pss[:], blockones[:], sq[:], start=True, stop=True, reads=rd(blockones, sq), writes=rd(pss))
            yield
            S.op("dve", "tensor_scalar", rn[:], pss[:], 1e-24, None, ALU.max, reads=rd(pss), writes=rd(rn))
            S.op("act", "activation", rn[:], rn[:], AF.Ln, reads=rd(rn), writes=rd(rn))
            S.op("act", "activation", rn[:], rn[:], AF.Exp, scale=-0.5, reads=rd(rn), writes=rd(rn))
            S.op("dve", "tensor_tensor", kkn[:], kks[:], rn[:], ALU.mult, reads=rd(kks, rn), writes=rd(kkn))
            S.op("pool", "tensor_tensor", bv[:], kkn[:], alpha[:], ALU.mult, reads=rd(kkn, alpha), writes=rd(bv))
            S.op("dve", "tensor_scalar", kmod[:], alpha[:], KA, OMKA, ALU.mult, ALU.add, reads=rd(alpha, pv, der), writes=rd(kmod))
            S.op("dve", "tensor_tensor", kmod[:], kmod[:], shk[:], ALU.mult, reads=rd(kmod, shk), writes=rd(kmod))
            yield
            S.op("dve", "tensor_tensor_scan", cwn[:], resetm[:], ew[:], 0.0, ALU.mult, ALU.add, reads=rd(resetm, ew), writes=rd(cwn))
            S.op("act", "activation", gam[:], cwn[:], AF.Exp, scale=-1.0, reads=rd(cwn), writes=rd(gam))
            S.op("act", "activation", ginv[:], cwn[:], AF.Exp, reads=rd(cwn), writes=rd(ginv))
            S.op("pool", "tensor_tensor", gprev[:], cwn[:], ew[:], ALU.subtract, reads=rd(cwn, ew), writes=rd(gprev))
            S.op("act", "activation", gprev[:], gprev[:], AF.Exp, scale=-1.0, reads=rd(gprev), writes=rd(gprev))
            yield
            cwn3 = cwn[:].rearrange("p (c t) -> p c t", t=128)
            S.op("dve", "tensor_scalar", ncC[:], cwn3[:, :, 127], -1.0, None, ALU.mult, reads=rd(cwn), writes=rd(ncC))
            for c in range(4):
                S.op("act", "activation", gco[:, c * 128:(c + 1) * 128], cwn[:, c * 128:(c + 1) * 128], AF.Exp,
                                                        bias=ncC[:, c:c + 1], reads=rd(cwn, ncC), writes=rd(gco))
            for h in range(2):
                hs = slice(64 * h, 64 * h + 64)
                S.op("dve", "tensor_tensor", RtTh[h][hs, :], shr[hs, :], gam[hs, :], ALU.mult, reads=rd(shr, gam), writes=rd(RtTh[h]))
                S.op("dve", "scalar_tensor_tensor", AtTh[h][hs, :], kkn[hs, :], -1.0, gprev[hs, :], ALU.mult, ALU.mult, reads=rd(kkn, gprev), writes=rd(AtTh[h]))
            S.op("dve", "tensor_tensor", KtT[:], kmod[:], ginv[:], ALU.mult, reads=rd(kmod, ginv), writes=rd(KtT))
            S.op("pool", "tensor_tensor", BtT[:], bv[:], ginv[:], ALU.mult, reads=rd(bv, ginv), writes=rd(BtT))
            S.op("dve", "tensor_tensor", KcT[:], kmod[:], gco[:], ALU.mult, reads=rd(kmod, gco), writes=rd(KcT))
            S.op("pool", "tensor_tensor", BcT[:], bv[:], gco[:], ALU.mult, reads=rd(bv, gco), writes=rd(BcT))
            S.op("act", "activation", VT[:], shv[:], AF.Copy, reads=rd(shv), writes=rd(VT))
            yield
            for h in range(2):
                hs = slice(64 * h, 64 * h + 64)
                S.op("dve", "scalar_tensor_tensor", rkTh[h][hs, :], shr[hs, :], pv[hs, 40:41], kmod[hs, :], ALU.mult, ALU.mult, reads=rd(shr, kmod, pv), writes=rd(rkTh[h]))
            for c in range(4):
                cs = slice(c * 128, (c + 1) * 128)
                pT = bank[B_T]
                pTv = bview(B_T, "p (k t) -> p k t", dt=BF16, t=128)
                for i, src in enumerate([KcT, BcT, VT]):
                    S.op("pe", "transpose", pTv[:, i, :], src[:, cs], ident[:], reads=rd(src, ident), writes=rd(pT))
                S.op("act", "activation", tokm[c][:], pTv[:, 0:3, :], AF.Copy, reads=rd(pT), writes=rd(tokm[c]))
                yield
            pbn = ipbank()
            for c in range(4):
                cs = slice(c * 128, (c + 1) * 128)
                for h in range(2):
                    hs = slice(64 * h, 64 * h + 64)
                    S.op("pe", "matmul", pbn[:, c * 128 + 64 * h: c * 128 + 64 * h + 64], rkTh[h][:, cs], ones_bf[:, :],
                                                                           start=True, stop=True, reads=rd(rkTh[h], ones_bf), writes=rd(pbn))
            for c in range(4):
                S.op("dve", "tensor_tensor", yb[:, c, :], pbn[:, c * 128:(c + 1) * 128], tokm[c][:, 2, :], ALU.mult,
                     reads=rd(pbn, tokm[c]), writes=rd(yb))
                yield

            for c in range(4):
                cs = slice(c * 128, (c + 1) * 128)
                pa = bank[B_SQ[c % 2]]
                pav = bview(B_SQ[c % 2], "p (k t) -> p k t", t=128)
                for h in range(2):
                    hs = slice(64 * h, 64 * h + 64)
                    S.op("pe", "matmul", pav[:, 2 * h, :], AtTh[h][:, cs], BtT[:, cs], start=True, stop=True,
                         reads=rd(AtTh[h], BtT), writes=rd(pa))
                    S.op("pe", "matmul", pav[:, 2 * h + 1, :], BtT[:, cs], AtTh[h][:, cs], start=True, stop=True,
                         reads=rd(AtTh[h], BtT), writes=rd(pa))
                L0 = Lab[c][0]
                S.op("dve", "tensor_tensor", L0[:], pav[:, :, :], maskA[:], ALU.mult, reads=rd(pa, maskA), writes=rd(L0))
                yield
                pb_ = ipbank()
                pbv = pb_[:].rearrange("p (k t) -> p k t", t=128)
                combos = [(KtT, AtTh[0]), (KtT, AtTh[1]), (BtT, RtTh[0]), (BtT, RtTh[1])]
                for i in range(4):
                    h = i % 2
                    hs = slice(64 * h, 64 * h + 64)
                    l, r = combos[i]
                    S.op("pe", "matmul", pbv[:, i, :], l[:, cs], r[:, cs], start=True, stop=True,
                         reads=rd(l, r), writes=rd(pb_))
                S.op("dve", "tensor_tensor", Mx[c][:, 0:4, :], pbv[:, :, :], maskB[:, 0:4, :], ALU.mult,
                     reads=rd(pb_, maskB), writes=rd(Mx[c]))
                pc_ = ipbank()
                pcv = pc_[:].rearrange("p (k t) -> p k t", t=128)
                for h in range(2):
                    hs = slice(64 * h, 64 * h + 64)
                    S.op("pe", "matmul", pcv[:, h, :], KtT[:, cs], RtTh[h][:, cs], start=True, stop=True,
                         reads=rd(KtT, RtTh[h]), writes=rd(pc_))
                S.op("dve", "tensor_tensor", Mx[c][:, 4:6, :], pcv[:, 0:2, :], maskB[:, 4:6, :], ALU.mult,
                     reads=rd(pc_, maskB), writes=rd(Mx[c]))
                yield
                M0 = MT[c][0]
                for h in range(2):
                    S.op("pool", "tensor_tensor", M0[:, h, :], L0[:, 2 * h + 1, :], ident[:], ALU.add,
                         reads=rd(L0, ident), writes=rd(M0))
            for k in range(1, 7):
                for c in range(4):
                    Lp = Lab[c][(k - 1) % 2]
                    Ln_ = Lab[c][k % 2]
                    Mp = MT[c][(k - 1) % 2]
                    Mn = MT[c][k % 2]
                    pa = bank[B_SQ[c % 2]]
                    pav = bview(B_SQ[c % 2], "p (k t) -> p k t", t=128)
                    for h in range(2):
                        S.op("pe", "matmul", pav[:, 2 * h, :], Lp[:, 2 * h + 1, :], Lp[:, 2 * h, :], start=True, stop=True,
                             reads=rd(Lp), writes=rd(pa))
                        S.op("pe", "matmul", pav[:, 2 * h + 1, :], Lp[:, 2 * h, :], Lp[:, 2 * h + 1, :], start=True, stop=True,
                             reads=rd(Lp), writes=rd(pa))
                    if INV_ENG == "act":
                        S.op("act", "activation", Ln_[:], pav[:, :, :], AF.Copy, reads=rd(pa), writes=rd(Ln_))
                    else:
                        S.op("dve", "tensor_copy", Ln_[:], pav[:, :, :], reads=rd(pa), writes=rd(Ln_))
                    yield
                    pm = bank[B_MM]
                    pmv = bview(B_MM, "p (k t) -> p k t", t=128)
                    off = 2 * (c % 2)
                    for h in range(2):
                        S.op("pe", "matmul", pmv[:, off + h, :], Ln_[:, 2 * h, :], Mp[:, h, :], start=True, stop=True,
                             reads=rd(Ln_, Mp), writes=rd(pm))
                    S.op("dve", "tensor_tensor", Mn[:], pmv[:, off:off + 2, :], Mp[:], ALU.add,
                         reads=rd(pm, Mp), writes=rd(Mn))
                yield
            MTf = [MT[c][0] for c in range(4)]

            gam3 = gam[:].rearrange("p (c t) -> p c t", t=128)
            for c in range(4):
                cs = slice(c * 128, (c + 1) * 128)
                pm = bank[B_MM]
                Vt = tokm[c]
                for h in range(2):
                    hs = slice(64 * h, 64 * h + 64)
                    S.op("pe", "matmul", pm[:, hs], AtTh[h][:, cs], Tb[:, :], start=True, stop=False,
                         reads=rd(AtTh[h], Tb), writes=rd(pm))
                    S.op("pe", "matmul", pm[:, hs], Mx[c][:, h, :], Vt[:, 2, hs], start=False, stop=True,
                         reads=rd(Mx[c], Vt), writes=rd(pm))
                S.op("dve", "tensor_copy", Xb[:], pm[:, 0:128], reads=rd(pm), writes=rd(Xb))
                yield
                for h in range(2):
                    hs = slice(64 * h, 64 * h + 64)
                    S.op("pe", "matmul", pm[:, 128 + 64 * h:128 + 64 * h + 64], MTf[c][:, h, :], Xb[:, hs], start=True, stop=True,
                         reads=rd(MTf[c], Xb), writes=rd(pm))
                S.op("act", "activation", Ub[:], pm[:, 128:256], AF.Copy, reads=rd(pm), writes=rd(Ub))
                yield
                for h in range(2):
                    hs = slice(64 * h, 64 * h + 64)
                    ys = slice(256 + 64 * h, 256 + 64 * h + 64)
                    S.op("pe", "matmul", pm[:, ys], RtTh[h][:, cs], Tb[:, :], start=True, stop=False,
                         reads=rd(RtTh[h], Tb), writes=rd(pm))
                    S.op("pe", "matmul", pm[:, ys], Mx[c][:, 2 + h, :], Ub[:, hs], start=False, stop=False,
                         reads=rd(Mx[c], Ub), writes=rd(pm))
                    S.op("pe", "matmul", pm[:, ys], Mx[c][:, 4 + h, :], Vt[:, 2, hs], start=False, stop=True,
                         reads=rd(Mx[c], Vt), writes=rd(pm))
                S.op("act", "activation", yr[:, c, :], pm[:, 256:384], AF.Copy, reads=rd(pm), writes=rd(yr))
                for h in range(2):
                    hs = slice(64 * h, 64 * h + 64)
                    S.op("pe", "matmul", pm[hs, 384:448], Vt[:, 1, hs], Ub[:, hs], start=True, stop=False,
                         reads=rd(Vt, Ub), writes=rd(pm))
                    S.op("pe", "matmul", pm[hs, 384:448], Vt[:, 0, hs], Vt[:, 2, hs], start=False, stop=True,
                         reads=rd(Vt), writes=rd(pm))
                S.op("dve", "scalar_tensor_tensor", Tst[:], Tst[:], gam3[:, c, 127:128], pm[:, 384:448], ALU.mult, ALU.add,
                     reads=rd(Tst, gam, pm), writes=rd(Tst))
                S.op("act", "activation", Tb[:], Tst[:], AF.Copy, reads=rd(Tst), writes=rd(Tb))
                yield

            yr8 = yr[:].rearrange("p c (h i) -> p (c h) i", i=64)
            ysq = gco
            ysq8 = ysq[:].rearrange("p (c h i) -> p (c h) i", h=2, i=64)
            S.op("dve", "tensor_reduce", st1[:], yr8, axis=AX.X, op=ALU.add, reads=rd(yr), writes=rd(st1))
            S.op("pool", "tensor_tensor", ysq[:], yr[:].rearrange("p c x -> p (c x)"), yr[:].rearrange("p c x -> p (c x)"), ALU.mult, reads=rd(yr), writes=rd(ysq))
            S.op("dve", "tensor_reduce", st2[:], ysq8, axis=AX.X, op=ALU.add, reads=rd(ysq), writes=rd(st2))
            S.op("dve", "tensor_scalar", st1[:], st1[:], 1.0 / 64, None, ALU.mult, reads=rd(st1), writes=rd(st1))
            S.op("dve", "tensor_tensor", st3[:], st1[:], st1[:], ALU.mult, reads=rd(st1), writes=rd(st3))
            S.op("dve", "scalar_tensor_tensor", st2[:], st2[:], 1.0 / 64, st3[:], ALU.mult, ALU.subtract, reads=rd(st2, st3), writes=rd(st2))
            S.op("act", "activation", st2[:], st2[:], AF.Ln, bias=GN_EPS, reads=rd(st2), writes=rd(st2))
            S.op("act", "activation", st2[:], st2[:], AF.Exp, scale=-0.5, reads=rd(st2), writes=rd(st2))
            S.op("dve", "tensor_tensor", yr8, yr8, st1[:].unsqueeze(2).to_broadcast([128, 8, 64]), ALU.subtract, reads=rd(yr, st1), writes=rd(yr))
            S.op("dve", "tensor_tensor", yr8, yr8, st2[:].unsqueeze(2).to_broadcast([128, 8, 64]), ALU.mult, reads=rd(yr, st2), writes=rd(yr))
            for c in range(4):
                S.op("pool", "tensor_tensor", yr[:, c, :], yr[:, c, :], bct[:, 0, :], ALU.mult, reads=rd(yr, bct), writes=rd(yr))
                S.op("pool", "tensor_tensor", yr[:, c, :], yr[:, c, :], bct[:, 1, :], ALU.add, reads=rd(yr, bct), writes=rd(yr))
            S.op("dve", "tensor_tensor", yr[:], yr[:], yb[:], ALU.add, reads=rd(yr, yb), writes=rd(yr))
            S.op("dve", "tensor_tensor", yst[:, :, 0:128], yr[:], gate[:, :, 0:128], ALU.mult, reads=rd(yr, gate), writes=rd(yst))

            S.dma("sp", y_d[tb * 512:(tb + 1) * 512, 0:128].rearrange("(ts p) c -> p ts c", p=128), yst[:], reads=rd(yst))
            S.dma("sp", g_d[tb * 512:(tb + 1) * 512, :].rearrange("(ts p) c -> p ts c", p=128), gate[:, :, 128:256], reads=rd(gate))
            yield

        sbn = [0]

        def stageS(tb):
            qTh = qTh2[tb % 2]
            tiles = [(h, kb) for h in range(2) for kb in range(4 * tb + 4)]
            n = len(tiles)
            g0 = sbn[0]
            sbn[0] += n

            def res(i):
                g = g0 + i
                h, kb = tiles[i]
                dk = max(0, kb - 4 * tb)
                return dict(h=h, kb=kb, dk=dk, q0=dk * 128, qsl=slice(dk * 128, 512), hs=slice(64 * h, 64 * h + 64),
                            zc=bank[ZC[g % len(ZC)]], pp=bank[PP[g % len(PP)]], et=e_t[g % NET], sa=sp_t[g % NET], Et=E_t[g % 2], sct=sc_t[g % 2],
                            oa=oacc[h])

            def front(i):
                r = res(i)
                S.op("pe", "matmul", r['zc'][:, r['qsl']], kT_all[:, r['kb'] * 128:(r['kb'] + 1) * 128], qTh[r['h']][:, r['qsl']],
                     start=True, stop=True, reads=[kTbuf[r['kb'] // 4]] + rd(qTh[r['h']]), writes=rd(r['zc']))

            def mid(i):
                r = res(i)
                qsl, et, sa, zc, pp, dk, q0 = r['qsl'], r['et'], r['sa'], r['zc'], r['pp'], r['dk'], r['q0']
                S.op("act", "activation", et[:, qsl], zc[:, qsl], AF.Exp, scale=0.125, reads=rd(zc), writes=rd(et))
                if r['kb'] >= 4 * tb:
                    S.op("pool", "affine_select", et[:, qsl], et[:, qsl], pattern=[[1, 512 - q0]], compare_op=ALU.is_gt, fill=0.0,
                         base=0, channel_multiplier=-1, reads=rd(et), writes=rd(et))
                S.op("act", "activation", sa[:, qsl], et[:, qsl], AF.Ln, bias=1.0, reads=rd(et), writes=rd(sa))
                S.op("pe", "matmul", zc[:, qsl], tri[:], sa[:, qsl], start=True, stop=True, reads=rd(tri, sa), writes=rd(zc))
                for qs in range(dk, 4):
                    S.op("pe", "matmul", pp[:, 256 + qs:257 + qs], sa[:, qs * 128:(qs + 1) * 128], ones_bf[:, 0:1],
                         start=True, stop=True, reads=rd(sa, ones_bf), writes=rd(pp))

            def back(i):
                r = res(i)
                qsl, et, sa, zc, pp, dk, Et, sct, oa, hs, kb, h = (r['qsl'], r['et'], r['sa'], r['zc'], r['pp'], r['dk'], r['Et'],
                                                                   r['sct'], r['oa'], r['hs'], r['kb'], r['h'])
                S.op("act", "activation", Et[:, qsl], zc[:, qsl], AF.Exp, scale=-1.0, reads=rd(zc), writes=rd(Et))
                if kb > 0:
                    S.op("act", "activation", sct[:, dk:4], pp[:, 256 + dk:260], AF.Exp, scale=-1.0, reads=rd(pp), writes=rd(sct))
                S.op("dve", "tensor_tensor", sa[:, qsl], et[:, qsl], Et[:, qsl], ALU.mult, reads=rd(et, Et), writes=rd(sa))
                for qs in range(dk, 4):
                    S.op("pe", "matmul", pp[:, qs * 64:(qs + 1) * 64], sa[:, qs * 128:(qs + 1) * 128], v_all[:, kb, hs],
                         start=True, stop=True, reads=rd(sa) + [vbuf[kb // 4]], writes=rd(pp))
                ppv = pp[:, 0:256].rearrange("p (q d) -> p q d", d=64)
                if kb == 0:
                    S.op("dve", "tensor_copy", oa[:], ppv, reads=rd(pp), writes=rd(oa))
                else:
                    S.op("pool", "tensor_tensor", oa[:, dk:4, :], oa[:, dk:4, :],
                         sct[:, dk:4].unsqueeze(2).to_broadcast([128, 4 - dk, 64]), ALU.mult, reads=rd(oa, sct), writes=rd(oa))
                    S.op("dve", "tensor_tensor", oa[:, dk:4, :], oa[:, dk:4, :], ppv[:, dk:4, :], ALU.add, reads=rd(oa, pp), writes=rd(oa))
                if kb == 4 * tb + 3:
                    S.op("pool", "tensor_tensor", osq[:], oa[:], oa[:], ALU.mult, reads=rd(oa), writes=rd(osq))
                    S.op("dve", "tensor_reduce", ost[:], osq[:], axis=AX.X, op=ALU.add, reads=rd(osq), writes=rd(ost))
                    S.op("act", "activation", ost[:], ost[:], AF.Ln, scale=1.0 / 64, bias=NORM_EPS, reads=rd(ost), writes=rd(ost))
                    S.op("act", "activation", ost[:], ost[:], AF.Exp, scale=-0.5, reads=rd(ost), writes=rd(ost))
                    S.op("dve", "tensor_tensor", oa[:], oa[:], ost[:].unsqueeze(2).to_broadcast([128, 4, 64]), ALU.mult,
                         reads=rd(oa, ost), writes=rd(oa))
                    for qs in range(4):
                        S.op("pool", "tensor_tensor", oa[:, qs, :], oa[:, qs, :], bct[:, 2, hs], ALU.mult, reads=rd(oa, bct), writes=rd(oa))
                    S.dma("sp", y_d.rearrange("(q nb) c -> q nb c", nb=NKB)[:, 4 * tb:4 * tb + 4, 128 + 64 * h:128 + 64 * h + 64],
                          oa[:], reads=rd(oa))

            front(0)
            if n > 1:
                front(1)
            mid(0)
            for i in range(n):
                if i + 2 < n:
                    front(i + 2)
                if i + 1 < n:
                    mid(i + 1)
                back(i)
                yield

        def drive(gens):
            lists = []
            for g in gens:
                lists.append(g)
            alive = list(lists)
            while alive:
                for g in list(alive):
                    try:
                        next(g)
                    except StopIteration:
                        alive.remove(g)

        def count(genf, tb):
            en = S.enabled
            S.enabled = False
            saved = sbn[0]
            c = sum(1 for _ in genf(tb))
            sbn[0] = saved
            S.enabled = en
            return c

        def merged(gS, nS, gA, nA):
            iS = iA = 0
            while iS < nS or iA < nA:
                if iA < nA and (iS >= nS or iA * max(nS, 1) <= iS * nA):
                    next(gA, None)
                    iA += 1
                else:
                    next(gS, None)
                    iS += 1
            for _ in gS:
                pass
            for _ in gA:
                pass

        for _ in stageA(0):
            pass
        for tb in range(NB):
            nS = count(stageS, tb)
            if tb + 1 < NB:
                nA = count(stageA, tb + 1)
                merged(stageS(tb), nS, stageA(tb + 1), nA)
            else:
                for _ in stageS(tb):
                    pass

        S.emit(st)
    return nc


def build_phase_b(ntok, final):
    NT = ntok // 128
    nc = bass.Bass("TRN2", target_bir_lowering=False)
    x_d = nc.dram_tensor("x", [ntok, 1024], F32, kind="ExternalInput").ap()
    y_d = nc.dram_tensor("y", [ntok, 1024], F32, kind="ExternalInput").ap()
    g_d = nc.dram_tensor("gs", [ntok, 512], F32, kind="ExternalInput").ap()
    wo_d = nc.dram_tensor("wo", [1024, 1024], F32, kind="ExternalInput").ap()
    adaw_d = nc.dram_tensor("adaw", [1024, 1024], F32, kind="ExternalInput").ap()
    pv_d = nc.dram_tensor("pv", [128, 8], F32, kind="ExternalInput").ap()
    brow_d = nc.dram_tensor("brow", [1, 1024], F32, kind="ExternalInput").ap()
    fg_d = nc.dram_tensor("fg", [128, 1024], F32, kind="ExternalInput").ap()
    o_d = nc.dram_tensor("xo", [ntok, 1024], F32, kind="ExternalOutput").ap()

    S = Sched(nc)
    st = ExitStack()
    with st:
        def sb(name, shape, dt=F32):
            return TL(st.enter_context(nc.sbuf_tensor("s_" + name, shape, dt)), name)

        def ps(name, shape, dt=F32):
            return TL(st.enter_context(nc.psum_tensor("p_" + name, shape, dt)), name)

        def rd(*ts):
            return [t.b for t in ts]

        bank = [ps("bank%d" % i, [128, 512], F32) for i in range(8)]
        pv = sb("pv", [128, 8])
        brow = sb("brow", [1, 1024])
        fg = sb("fg", [128, 1024])
        S.dma("sp", pv[:], pv_d[:, :], writes=rd(pv))
        S.dma("sp", brow[:], brow_d[:, :], writes=rd(brow))
        S.dma("sp", fg[:], fg_d[:, :], writes=rd(fg))
        identf = sb("identf", [128, 128])
        ident = sb("ident", [128, 128], BF16)
        S.op("pool", "memset", identf[:], 1.0, writes=rd(identf))
        S.op("pool", "affine_select", identf[:], identf[:], pattern=[[-1, 128]], compare_op=ALU.is_equal,
             fill=0.0, base=0, channel_multiplier=1, reads=rd(identf), writes=rd(identf))
        S.op("dve", "tensor_copy", ident[:], identf[:], reads=rd(identf), writes=rd(ident))
        onesf = sb("onesf", [1, 128])
        S.op("pool", "memset", onesf[:], 1.0, writes=rd(onesf))
        cact = sb("cact", [128, 8])
        tmp8 = sb("tmp8", [128, 8])
        S.op("act", "activation", tmp8[:], pv[:, 0:8], AF.Exp, scale=-1.0, reads=rd(pv), writes=rd(tmp8))
        S.op("dve", "tensor_scalar", tmp8[:], tmp8[:], 1.0, None, ALU.add, reads=rd(tmp8), writes=rd(tmp8))
        S.op("dve", "reciprocal", tmp8[:], tmp8[:], reads=rd(tmp8), writes=rd(tmp8))
        S.op("dve", "tensor_tensor", cact[:], tmp8[:], pv[:, 0:8], ALU.mult, reads=rd(tmp8, pv), writes=rd(cact))
        stage = [sb("stage%d" % i, [128, 1024]) for i in range(4)]
        wstage = [sb("wstage%d" % i, [128, 1024]) for i in range(4)]
        grow = sb("grow", [1, 1024])
        for kc in range(8):
            stg = stage[kc % 4]
            S.dma("sp", stg[:], adaw_d[kc * 128:(kc + 1) * 128, :], writes=rd(stg))
            for hf in range(2):
                S.op("pe", "matmul", bank[hf][0:1, :], cact[:, kc:kc + 1], stg[:, hf * 512:(hf + 1) * 512],
                     start=(kc == 0), stop=(kc == 7), reads=rd(cact, stg), writes=rd(bank[hf]))
        for hf in range(2):
            S.op("dve", "tensor_tensor", grow[:, hf * 512:(hf + 1) * 512], bank[hf][0:1, :], brow[:, hf * 512:(hf + 1) * 512], ALU.add,
                 reads=rd(bank[hf], brow), writes=rd(grow))
        gbc = sb("gbc", [128, 1024])
        for hf in range(2):
            S.op("pe", "matmul", bank[2 + hf][:], onesf[0:1, :], grow[0:1, hf * 512:(hf + 1) * 512], start=True, stop=True,
                 reads=rd(onesf, grow), writes=rd(bank[2 + hf]))
            S.op("act", "activation", gbc[:, hf * 512:(hf + 1) * 512], bank[2 + hf][:], AF.Copy, reads=rd(bank[2 + hf]), writes=rd(gbc))
        Wo = sb("Wo", [128, 8, 1024], BF16)
        for kc in range(8):
            stg = wstage[kc % 4]
            S.dma("sp", stg[:], wo_d[kc * 128:(kc + 1) * 128, :], writes=rd(stg))
            if kc % 2 == 0:
                S.op("pool", "tensor_copy", Wo[:, kc, :], stg[:], reads=rd(stg), writes=rd(Wo))
            else:
                S.op("act", "activation", Wo[:, kc, :], stg[:], AF.Copy, reads=rd(stg), writes=rd(Wo))

        NBUF = 3
        xt = [sb("xt%d" % i, [128, 1024]) for i in range(NBUF)]
        yt = [sb("yt%d" % i, [128, 1024]) for i in range(NBUF)]
        gt = [sb("gt%d" % i, [128, 512]) for i in range(NBUF)]
        yb = [sb("yb%d" % i, [128, 1024], BF16) for i in range(2)]
        yT = [sb("yT%d" % i, [128, 8, 128], BF16) for i in range(2)]
        xo = [sb("xo%d" % i, [128, 1024]) for i in range(2)]
        tg = [sb("tg%d" % i, [128, 1024]) for i in range(2)]
        junk = sb("junk", [128, 1024], BF16)
        ss = sb("ss", [128, 1])
        for t in range(NT):
            i = t % 2
            j = t % NBUF
            rows = slice(t * 128, (t + 1) * 128)
            S.dma("sp", yt[j][:], y_d[rows, :], writes=rd(yt[j]))
            S.dma("sp", gt[j][:], g_d[rows, :], writes=rd(gt[j]))
            S.dma("sp", xt[j][:], x_d[rows, :], writes=rd(xt[j]))
            S.op("act", "activation", yb[i][:, 0:512], yt[j][:, 0:512], AF.Copy, reads=rd(yt[j]), writes=rd(yb[i]))
            S.op("dve", "tensor_tensor", yb[i][:, 512:1024], yt[j][:, 512:1024], gt[j][:], ALU.mult, reads=rd(yt[j], gt[j]), writes=rd(yb[i]))
            pT = bank[4 + i]
            pTv = pT[:].bitcast(BF16).rearrange("p (k t) -> p k t", t=128)
            for kc in range(8):
                S.op("pe", "transpose", pTv[:, kc, :], yb[i][:, kc * 128:(kc + 1) * 128], ident[:], reads=rd(yb[i], ident), writes=rd(pT))
            S.op("act", "activation", yT[i][:], pTv[:, 0:8, :], AF.Copy, reads=rd(pT), writes=rd(yT[i]))
            for hf in range(2):
                pb = bank[hf * 2 + i]
                for kc in range(8):
                    S.op("pe", "matmul", pb[:], yT[i][:, kc, :], Wo[:, kc, hf * 512:(hf + 1) * 512], start=(kc == 0), stop=(kc == 7),
                         reads=rd(yT[i], Wo), writes=rd(pb))
                hsl = slice(hf * 512, (hf + 1) * 512)
                S.op("dve", "tensor_tensor", tg[i][:, hsl], pb[:], gbc[:, hsl], ALU.mult, reads=rd(pb, gbc), writes=rd(tg[i]))
                S.op("pool", "tensor_tensor", xo[i][:, hsl], tg[i][:, hsl], xt[j][:, hsl], ALU.add, reads=rd(tg[i], xt[j]), writes=rd(xo[i]))
            if final:
                S.op("act", "activation", junk[:], xo[i][:], AF.Square, scale=1.0 / 32.0, accum_out=ss[:], reads=rd(xo[i]), writes=rd(junk, ss))
                S.op("act", "activation", ss[:], ss[:], AF.Ln, bias=NORM_EPS, reads=rd(ss), writes=rd(ss))
                S.op("act", "activation", ss[:], ss[:], AF.Exp, scale=-0.5, reads=rd(ss), writes=rd(ss))
                S.op("dve", "scalar_tensor_tensor", xo[i][:], xo[i][:], ss[:, 0:1], fg[:], ALU.mult, ALU.mult, reads=rd(xo[i], ss, fg), writes=rd(xo[i]))
            S.dma("sp", o_d[rows, :], xo[i][:], reads=rd(xo[i]))
        S.emit(st)
    return nc


D=1024; RW=512; SHIFT=1664; RWKV_COLS=2176

def cols_for(hp):
    r = np.arange(128)+128*hp
    return dict(r=r, k=512+r, v=1024+r, zz=np.arange(1536,1664), g=1664+r,
                q=RWKV_COLS+r, ks=RWKV_COLS+512+r, vs=RWKV_COLS+1024+r, gs=RWKV_COLS+1536+r)

def prep_a(inp, l, xcur, S_len):
    maps = []
    for core in range(8):
        b, hp = core // 4, core % 4
        cs = cols_for(hp)
        order = [cs['r'], cs['k'], cs['v'], cs['zz'], cs['q'], cs['ks'], cs['vs'], cs['g'], cs['gs']]
        wc = np.ascontiguousarray(inp['w_in'][l][:, np.concatenate(order)])
        adaw = np.ascontiguousarray(inp['ada_w'][l][:, :2048])
        pv = np.zeros((128, 48), np.float32)
        t8 = lambda v: np.ascontiguousarray(v.reshape(8, 128).T)
        pv[:, 0:8] = t8(inp['c'][b])
        pv[:, 8:16] = t8(inp['norm_g'][l])
        pv[:, 16:24] = t8(inp['ada_b'][l][0:1024])
        pv[:, 24:32] = t8(inp['ada_b'][l][1024:2048])
        mu = inp['tshift_mu'][l]
        pv[:, 32] = mu[cs['r']]; pv[:, 33] = mu[cs['k']]; pv[:, 34] = mu[cs['v']]; pv[:, 35] = mu[cs['zz']]
        hc = cs['r']
        pv[:, 36] = inp['decay_w0'][l][hc]; pv[:, 37] = inp['iclr_a0'][l][hc]
        pv[:, 38] = inp['k_k'][l][hc]; pv[:, 39] = inp['k_a'][l][hc]
        pv[:, 40] = inp['r_k'][l].reshape(512)[hc]
        w2a2 = np.concatenate([inp['decay_w2'][l][:, hc], inp['iclr_a2'][l][:, hc]], axis=0).astype(np.float32)
        bc = np.stack([np.broadcast_to(inp['rwkv_ln_w'][l][hc], (128, 128)),
                       np.broadcast_to(inp['rwkv_ln_b'][l][hc], (128, 128)),
                       np.broadcast_to(inp['sb_norm_g'][l][hc], (128, 128))], axis=1).astype(np.float32)
        maps.append(dict(x=np.ascontiguousarray(xcur[b, :S_len]), wc=wc, adaw=adaw, pv=pv,
                         w2a2=np.ascontiguousarray(w2a2), bc=np.ascontiguousarray(bc)))
    return maps

def assemble_y(results, S_len):
    y = np.zeros((2, S_len, 1024), np.float32)
    g = np.zeros((2, S_len, 512), np.float32)
    for core in range(8):
        b, hp = core // 4, core % 4
        yc = results[core]['y']
        y[b, :, 128*hp:128*hp+128] = yc[:, 0:128]
        y[b, :, 512+128*hp:512+128*hp+128] = yc[:, 128:256]
        g[b, :, 128*hp:128*hp+128] = results[core]['gs']
    return y, g


_CACHE = {}


def _get(kind, *args):
    key = (kind,) + args
    if key not in _CACHE:
        _CACHE[key] = build_phase_a(*args) if kind == "a" else build_phase_b(*args)
    return _CACHE[key]


def prep_b(inp, l, xcur, y, g):
    maps = []
    t8 = lambda v: np.ascontiguousarray(v.reshape(8, 128).T)
    xf = xcur.reshape(16384, 1024)
    yf = y.reshape(16384, 1024)
    gf = g.reshape(16384, 512)
    for core in range(8):
        b = core // 4
        rows = slice(core * 2048, (core + 1) * 2048)
        maps.append(dict(
            x=np.ascontiguousarray(xf[rows]), y=np.ascontiguousarray(yf[rows]), gs=np.ascontiguousarray(gf[rows]),
            wo=np.ascontiguousarray(inp['w_out'][l]), adaw=np.ascontiguousarray(inp['ada_w'][l][:, 2048:3072]),
            pv=t8(inp['c'][b]), brow=np.ascontiguousarray(inp['ada_b'][l][2048:3072].reshape(1, 1024)),
            fg=np.ascontiguousarray(np.broadcast_to(inp['final_g'], (128, 1024)))))
    return maps


def kernel(**inputs):
    inp = {k: np.asarray(v, dtype=np.float32) for k, v in inputs.items()}
    S_len = inp['x'].shape[1]
    depth = inp['w_in'].shape[0]
    xcur = inp['x']
    for l in range(depth):
        nca = _get("a", S_len)
        res = run_bass_kernel_spmd(nca, prep_a(inp, l, xcur, S_len), core_ids=list(range(8)))
        y, g = assemble_y(res.results, S_len)
        ncb = _get("b", 2048, l == depth - 1)
        res = run_bass_kernel_spmd(ncb, prep_b(inp, l, xcur, y, g), core_ids=list(range(8)))
        xcur = np.concatenate([r['xo'] for r in res.results], axis=0).reshape(2, S_len, 1024)
    return np.ascontiguousarray(xcur.astype(np.float32))
```
